# Optimizing a Trainium2 kernel written in Bass

```python
import math
import jax
import jax.numpy as jnp
from jax import lax
import numpy as np

D_MODEL = 1024
BATCH = 16
SEQ = 2048
DEPTH = 2

N_GROUPS = 4
HEAD_DIM = 64
HEADS_PER_GROUP = D_MODEL // (N_GROUPS * HEAD_DIM)
GROUP_W = HEADS_PER_GROUP * HEAD_DIM
D_MIX = N_GROUPS * GROUP_W
Q_BLOCK = 128

DIFF_QK_DIM = HEAD_DIM // 2
IDX_HEADS = 8
IDX_DIM = 32
TOPK_MAX = 256
MLA_Q_RANK = 256
MLA_KV_RANK = 128
MLA_NOPE = 64
MLA_ROPE = 32
MLA_V = HEAD_DIM
ROPE_BASE = 10000.0
REL_BUCKETS = 32
REL_MAX_DIST = 128
N_BIAS_HEADS = 2 * HEADS_PER_GROUP
ALPHA = (2.0 * DEPTH) ** 0.25
BETA = (8.0 * DEPTH) ** -0.25
NORM_EPS = 1e-5

SPLITS = (
    ("a_q", GROUP_W), ("a_k", GROUP_W), ("a_v", GROUP_W),
    ("b_q", GROUP_W), ("b_k", GROUP_W), ("b_v", GROUP_W),
    ("c_q", GROUP_W), ("c_k", GROUP_W), ("c_v", GROUP_W),
    ("c_iq", IDX_HEADS * IDX_DIM), ("c_ik", IDX_DIM), ("c_iw", IDX_HEADS),
    ("d_cq", MLA_Q_RANK), ("d_ckv", MLA_KV_RANK), ("d_kr", MLA_ROPE),
    ("gate", D_MIX),
)
VALUE_COLUMNS = ("a_v", "b_v", "c_v")
D_IN = sum(w for _, w in SPLITS)

kernel_name = "hybrid_sb_diff_dsa_mla_deepnorm"


def split_columns(h):
    out = {}
    off = 0
    for name, w in SPLITS:
        out[name] = h[..., off:off + w]
        off += w
    return out


def rms_norm(x, g, eps=1e-6):
    xf = x.astype(jnp.float32)
    y = xf * lax.rsqrt(jnp.mean(xf * xf, axis=-1, keepdims=True) + eps)
    return (y * g.astype(jnp.float32)).astype(x.dtype)


def layer_norm(x, g, b):
    xf = x.astype(jnp.float32)
    mu = jnp.mean(xf, axis=-1, keepdims=True)
    var = jnp.mean(jnp.square(xf - mu), axis=-1, keepdims=True)
    y = (xf - mu) * lax.rsqrt(var + NORM_EPS)
    return (y * g.astype(jnp.float32) + b.astype(jnp.float32)).astype(x.dtype)


def t5_bucket(dist):
    max_exact = REL_BUCKETS // 2
    n = jnp.maximum(dist, 0)
    nf = jnp.maximum(n, max_exact).astype(jnp.float32)
    large = max_exact + (jnp.log(nf / max_exact) / math.log(REL_MAX_DIST / max_exact)
                         * (REL_BUCKETS - max_exact)).astype(jnp.int32)
    large = jnp.minimum(large, REL_BUCKETS - 1)
    return jnp.where(n < max_exact, n, large)


def rope(x, ang):
    half = x.shape[-1] // 2
    x1 = x[..., :half].astype(jnp.float32)
    x2 = x[..., half:].astype(jnp.float32)
    c, s = jnp.cos(ang), jnp.sin(ang)
    return jnp.concatenate([x1 * c - x2 * s, x1 * s + x2 * c], axis=-1).astype(x.dtype)


def stick_breaking_block(q, k, v, qpos, kpos):
    z = jnp.einsum("bqhd,bkhd->bhqk", q, k).astype(jnp.float32) * (HEAD_DIM ** -0.5)
    strict = kpos[None, :] < qpos[:, None]
    log1m = jnp.where(strict, jax.nn.log_sigmoid(-z), 0.0)
    after = lax.cumsum(log1m, axis=3, reverse=True) - log1m
    w = jnp.where(strict, jnp.exp(jax.nn.log_sigmoid(z) + after), 0.0)
    return jnp.einsum("bhqk,bkhd->bqhd", w.astype(v.dtype), v)


def diff_block(q, k, v, causal, bias, lam, lam_init, subln):
    s = jnp.einsum("bqhcd,bkhcd->bchqk", q, k).astype(jnp.float32) * (DIFF_QK_DIM ** -0.5)
    s = jnp.where(causal, s + bias, -jnp.inf)
    p = jax.nn.softmax(s, axis=-1)
    a = p[:, 0] - lam * p[:, 1]
    o = jnp.einsum("bhqk,bkhd->bqhd", a.astype(v.dtype), v)
    return rms_norm(o, subln) * (1.0 - lam_init)


def dsa_block(qi, ki, iw, q, k, v, qpos, bias_table, n_sel):
    kpos = jnp.arange(ki.shape[1])
    dots = jnp.einsum("bqhe,bke->bqhk", qi, ki).astype(jnp.float32) * (IDX_DIM ** -0.5)
    score = jnp.einsum("bqh,bqhk->bqk", iw.astype(jnp.float32), jax.nn.relu(dots))
    score = jnp.where(kpos[None, None, :] <= qpos[None, :, None], score, -jnp.inf)
    _, idx = lax.top_k(score, n_sel)
    valid = idx <= qpos[None, :, None]
    gather = jax.vmap(lambda kk, ii: kk[ii])
    ks = gather(k, idx)
    vs = gather(v, idx)
    s = jnp.einsum("bqhd,bqkhd->bhqk", q, ks).astype(jnp.float32) * (HEAD_DIM ** -0.5)
    bias = bias_table[t5_bucket(qpos[None, :, None] - idx)]
    s = s + jnp.transpose(bias, (0, 3, 1, 2)).astype(jnp.float32)
    s = jnp.where(valid[:, None], s, -jnp.inf)
    p = jax.nn.softmax(s, axis=-1)
    return jnp.einsum("bhqk,bqkhd->bqhd", p.astype(vs.dtype), vs)


def mla_block(q_nope, q_rope, k_nope, k_rope, v, causal):
    s = (jnp.einsum("bqhd,bkhd->bhqk", q_nope, k_nope)
         + jnp.einsum("bqhr,bkr->bhqk", q_rope, k_rope)).astype(jnp.float32)
    s = jnp.where(causal, s * ((MLA_NOPE + MLA_ROPE) ** -0.5), -jnp.inf)
    p = jax.nn.softmax(s, axis=-1)
    return jnp.einsum("bhqk,bkhd->bqhd", p.astype(v.dtype), v)


def hybrid_layer(x, layer_idx, w_in, w_out, ln_g, ln_b, rel_bias, diff_lambda, diff_subln,
                 mla_q_norm, mla_kv_norm, mla_w_uq, mla_w_ukv):
    B, S, _ = x.shape
    H = HEADS_PER_GROUP
    p = split_columns(jnp.einsum("bsd,de->bse", x, w_in))
    a_q = p["a_q"].reshape(B, S, H, HEAD_DIM)
    a_k = p["a_k"].reshape(B, S, H, HEAD_DIM)
    a_v = p["a_v"].reshape(B, S, H, HEAD_DIM)
    b_q = p["b_q"].reshape(B, S, H, 2, DIFF_QK_DIM)
    b_k = p["b_k"].reshape(B, S, H, 2, DIFF_QK_DIM)
    b_v = p["b_v"].reshape(B, S, H, HEAD_DIM)
    lam_init = 0.8 - 0.6 * math.exp(-0.3 * layer_idx)
    lf = diff_lambda.astype(jnp.float32)
    lam = jnp.exp(jnp.sum(lf[0] * lf[1])) - jnp.exp(jnp.sum(lf[2] * lf[3])) + lam_init
    c_q = p["c_q"].reshape(B, S, H, HEAD_DIM)
    c_k = p["c_k"].reshape(B, S, H, HEAD_DIM)
    c_v = p["c_v"].reshape(B, S, H, HEAD_DIM)
    c_iq = p["c_iq"].reshape(B, S, IDX_HEADS, IDX_DIM)
    c_ik = p["c_ik"]
    c_iw = p["c_iw"] * (IDX_HEADS ** -0.5)
    n_sel = min(TOPK_MAX, S // 4)
    cq = rms_norm(p["d_cq"], mla_q_norm)
    ckv = rms_norm(p["d_ckv"], mla_kv_norm)
    d_q = jnp.einsum("bsr,re->bse", cq, mla_w_uq).reshape(B, S, H, MLA_NOPE + MLA_ROPE)
    d_kv = jnp.einsum("bsr,re->bse", ckv, mla_w_ukv).reshape(B, S, H, MLA_NOPE + MLA_V)
    pos = jnp.arange(S)
    inv_freq = ROPE_BASE ** (-jnp.arange(MLA_ROPE // 2, dtype=jnp.float32) / (MLA_ROPE // 2))
    ang = pos[:, None].astype(jnp.float32) * inv_freq[None, :]
    d_qn = d_q[..., :MLA_NOPE]
    d_qr = rope(d_q[..., MLA_NOPE:], ang[:, None, :])
    d_kn = d_kv[..., :MLA_NOPE]
    d_v = d_kv[..., MLA_NOPE:]
    d_kr = rope(p["d_kr"], ang)

    outs = []
    for i in range(S // Q_BLOCK):
        q0, q1 = i * Q_BLOCK, (i + 1) * Q_BLOCK
        qpos = jnp.arange(q0, q1)
        kpos = jnp.arange(q1)
        causal = kpos[None, :] <= qpos[:, None]
        bias_b = jnp.transpose(rel_bias[t5_bucket(qpos[:, None] - kpos[None, :])][..., :H],
                               (2, 0, 1)).astype(jnp.float32)
        o_a = stick_breaking_block(a_q[:, q0:q1], a_k[:, :q1], a_v[:, :q1], qpos, kpos)
        o_b = diff_block(b_q[:, q0:q1], b_k[:, :q1], b_v[:, :q1], causal, bias_b,
                         lam, lam_init, diff_subln)
        o_c = dsa_block(c_iq[:, q0:q1], c_ik, c_iw[:, q0:q1], c_q[:, q0:q1], c_k, c_v,
                        qpos, rel_bias[:, H:], n_sel)
        o_d = mla_block(d_qn[:, q0:q1], d_qr[:, q0:q1], d_kn[:, :q1], d_kr[:, :q1],
                        d_v[:, :q1], causal)
        o = jnp.concatenate([o_a, o_b.astype(o_a.dtype), o_c.astype(o_a.dtype),
                             o_d.astype(o_a.dtype)], axis=2)
        outs.append(o.reshape(B, Q_BLOCK, D_MIX))
    y = jnp.concatenate(outs, axis=1) * jax.nn.silu(p["gate"])
    y = jnp.einsum("bse,ed->bsd", y.astype(x.dtype), w_out)
    return layer_norm(ALPHA * x + y, ln_g, ln_b)


def setup_inputs(seed: int = 0) -> dict:
    key = jax.random.key(seed)
    ks = jax.random.split(key, 12)
    f32 = jnp.float32
    H = HEADS_PER_GROUP
    col_scale = jnp.asarray(np.concatenate(
        [np.full((w,), BETA if name in VALUE_COLUMNS else 1.0, np.float32) for name, w in SPLITS]))
    x = jax.random.normal(ks[0], (BATCH, SEQ, D_MODEL), f32)
    w_in = jax.random.normal(ks[1], (DEPTH, D_MODEL, D_IN), f32) * (D_MODEL ** -0.5) * col_scale
    w_out = jax.random.normal(ks[2], (DEPTH, D_MIX, D_MODEL), f32) * ((D_MIX ** -0.5) * BETA)
    ln_g = 1.0 + 0.02 * jax.random.normal(ks[3], (DEPTH, D_MODEL), f32)
    ln_b = 0.02 * jax.random.normal(ks[4], (DEPTH, D_MODEL), f32)
    rel_bias = 0.1 * jax.random.normal(ks[5], (REL_BUCKETS, N_BIAS_HEADS), f32)
    diff_lambda = 0.1 * jax.random.normal(ks[6], (DEPTH, 4, DIFF_QK_DIM), f32)
    diff_subln = 1.0 + 0.02 * jax.random.normal(ks[7], (DEPTH, HEAD_DIM), f32)
    mla_q_norm = 1.0 + 0.02 * jax.random.normal(ks[8], (DEPTH, MLA_Q_RANK), f32)
    mla_kv_norm = 1.0 + 0.02 * jax.random.normal(ks[9], (DEPTH, MLA_KV_RANK), f32)
    mla_w_uq = jax.random.normal(ks[10], (DEPTH, MLA_Q_RANK, H * (MLA_NOPE + MLA_ROPE)), f32) \
        * (MLA_Q_RANK ** -0.5)
    ukv_scale = jnp.concatenate([jnp.ones((MLA_NOPE,), f32), jnp.full((MLA_V,), BETA, f32)])
    mla_w_ukv = (jax.random.normal(ks[11], (DEPTH, MLA_KV_RANK, H, MLA_NOPE + MLA_V), f32)
                 * (MLA_KV_RANK ** -0.5) * ukv_scale).reshape(DEPTH, MLA_KV_RANK, H * (MLA_NOPE + MLA_V))
    return {"x": x, "w_in": w_in, "w_out": w_out, "ln_g": ln_g, "ln_b": ln_b,
            "rel_bias": rel_bias, "diff_lambda": diff_lambda, "diff_subln": diff_subln,
            "mla_q_norm": mla_q_norm, "mla_kv_norm": mla_kv_norm,
            "mla_w_uq": mla_w_uq, "mla_w_ukv": mla_w_ukv}


def reference(x, w_in, w_out, ln_g, ln_b, rel_bias, diff_lambda, diff_subln,
              mla_q_norm, mla_kv_norm, mla_w_uq, mla_w_ukv):
    for l in range(DEPTH):
        x = hybrid_layer(x, l, w_in[l], w_out[l], ln_g[l], ln_b[l], rel_bias,
                         diff_lambda[l], diff_subln[l], mla_q_norm[l], mla_kv_norm[l],
                         mla_w_uq[l], mla_w_ukv[l])
    return x
```

```python
import math
from contextlib import ExitStack

import numpy as np
import concourse.bass as bass
import concourse.mybir as mybir
from concourse.bass_utils import run_bass_kernel_spmd

F32 = mybir.dt.float32
BF16 = mybir.dt.bfloat16
AF = mybir.ActivationFunctionType
ALU = mybir.AluOpType
AX = mybir.AxisListType

D_MODEL = 1024
DEPTH = 2
ALPHA = (2.0 * DEPTH) ** 0.25
NORM_EPS = 1e-5
NEG = -30000.0
D_IN = 4040

OFF = {}
_o = 0
for _n, _w in (("a_q", 256), ("a_k", 256), ("a_v", 256), ("b_q", 256), ("b_k", 256), ("b_v", 256),
               ("c_q", 256), ("c_k", 256), ("c_v", 256), ("c_iq", 256), ("c_ik", 32), ("c_iw", 8),
               ("d_cq", 256), ("d_ckv", 128), ("d_kr", 32), ("gate", 1024)):
    OFF[_n] = (_o, _w)
    _o += _w


class Sched:
    CH = 20000
    K_DMA = 8

    def __init__(self, nc, es):
        self.nc = nc
        self.es = es
        self.ops = []
        self.eng = {'pe': nc.tensor, 'act': nc.scalar, 'dve': nc.vector, 'pool': nc.gpsimd, 'sp': nc.sync}
        self.reorder = True

    def op(self, eng, fn, reads=(), writes=(), dma=False, cost=None):
        isb = lambda t: isinstance(t, str) and len(t) == 2 and t[0] == 'b' and t[1].isdigit()
        br = [t for t in reads if isb(t)]
        if br:
            reads = [t for t in reads if not isb(t)]
            writes = list(writes) + [t for t in br if t not in writes]
        if cost is None:
            cost = 2500.0 if dma else 300.0
        self.ops.append(dict(eng=eng, fn=fn, reads=tuple(reads), writes=tuple(writes), dma=dma,
                             signal=False, deps=(), cost=float(cost)))

    def schedule(self):
        import heapq
        ops = self.ops
        n = len(ops)
        succ = [[] for _ in range(n)]
        indeg = [0] * n
        for i, o in enumerate(ops):
            indeg[i] = len(o['alldeps'])
            for p in o['alldeps']:
                succ[p].append(i)
        fin = [0.0] * n
        ready_t = [0.0] * n
        engs = list(self.eng)
        heaps = {e: [] for e in engs}
        now = {e: [] for e in engs}
        free = {e: 0.0 for e in engs}
        for i, o in enumerate(ops):
            if indeg[i] == 0:
                heapq.heappush(heaps[o['eng']], (0.0, i))
        order = []
        while len(order) < n:
            best = None
            for e in engs:
                h, nw = heaps[e], now[e]
                while h and h[0][0] <= free[e]:
                    heapq.heappush(nw, heapq.heappop(h)[1])
                if nw:
                    cand = (free[e], nw[0], e, True)
                elif h:
                    cand = (h[0][0], h[0][1], e, False)
                else:
                    continue
                if best is None or cand[:2] < best[:2]:
                    best = cand
            st, i, e, from_now = best
            if from_now:
                heapq.heappop(now[e])
            else:
                heapq.heappop(heaps[e])
            o = ops[i]
            if o['dma']:
                free[e] = st + 150.0
                fin[i] = st + o['cost']
            else:
                free[e] = st + o['cost']
                fin[i] = st + o['cost'] + 60.0
            order.append(i)
            for j in succ[i]:
                indeg[j] -= 1
                if fin[i] > ready_t[j]:
                    ready_t[j] = fin[i]
                if indeg[j] == 0:
                    heapq.heappush(heaps[ops[j]['eng']], (ready_t[j], j))
        self.est_ns = max(fin) if fin else 0.0
        self.fin = fin
        self.eng_busy = {e: sum(o['cost'] if not o['dma'] else 150.0 for o in ops if o['eng'] == e) / 1e6 for e in engs}
        return order

    def finalize(self):
        nc = self.nc
        ops = self.ops
        last_w = {}
        readers = {}
        for i, o in enumerate(ops):
            deps = set()
            for r in o['reads']:
                if r in last_w:
                    deps.add(last_w[r])
            for w in o['writes']:
                if w in last_w:
                    deps.add(last_w[w])
                rd = readers.get(w)
                if rd:
                    deps.update(rd)
            deps.discard(i)
            o['alldeps'] = sorted(deps)
            for w in o['writes']:
                last_w[w] = i
                readers[w] = []
            for r in o['reads']:
                if r not in o['writes']:
                    readers.setdefault(r, []).append(i)
        order = self.schedule() if self.reorder else list(range(len(ops)))
        pos = [0] * len(ops)
        for k, i in enumerate(order):
            pos[i] = k
        for i in order:
            o = ops[i]
            latest = {}
            keep = []
            for p in o['alldeps']:
                po = ops[p]
                if po['dma']:
                    keep.append(p)
                    continue
                if po['eng'] == 'pe' and o['eng'] == 'pe' and not o['dma']:
                    continue
                e = po['eng']
                if e not in latest or pos[p] > pos[latest[e]]:
                    latest[e] = p
            keep.extend(latest.values())
            for p in keep:
                ops[p]['signal'] = True
            o['deps'] = keep
        sems = {}

        def getsem(key):
            if key not in sems:
                sems[key] = self.es.enter_context(nc.semaphore("s_%s_%s" % key))
            return sems[key]
        cnt = {e: 0 for e in self.eng}
        ndma = {e: 0 for e in self.eng}
        dma_sig = {e: [] for e in self.eng}
        waited = {e: {} for e in self.eng}
        self.nwaits = 0

        def do_wait(E, sig):
            sem, val = sig
            k = id(sem)
            if waited[E].get(k, 0) < val:
                self.eng[E].wait_ge(sem, val)
                waited[E][k] = val
                self.nwaits += 1
        for i in order:
            o = ops[i]
            E = o['eng']
            for p in o['deps']:
                do_wait(E, ops[p]['sig'])
            if o['dma']:
                n = ndma[E]
                if n >= self.K_DMA:
                    do_wait(E, dma_sig[E][n - self.K_DMA])
                sem = getsem((E + 'd', n % self.K_DMA))
                val = 16 * (n // self.K_DMA + 1)
                ins = o['fn']()
                ins.then_inc(sem, 16)
                o['sig'] = (sem, val)
                dma_sig[E].append(o['sig'])
                ndma[E] = n + 1
            else:
                ins = o['fn']()
                if o['signal']:
                    c = cnt[E]
                    sem = getsem((E, c // self.CH))
                    ins.then_inc(sem, 1)
                    o['sig'] = (sem, c % self.CH + 1)
                    cnt[E] = c + 1
            o['fn'] = None
        for E in self.eng:
            for s in dma_sig[E][-self.K_DMA:]:
                do_wait('sp', s)
        self.nc.sync.nop()
        self.stats = (len(ops), self.nwaits, len(sems), dict(cnt), dict(ndma), getattr(self, 'est_ns', 0.0) / 1e6, getattr(self, 'eng_busy', None))


class Builder:
    def __init__(self, S=2048, NSEQ=2, L=2, NIT=13, groups="ABCD"):
        self.Sq = S
        self.NSEQ = NSEQ
        self.L = L
        self.NIT = NIT
        self.groups = groups
        self.NB = S // 128
        self.NJ = S // 512
        self.NSEL = min(256, S // 4)
        self.rot = {}
        self.debug = False
        self.dbg_names = []
        self.marks = []

    def mm(self, out, lhsT, rhs, start, stop, R, W):
        nc = self.nc
        N = rhs.free_size()
        self.S.op('pe', lambda: nc.tensor.matmul(out, lhsT=lhsT, rhs=rhs, start=start, stop=stop,
                                                 skip_group_check=True), R, W, cost=(max(N, 64) + 110) / 2.2)

    def tr(self, out, in_, R, W):
        nc = self.nc
        idt = self.c_ident
        self.S.op('pe', lambda: nc.tensor.transpose(out, in_, idt), list(R) + ['const'], W, cost=110.0)

    def act(self, out, in_, func, R, W, scale=1.0, bias=0.0, accum=None):
        nc = self.nc
        self.S.op('act', lambda: nc.scalar.activation(out=out, in_=in_, func=func, bias=bias, scale=scale,
                                                      accum_out=accum), R, W, cost=in_.free_size() * 0.9 + 180)

    def ts(self, eng, out, in0, s1, s2, op0, op1, R, W, accum=None):
        eng = 'dve'
        e = self.nc.vector if eng == 'dve' else self.nc.gpsimd
        c = in0.free_size() * 1.05 + 120
        if op1 is None:
            self.S.op(eng, lambda: e.tensor_scalar(out=out, in0=in0, scalar1=s1, scalar2=None, op0=op0), R, W, cost=c)
        else:
            self.S.op(eng, lambda: e.tensor_scalar(out=out, in0=in0, scalar1=s1, scalar2=s2, op0=op0, op1=op1,
                                                   accum_out=accum), R, W, cost=c)

    def tt(self, eng, out, in0, in1, op, R, W):
        eng = 'dve'
        e = self.nc.vector if eng == 'dve' else self.nc.gpsimd
        self.S.op(eng, lambda: e.tensor_tensor(out=out, in0=in0, in1=in1, op=op), R, W, cost=in0.free_size() * 1.3 + 120)

    def stt(self, out, in0, scalar, in1, op0, op1, R, W, accum=None):
        nc = self.nc
        self.S.op('dve', lambda: nc.vector.scalar_tensor_tensor(out=out, in0=in0, scalar=scalar, in1=in1,
                                                                op0=op0, op1=op1, accum_out=accum), R, W,
                  cost=in0.free_size() * 1.3 + 120)

    def cp(self, eng, out, in_, R, W):
        nc = self.nc
        if eng == 'pool':
            eng = 'dve'
        N = in_.free_size()
        if eng == 'act':
            self.S.op('act', lambda: nc.scalar.copy(out=out, in_=in_), R, W, cost=N * 0.9 + 180)
        elif eng == 'dve':
            self.S.op('dve', lambda: nc.vector.tensor_copy(out=out, in_=in_), R, W, cost=N * 0.6 + 120)
        else:
            self.S.op('pool', lambda: nc.gpsimd.tensor_copy(out=out, in_=in_), R, W, cost=N * 0.6 + 250)

    def memset(self, eng, ap, val, W):
        eng = 'dve'
        e = self.nc.vector if eng == 'dve' else self.nc.gpsimd
        self.S.op(eng, lambda: e.memset(ap, val), (), W, cost=ap.free_size() * 0.6 + 120)

    def dma(self, q, out, in_, R, W):
        e = {'sp': self.nc.sync, 'pool': self.nc.gpsimd, 'act': self.nc.scalar}[q]
        self.S.op(q, lambda: e.dma_start(out=out, in_=in_), R, W, dma=True, cost=2200 + out.free_size() * 128 * 4 / 150.0)

    def dbg(self, name, ap, toks):
        if not getattr(self, 'debug', False):
            return
        d = self.nc.dram_tensor("dbg_" + name, list(ap.shape), ap.dtype, kind="ExternalOutput").ap()
        self.dma('sp', d, ap, toks, [('dbg', name)])
        self.dbg_names.append("dbg_" + name)

    def mark(self, label):
        self.marks.append((label, len(self.S.ops)))

    def rr(self, name, n):
        v = self.rot.get(name, 0)
        self.rot[name] = v + 1
        return v % n

    def evac_eng(self):
        return ('act', 'dve')[self.rr('evac', 2)]

    def evac(self, out, in_, R, W, scale=1.0, eng=None):
        eng = eng or self.evac_eng()
        if eng == 'act':
            if scale == 1.0:
                self.cp('act', out, in_, R, W)
            else:
                self.act(out, in_, AF.Copy, R, W, scale=scale)
        else:
            if scale == 1.0:
                self.cp('dve', out, in_, R, W)
            else:
                self.ts('dve', out, in_, scale, None, ALU.mult, None, R, W)

    def build(self):
        S_, NSEQ, L, NB = self.Sq, self.NSEQ, self.L, self.NB
        nc = bass.Bass("TRN2", target_bir_lowering=False)
        self.nc = nc
        dt_in = lambda name, shape: nc.dram_tensor(name, list(shape), F32, kind="ExternalInput").ap()
        self.d_x = dt_in("x", [NSEQ, S_, D_MODEL])
        self.d_win = dt_in("w_in_l", [L, 128, 8, D_IN])
        self.d_wout = dt_in("w_out_l", [L, 128, 8, D_MODEL])
        self.d_lnp = dt_in("lnp", [L, 2, 128, D_MODEL])
        self.d_relb = dt_in("relb", [128, 8, 2, 128])
        self.d_c31 = dt_in("c31", [128, 8])
        self.d_dlam = dt_in("dlam", [L, 128, 128])
        self.d_subln = dt_in("subln", [L, 128, 64])
        self.d_qn = dt_in("qn", [L, 128, 2])
        self.d_kvn = dt_in("kvn", [L, 128, 1])
        self.d_wuq = dt_in("wuq", [L, 128, 2, 384])
        self.d_wukv = dt_in("wukv", [L, 128, 512])
        self.d_consts = dt_in("consts", [128, 7, 128])
        self.d_rope = dt_in("rope", [2, 128, S_])
        self.d_out = nc.dram_tensor("out", [NSEQ, S_, D_MODEL], F32, kind="ExternalOutput").ap()
        self.d_xmid = nc.dram_tensor("xmid", [NSEQ, S_, D_MODEL], F32, kind="Internal").ap()
        self.d_ogd = nc.dram_tensor("ogd", [S_, D_MODEL], BF16, kind="Internal").ap()
        self.d_mb = nc.dram_tensor("mbd", [NB, 128, S_], BF16, kind="Internal").ap()
        es = ExitStack()
        with es:
            self.S = Sched(nc, es)
            self.alloc(es)
            self.setup_consts()
            for l in range(L):
                self.layer_setup(l)
                for s in range(NSEQ):
                    self.mark('xT %d %d' % (l, s))
                    self.build_xT(l, s)
                    self.dbg('xT', self.xT[:], ['xT'])
                    if "C" in self.groups:
                        self.mark('proj I')
                        self.project_indexer(l)
                    for g in "ABDC":
                        if g in self.groups:
                            self.mark('proj ' + g)
                            self.project_group(l, g)
                            self.dbg('QT' + g, self.QT, ['QT'])
                            self.dbg('KT' + g, self.KT, ['KT'])
                            self.dbg('VA' + g, self.VA[:], ['VA'])
                            self.dbg('SG' + g, self.SG[:], ['SG'])
                            self.mark('att ' + g)
                            self.attend_group(l, g)
                            self.dbg('OG' + g, self.OG[0][:], ['OG0'])
                        else:
                            self.zero_group(g)
                        if g == "A" and "C" in self.groups:
                            self.mark('indexer')
                            if "B" in self.groups:
                                self.indexer(0)
                            else:
                                self.indexer_all()
                    self.mark('out')
                    self.output_phase(l, s)
            self.S.finalize()
        return nc

    def alloc(self, es):
        nc = self.nc
        S_, NB = self.Sq, self.NB
        sb = lambda name, shape, dt: es.enter_context(nc.sbuf_tensor("sb_" + name, list(shape), dt))
        self.bank = [es.enter_context(nc.psum_tensor("bank%d" % k, [128, 512], F32)) for k in range(8)]
        self.cst_b = sb("cst_b", [128, 7, 128], BF16)
        self.c_ident = self.cst_b[:, 0, :]
        self.c_negtri = self.cst_b[:, 1, :]
        self.c_negones = self.cst_b[:, 2, :]
        self.c_CM = self.cst_b[:, 3, :]
        self.c_MA = self.cst_b[:, 4, :]
        self.c_strictT = self.cst_b[:, 5, :]
        self.c_ones = self.cst_b[:, 6, :]
        self.CMf = sb("CMf", [128, 128], F32)
        self.IDf = sb("IDf", [128, 128], F32)
        self.OTs = [sb("OTs%d" % k, [65, 512], F32) for k in range(1)]
        self.c_CMf = self.CMf[:]
        self.BT = sb("BT", [128, 8, 2, 128], BF16)
        self.c31 = sb("c31", [128, 8], F32)
        self.dummy = sb("dummy", [128, 2], F32)
        self.DL = sb("DL", [128, 128], F32)
        self.G64 = sb("G64", [128, 64], F32)
        self.sm = sb("sm", [128, 96], F32)
        self.QN = sb("QN", [128, 2], F32)
        self.KVN = sb("KVN", [128, 1], F32)
        self.WUQb = sb("WUQb", [128, 2, 384], BF16)
        self.WROT = sb("WROT", [128, 2, 4, 32], BF16)
        self.WUKVb = sb("WUKVb", [128, 512], BF16)
        self.WS = [sb("WS0", [128, 8, 256], F32)]
        self.WB = [sb("WB%d" % k, [128, 8, 256], BF16) for k in range(2)]
        self.WX = sb("WX", [128, 8, 128], BF16)
        self.xT = sb("xT", [128, 8, S_], BF16)
        self.XB = [sb("XB%d" % k, [128, 1024], F32) for k in range(2)]
        self.XBbraw = sb("XBbraw", [128, 2048], BF16)
        self.XBb = [self.XBbraw[:, k * 1024:(k + 1) * 1024] for k in range(2)]
        self.QTraw = sb("QTraw", [128, 8192], BF16)
        self.KTraw = sb("KTraw", [128, 8192], BF16)
        self.QT = self.QTraw[:, 0:4 * S_].rearrange("p (a s) -> p a s", a=4)
        self.KT = self.KTraw[:, 0:4 * S_].rearrange("p (a s) -> p a s", a=4)
        self.WO = self.QTraw[:, :].rearrange("p (a s) -> p a s", a=8)
        self.LNG = self.KTraw[:, 0:2048].bitcast(F32)
        self.LNB = self.KTraw[:, 2048:4096].bitcast(F32)
        self.IQ = sb("IQ", [128, 3, S_], BF16)
        if S_ >= 2048:
            iqf = self.IQ[:].rearrange("p a s -> p (a s)")
            self.XS = [iqf[:, k * 2048:(k + 1) * 2048].bitcast(F32) for k in range(2)]
            self.XSb = [iqf[:, 4096 + k * 1024:4096 + (k + 1) * 1024] for k in range(2)]
            self.xs_tok = (['IQs0', 'IQs1'], ['IQb0', 'IQb1'])
        else:
            self.XS = self.XSb = None
        self.IK3 = sb("IK3", [128, S_], BF16)
        self.VA = sb("VA", [128, NB, 4, 65], BF16)
        self.SG = sb("SG", [128, NB, 256], BF16)
        self.IW = sb("IW", [128, NB, 8], F32)
        mbsz = max(4 * S_, 7680)
        nscr = mbsz + S_ + 1536
        self.SCRb = sb("SCRb", [128, nscr], BF16)
        X = self.SCRb
        self.MB = X[:, 0:4 * S_].rearrange("p (a s) -> p a s", a=4)
        self.JK = X[:, mbsz:mbsz + S_]
        self.RL = [X[:, mbsz + S_ + k * 512:mbsz + S_ + (k + 1) * 512] for k in range(3)]
        self.CQT = X[:, 0:1024].rearrange("p (a s) -> p a s", a=2)
        self.CKVT = X[:, 1024:1536]
        self.SQT = X[:, 1536:3072].rearrange("p (a s) -> p a s", a=3)
        self.KRr = X[:, 3072:3584]
        self.RSQ = X[:, 3584:4608].bitcast(F32)
        self.RSK = X[:, 4608:5632].bitcast(F32)
        self.ROPE = X[:, 5632:7680].bitcast(F32).rearrange("p (a s) -> p a s", a=2)
        self.scr_tokens = [('MB', i) for i in range(4)] + ['CQT', 'CKVT', 'SQT', 'KRr', 'RSQ', 'RSK', 'ROPE']
        self.E = [sb("E%d" % k, [128, 512], F32) for k in range(3)]
        self.SPb = [sb("SPb%d" % k, [128, 512], BF16) for k in range(3)]
        self.MT = [self.E[0], self.E[1]]
        self.R32 = sb("R32", [128, 512], F32)
        self.Rbf = sb("Rbf", [128, 512], BF16)
        self.PT = [sb("PT%d" % k, [128, 512], BF16) for k in range(4)]
        self.SCraw = sb("SC", [128, max(S_, 2048)], F32)
        self.SC2 = sb("SC2", [128, S_], F32)
        self.SCs = [self.SCraw[:, 0:S_], self.SC2[:, :]]
        self.P2 = sb("P2", [128, 32], F32)
        self.DG = sb("DG", [128, 8, 128], BF16)
        self.OG = [sb("OG%d" % k, [128, 4, 256], BF16) for k in range(1)]
        self.EP = [sb("EP%d" % k, [128, 4, 64], F32) for k in range(4)]
        self.OGB = sb("OGB", [128, 1024], BF16)
        self.OGT = sb("OGT", [128, 8, 128], BF16)
        self.XR = self.XB[0]
        self.Z = self.XB[1]
        self.ZC = self.XBbraw[:, :].bitcast(F32)

    def fence(self, tokens):
        d = self.dummy
        self.S.op('pool', lambda: self.nc.gpsimd.memset(d[:, 0:1], 0.0), (), list(tokens))

    def setup_consts(self):
        cst_f = self.SCraw[:, 0:896].rearrange("p (a s) -> p a s", a=7)
        self.dma('sp', cst_f, self.d_consts, (), ['SC0'])
        self.cp('dve', self.cst_b[:], cst_f, ['SC0'], ['const'])
        self.cp('dve', self.CMf[:], cst_f[:, 3, :], ['SC0'], ['CMf'])
        self.cp('dve', self.IDf[:], cst_f[:, 0, :], ['SC0'], ['const'])
        relb_f = self.SCraw[:, 0:2048].rearrange("p (h a s) -> p h a s", h=8, a=2)
        self.dma('sp', relb_f, self.d_relb, ['SC0'], ['SC0'])
        self.dma('sp', self.c31[:], self.d_c31, (), ['c31'])
        for h in range(8):
            self.ts('dve', relb_f[:, h, :, :], relb_f[:, h, :, :], self.c31[:, h:h + 1], None,
                    ALU.subtract, None, ['SC0', 'c31'], ['SC0'])
            self.tt('dve', relb_f[:, h, 0, :], relb_f[:, h, 0, :], self.c_CMf, ALU.add,
                    ['SC0', 'CMf'], ['SC0'])
        self.cp('dve', self.BT[:], relb_f, ['SC0'], ['BT'])
        self.memset('pool', self.VA[:, :, :, 64:65], 1.0, ['VA'])
        for j in range(self.NIT):
            self.memset('pool', self.P2[:, j:j + 1], 2.0 ** -(j + 1), ['P2'])

    def load_piece(self, l, col0, ncols, tag):
        k = self.rr('wb', 2)
        self.dma('sp', self.WS[0][:, :, 0:ncols], self.d_win[l, :, :, col0:col0 + ncols], (), ['WS0'])
        self.cp('pool', self.WB[k][:, :, 0:ncols], self.WS[0][:, :, 0:ncols], ['WS0'], ['WB%d' % k])
        return self.WB[k], 'WB%d' % k

    def layer_setup(self, l):
        sm = self.sm
        lam_init = 0.8 - 0.6 * math.exp(-0.3 * l)
        self.lam_init = lam_init
        self.dma('sp', self.DL[:], self.d_dlam[l], (), ['DL'])
        self.dma('sp', self.G64[:], self.d_subln[l], (), ['G64'])
        self.dma('sp', self.QN[:], self.d_qn[l], (), ['QN'])
        self.dma('sp', self.KVN[:], self.d_kvn[l], (), ['KVN'])
        self.tt('dve', self.DL[:, 0:32], self.DL[:, 0:32], self.DL[:, 32:64], ALU.mult, ['DL'], ['DL'])
        self.tt('dve', self.DL[:, 64:96], self.DL[:, 64:96], self.DL[:, 96:128], ALU.mult, ['DL'], ['DL'])
        self.S.op('dve', lambda: self.nc.vector.tensor_reduce(out=sm[:, 0:1], in_=self.DL[:, 0:32], axis=AX.X, op=ALU.add),
                  ['DL'], ['sm0'])
        self.S.op('dve', lambda: self.nc.vector.tensor_reduce(out=sm[:, 1:2], in_=self.DL[:, 64:96], axis=AX.X, op=ALU.add),
                  ['DL'], ['sm1'])
        self.act(sm[:, 2:3], sm[:, 0:1], AF.Exp, ['sm0'], ['sm2'])
        self.act(sm[:, 3:4], sm[:, 1:2], AF.Exp, ['sm1'], ['sm3'])
        self.tt('dve', sm[:, 4:5], sm[:, 3:4], sm[:, 2:3], ALU.subtract, ['sm2', 'sm3'], ['sm4'])
        self.ts('dve', sm[:, 5:6], sm[:, 4:5], -lam_init, None, ALU.add, None, ['sm4'], ['neglam'])
        self.neglam = sm[:, 5:6]
        self.ts('dve', self.G64[:], self.G64[:], 1.0 - lam_init, None, ALU.mult, None, ['G64'], ['G64'])
        k = 0
        wsv = self.WS[k][:].rearrange("p a b -> p (a b)")
        self.dma('sp', wsv[:, 0:768], self.d_wuq[l].rearrange("p a b -> p (a b)"), (), ['WS%d' % k])
        for c in range(2):
            self.ts('dve', self.WUQb[:, c, :], wsv[:, c * 384:(c + 1) * 384], self.QN[:, c:c + 1], None,
                    ALU.mult, None, ['WS%d' % k, 'QN'], ['WUQb'])
        for c in range(2):
            for h in range(4):
                self.ts('dve', self.WROT[:, c, h, 0:16], self.WUQb[:, c, h * 96 + 80:h * 96 + 96], -1.0, None,
                        ALU.mult, None, ['WUQb'], ['WROT'])
                self.cp('dve', self.WROT[:, c, h, 16:32], self.WUQb[:, c, h * 96 + 64:h * 96 + 80], ['WUQb'], ['WROT'])
        k = 0
        wsv = self.WS[k][:].rearrange("p a b -> p (a b)")
        self.dma('sp', wsv[:, 0:512], self.d_wukv[l], (), ['WS%d' % k])
        self.ts('dve', self.WUKVb[:], wsv[:, 0:512], self.KVN[:, 0:1], None, ALU.mult, None,
                ['WS%d' % k, 'KVN'], ['WUKVb'])

    def xsrc(self, l, s):
        return self.d_x[s] if l == 0 else self.d_xmid[s]

    def build_xT(self, l, s):
        self.cur = (l, s)
        src = self.xsrc(l, s)
        b7 = self.bank[7][:].bitcast(BF16)
        if self.XS is not None:
            XB, XBb, (xt_, xbt_) = self.XS, self.XSb, self.xs_tok
            self.fence(['IQ'] + xt_ + xbt_)
        else:
            XB, XBb = [t[:] for t in self.XB], self.XBb
            xt_, xbt_ = ['XB0', 'XB1'], ['XBb0', 'XBb1']
        for tb in range(self.NB):
            k = self.rr('xb', 2)
            self.dma('sp', XB[k], src[tb * 128:(tb + 1) * 128, :], [('xm', s, tb)], [xt_[k]])
            self.cp('dve', XBb[k], XB[k], [xt_[k]], [xbt_[k]])
            for c in range(8):
                self.tr(b7[:, c * 128:(c + 1) * 128], XBb[k][:, c * 128:(c + 1) * 128], [xbt_[k]], ['b7'])
            self.evac(self.xT[:, :, tb * 128:(tb + 1) * 128], b7.rearrange("p (c t) -> p c t", c=8), ['b7'], ['xT'])
        if self.XS is not None:
            self.fence(['IQ'] + xt_ + xbt_)

    def proj_fm(self, wt, wtok, c0, M, dest_fn, extra_R=()):
        for tc in range(self.NJ):
            b = self.rr('pb', 4)
            ps = self.bank[b]
            for c in range(8):
                self.mm(ps[0:M, :], wt[:, c, c0:c0 + M], self.xT[:, c, tc * 512:(tc + 1) * 512], c == 0, c == 7,
                        [wtok, 'xT'] + list(extra_R), ['b%d' % b])
            dest_fn(tc, ps, 'b%d' % b)

    def proj_tm(self, wt, wtok, c0, N, dest_fn):
        for tb in range(self.NB):
            b = self.rr('pb', 4)
            ps = self.bank[b]
            for c in range(8):
                self.mm(ps[:, 0:N], self.xT[:, c, tb * 128:(tb + 1) * 128], wt[:, c, c0:c0 + N], c == 0, c == 7,
                        [wtok, 'xT'], ['b%d' % b])
            dest_fn(tb, ps, 'b%d' % b)

    def project_group(self, l, g):
        S_ = self.Sq
        gi = "ABCD".index(g)
        qname, kname, vname = {"A": ("a_q", "a_k", "a_v"), "B": ("b_q", "b_k", "b_v"),
                               "C": ("c_q", "c_k", "c_v"), "D": (None, None, None)}[g]
        wt, wtok = self.load_piece(l, OFF["gate"][0] + gi * 256, 256, 'gate')

        def gate_dest(tb, ps, btok):
            self.act(self.SG[:, tb, :], ps[:, 0:256], AF.Silu, [btok], ['SG'])
        self.proj_tm(wt, wtok, 0, 256, gate_dest)
        if g in "ABC":
            wt, wtok = self.load_piece(l, OFF[vname][0], 256, 'v')

            def v_dest(tb, ps, btok):
                self.evac(self.VA[:, tb, :, 0:64], ps[:, 0:256].rearrange("p (h d) -> p h d", h=4), [btok], ['VA'])
            self.proj_tm(wt, wtok, 0, 256, v_dest)
            if g == "B":
                qs = 32 ** -0.5
                wt, wtok = self.load_piece(l, OFF[qname][0], 256, qname)
                for h in range(4):
                    self.memset('dve', self.WX[:, :, 32:96], 0.0, ['WX'])
                    self.cp('dve', self.WX[:, :, 0:32], wt[:, :, h * 64:h * 64 + 32], [wtok], ['WX'])
                    self.cp('dve', self.WX[:, :, 96:128], wt[:, :, h * 64 + 32:h * 64 + 64], [wtok], ['WX'])

                    def qdest(tc, ps, btok, h=h):
                        self.evac(self.QT[:, h, tc * 512:(tc + 1) * 512], ps[:, :], [btok], ['QT'], scale=qs)
                    self.proj_fm(self.WX, 'WX', 0, 128, qdest)
                wt, wtok = self.load_piece(l, OFF[kname][0], 256, kname)
                for h in range(4):
                    self.cp('dve', self.WX[:, :, 0:64], wt[:, :, h * 64:h * 64 + 64], [wtok], ['WX'])
                    self.cp('dve', self.WX[:, :, 64:128], wt[:, :, h * 64:h * 64 + 64], [wtok], ['WX'])

                    def kdest(tc, ps, btok, h=h):
                        self.evac(self.KT[:, h, tc * 512:(tc + 1) * 512], ps[:, :], [btok], ['KT'])
                    self.proj_fm(self.WX, 'WX', 0, 128, kdest)
            else:
                qs = 64 ** -0.5
                for cg in range(2):
                    self.memset('dve', self.QT[64:128, 2 * cg, :], 0.0, ['QT'])
                    self.memset('dve', self.QT[0:64, 2 * cg + 1, :], 0.0, ['QT'])
                wt, wtok = self.load_piece(l, OFF[qname][0], 256, qname)
                for cg in range(2):
                    def qdest(tc, ps, btok, cg=cg):
                        tsl = slice(tc * 512, (tc + 1) * 512)
                        self.evac(self.QT[0:64, 2 * cg, tsl], ps[0:64, :], [btok], ['QT'], scale=qs)
                        self.evac(self.QT[64:128, 2 * cg + 1, tsl], ps[64:128, :], [btok], ['QT'], scale=qs)
                    self.proj_fm(wt, wtok, cg * 128, 128, qdest)
                wt, wtok = self.load_piece(l, OFF[kname][0], 256, kname)
                for cg in range(2):
                    def kdest(tc, ps, btok, cg=cg):
                        self.evac(self.KT[:, cg, tc * 512:(tc + 1) * 512], ps[:, :], [btok], ['KT'])
                    self.proj_fm(wt, wtok, cg * 128, 128, kdest)
        else:
            self.fence(self.scr_tokens)
            self.memset('dve', self.QT[96:128, :, :], 0.0, ['QT'])
            self.memset('dve', self.KT[96:128, :, :], 0.0, ['KT'])
            self.project_mla(l)
            self.fence(self.scr_tokens)

    def project_indexer(self, l):
            wt, wtok = self.load_piece(l, OFF["c_iq"][0], 256, 'iq')
            for cg in range(3):
                M = 96 if cg < 2 else 64

                def dest(tc, ps, btok, cg=cg, M=M):
                    self.evac(self.IQ[0:M, cg, tc * 512:(tc + 1) * 512], ps[0:M, :], [btok], ['IQ'], scale=32 ** -0.5)
                self.proj_fm(wt, wtok, cg * 96, M, dest)
            wt, wtok = self.load_piece(l, OFF["c_ik"][0], 40, 'ik')
            for r in range(3):
                self.cp('pool', self.WX[:, :, r * 32:(r + 1) * 32], wt[:, :, 0:32], [wtok], ['WX'])

            def ik_dest(tc, ps, btok):
                self.evac(self.IK3[0:96, tc * 512:(tc + 1) * 512], ps[0:96, :], [btok], ['IK3'])
            self.proj_fm(self.WX, 'WX', 0, 96, ik_dest)

            def iw_dest(tb, ps, btok):
                self.evac(self.IW[:, tb, :], ps[:, 0:8], [btok], ['IW'], scale=8 ** -0.5, eng='dve')
            self.proj_tm(wt, wtok, 32, 8, iw_dest)

    def project_mla(self, l):
        S_ = self.Sq
        sc_q = 96 ** -0.5
        wcq, tcq = self.load_piece(l, OFF["d_cq"][0], 256, 'cq')
        wkv, tkv = self.load_piece(l, OFF["d_ckv"][0], 160, 'ckv')
        self.ts('pool', self.WX[:, :, 0:16], wkv[:, :, 144:160], -1.0, None, ALU.mult, None, [tkv], ['WX'])
        self.cp('pool', self.WX[:, :, 16:32], wkv[:, :, 128:144], [tkv], ['WX'])
        bk = self.bank
        for tc in range(self.NJ):
            tsl = slice(tc * 512, (tc + 1) * 512)
            self.dma('sp', self.ROPE[:], self.d_rope[:, :, tsl].rearrange("a p t -> p a t"), (), ['ROPE'])
            Ct = self.ROPE[:, 0, :]
            St = self.ROPE[:, 1, :]
            for cg in range(3):
                b = self.rr('pb', 4)
                ps = bk[b]
                wt, wtok, c0 = (wcq, tcq, cg * 128) if cg < 2 else (wkv, tkv, 0)
                for c in range(8):
                    self.mm(ps[:, :], wt[:, c, c0:c0 + 128], self.xT[:, c, tsl], c == 0, c == 7, [wtok, 'xT'], ['b%d' % b])
                dst = self.CQT[:, cg, :] if cg < 2 else self.CKVT[:]
                self.cp('dve', dst, ps[:, :], ['b%d' % b], ['CQT' if cg < 2 else 'CKVT'])
                self.act(self.SQT[:, cg, :], ps[:, :], AF.Square, ['b%d' % b], ['SQT'])
            for which, ncg, dst, dtok, rank, scl in (('q', (0, 1), self.RSQ, 'RSQ', 256, sc_q), ('k', (2,), self.RSK, 'RSK', 128, 1.0)):
                b = self.rr('pb', 4)
                ps = bk[b]
                for j, cg in enumerate(ncg):
                    self.mm(ps[:, :], self.c_ones, self.SQT[:, cg, :], j == 0, j == len(ncg) - 1, ['const', 'SQT'], ['b%d' % b])
                self.act(dst[:], ps[:, :], AF.Ln, ['b%d' % b], [dtok], scale=1.0 / rank, bias=1e-6)
                self.act(dst[:], dst[:], AF.Exp, [dtok], [dtok], scale=-0.5)
                if scl != 1.0:
                    self.ts('dve', dst[:], dst[:], scl, None, ALU.mult, None, [dtok], [dtok])
            b = self.rr('pb', 4)
            b2 = self.rr('pb', 4)
            for c in range(8):
                self.mm(bk[b][64:96, :], wkv[:, c, 128:160], self.xT[:, c, tsl], c == 0, c == 7, [tkv, 'xT'], ['b%d' % b])
            for c in range(8):
                self.mm(bk[b2][64:96, :], self.WX[:, c, 0:32], self.xT[:, c, tsl], c == 0, c == 7, ['WX', 'xT'], ['b%d' % b2])
            m0, m1 = self.MT[0], self.MT[1]
            self.tt('dve', m0[64:96, :], bk[b][64:96, :], Ct[64:96, :], ALU.mult, ['b%d' % b, 'ROPE'], ['E0'])
            self.tt('dve', m1[64:96, :], bk[b2][64:96, :], St[64:96, :], ALU.mult, ['b%d' % b2, 'ROPE'], ['E1'])
            self.tt('pool', self.KRr[64:96, :], m0[64:96, :], m1[64:96, :], ALU.add, ['E0', 'E1'], ['KRr'])
            for h in range(4):
                b = self.rr('pb', 4)
                self.mm(bk[b][0:64, :], self.WUKVb[:, h * 128:h * 128 + 64], self.CKVT[:], True, True,
                        ['WUKVb', 'CKVT'], ['b%d' % b])
                self.tt('dve', self.KT[0:64, h, tsl], bk[b][0:64, :], self.RSK[0:64, :], ALU.mult,
                        ['b%d' % b, 'RSK'], ['KT'])
                self.cp('pool', self.KT[64:96, h, tsl], self.KRr[64:96, :], ['KRr'], ['KT'])
                b = self.rr('pb', 4)
                b2 = self.rr('pb', 4)
                for c in range(2):
                    self.mm(bk[b][0:96, :], self.WUQb[:, c, h * 96:(h + 1) * 96], self.CQT[:, c, :], c == 0, c == 1,
                            ['WUQb', 'CQT'], ['b%d' % b])
                for c in range(2):
                    self.mm(bk[b2][64:96, :], self.WROT[:, c, h, :], self.CQT[:, c, :], c == 0, c == 1,
                            ['WROT', 'CQT'], ['b%d' % b2])
                self.tt('dve', self.QT[0:64, h, tsl], bk[b][0:64, :], self.RSQ[0:64, :], ALU.mult,
                        ['b%d' % b, 'RSQ'], ['QT'])
                self.tt('dve', m0[64:96, :], bk[b][64:96, :], Ct[64:96, :], ALU.mult, ['b%d' % b, 'ROPE'], ['E0'])
                self.tt('dve', m1[64:96, :], bk[b2][64:96, :], St[64:96, :], ALU.mult, ['b%d' % b2, 'ROPE'], ['E1'])
                self.tt('pool', m0[64:96, :], m0[64:96, :], m1[64:96, :], ALU.add, ['E0', 'E1'], ['E0'])
                self.tt('pool', self.QT[64:96, h, tsl], m0[64:96, :], self.RSQ[64:96, :], ALU.mult, ['E0', 'RSQ'], ['QT'])
            for t4 in range(4):
                tb = tc * 4 + t4
                b = self.rr('pb', 4)
                ps = bk[b]
                for h in range(4):
                    self.mm(ps[:, h * 64:(h + 1) * 64], self.CKVT[:, t4 * 128:(t4 + 1) * 128],
                            self.WUKVb[:, h * 128 + 64:h * 128 + 128], True, True, ['CKVT', 'WUKVb'], ['b%d' % b])
                self.mm(ps[:, 256:257], self.SQT[:, 2, t4 * 128:(t4 + 1) * 128], self.c_ones[:, 0:1], True, True,
                        ['SQT', 'const'], ['b%d' % b])
                sm = self.sm
                self.act(sm[:, 8:9], ps[:, 256:257], AF.Ln, ['b%d' % b], ['sm8'], scale=1.0 / 128, bias=1e-6)
                self.act(sm[:, 9:10], sm[:, 8:9], AF.Exp, ['sm8'], ['sm9'], scale=-0.5)
                self.ts('dve', self.VA[:, tb, :, 0:64], ps[:, 0:256].rearrange("p (h d) -> p h d", h=4), sm[:, 9:10], None,
                        ALU.mult, None, ['b%d' % b, 'sm9'], ['VA'])

    def attend_group(self, l, g):
        self.sbanks = [0, 1, 2, 5, 6, 7] if g == "D" else [0, 1, 2, 5]
        for J in range(self.NJ):
            k = 0
            OG, ogtok = self.OG[k], 'OG%d' % k
            if g == "C":
                for il in range(4):
                    i = 4 * J + il
                    if self.need_sel(i):
                        Nk = 128 * (i + 1)
                        self.dma('sp', self.MB[:, il, 0:Nk], self.d_mb[i, :, 0:Nk], [('mbd', i)], [('MB', il)])
            for h in range(4):
                if g == "A":
                    self.attn_A(J, h, OG, ogtok)
                elif g == "B":
                    self.attn_B(J, h, OG, ogtok)
                elif g == "C":
                    self.attn_CD(J, h, OG, ogtok, True)
                else:
                    self.attn_CD(J, h, OG, ogtok, False)
            gi = "ABCD".index(g)
            dst = self.d_ogd[J * 512:(J + 1) * 512, gi * 256:(gi + 1) * 256].rearrange("(i p) c -> p i c", p=128)
            self.dma('pool', dst, OG[:], [ogtok], [('ogd', J * 4 + i) for i in range(4)])
            if g == "B" and "C" in self.groups and J + 1 < self.NJ:
                self.indexer(J + 1)

    def zero_group(self, g):
        gi = "ABCD".index(g)
        for J in range(self.NJ):
            k = 0
            OG, ogtok = self.OG[k], 'OG%d' % k
            self.memset('pool', OG[:], 0.0, [ogtok])
            dst = self.d_ogd[J * 512:(J + 1) * 512, gi * 256:(gi + 1) * 256].rearrange("(i p) c -> p i c", p=128)
            self.dma('pool', dst, OG[:], [ogtok], [('ogd', J * 4 + i) for i in range(4)])

    def sbank(self):
        b = self.sbanks[self.rr('sb', len(self.sbanks))]
        return self.bank[b], 'b%d' % b

    def pv(self, acc, acctok, PT, pttok, a, h, c0, first):
        self.mm(acc[0:65, c0:512], self.VA[:, a, h, :], PT[:, c0:512], bool(first), True, [pttok, 'VA'], [acctok])

    def acc_finish(self, acc, acctok):
        k = 0
        ot, ottok = self.OTs[k], 'OTs%d' % k
        self.evac(ot[:, :], acc[0:65, :], [acctok], [ottok])
        accT, acctT = self.sbank()
        nc = self.nc
        for il in range(4):
            o_ = accT[:, il * 65:(il + 1) * 65]
            i_ = ot[:, il * 128:(il + 1) * 128]
            idf = self.IDf[0:65, 0:65]
            self.S.op('pe', lambda o_=o_, i_=i_, idf=idf: nc.tensor.transpose(o_, i_, idf), [ottok, 'const'], [acctT], cost=110.0)
        return accT, acctT

    def attn_A(self, J, h, OG, ogtok):
        hb = (h % 2) * 64
        hs = h // 2
        ab = 3 + self.rr('acc', 2)
        acc, acctok = self.bank[ab], 'b%d' % ab
        self.memset('dve', acc[0:65, :], 0.0, [acctok])
        self.memset('pool', self.R32[:], 0.0, ['R32'])
        amax = 4 * J + 3
        for a in range(amax, -1, -1):
            m = a - 4 * J
            c0 = 128 * max(0, m)
            qsl = slice(J * 512 + c0, (J + 1) * 512)
            ksl = slice(a * 128, (a + 1) * 128)
            K = self.KT[:, hs, ksl]
            Q = self.QT[:, h, qsl]
            ps, pstok = self.sbank()
            self.mm(ps[:, c0:512], K, Q, True, True, ['KT', 'QT'], [pstok])
            e = self.rr('E', 3)
            E, etok = self.E[e], 'E%d' % e
            SP, sptok = self.SPb[e], 'SP%d' % e
            self.act(E[:, c0:512], ps[:, c0:512], AF.Exp, [pstok], [etok])
            self.act(SP[:, c0:512], E[:, c0:512], AF.Ln, [etok], [sptok], bias=1.0)
            if m >= 0:
                self.tt('pool', SP[:, c0:c0 + 128], SP[:, c0:c0 + 128], self.c_strictT, ALU.mult, [sptok, 'const'], [sptok])
            ps2, ps2tok = ps, pstok
            first = (a == amax)
            if m >= 0:
                self.mm(ps2[:, c0:c0 + 128], self.c_MA, self.c_ident, False, False, ['const', etok], [ps2tok])
            if not first:
                self.mm(ps2[:, c0:512], self.c_negones, self.Rbf[:, c0:512], False, False, ['const', 'Rbf', etok], [ps2tok])
            self.mm(ps2[:, c0:512], self.c_negtri, SP[:, c0:512], False, True, ['const', sptok], [ps2tok])
            p = self.rr('PT', 4)
            PT, pttok = self.PT[p], 'PT%d' % p
            self.act(PT[:, c0:512], ps2[:, c0:512], AF.Exp, [ps2tok], [pttok])
            self.pv(acc, acctok, PT, pttok, a, h, c0, False)
            if a > 0:
                self.tt('pool', self.R32[:, c0:512], self.R32[:, c0:512], SP[:, c0:512], ALU.add, ['R32', sptok], ['R32'])
                cn = 128 * max(0, m - 1)
                self.cp('pool', self.Rbf[:, cn:512], self.R32[:, cn:512], ['R32'], ['Rbf'])
        accT, acctT = self.acc_finish(acc, acctok)
        accv = accT[:, 0:260].rearrange("p (i d) -> p i d", d=65)
        self.tt('dve', OG[:, :, h * 64:(h + 1) * 64], accv[:, :, 0:64], self.SG[:, 4 * J:4 * J + 4, h * 64:(h + 1) * 64],
                ALU.mult, [acctT, 'SG'], [ogtok])

    def near_bias(self, ps, pstok, a, J, c0, bh):
        for il in range(c0 // 128, 4):
            i = 4 * J + il
            if i == a:
                self.mm(ps[:, il * 128:(il + 1) * 128], self.BT[:, bh, 0, :], self.c_ident, False, False, ['BT', 'const'], [pstok])
            elif i == a + 1:
                self.mm(ps[:, il * 128:(il + 1) * 128], self.BT[:, bh, 1, :], self.c_ident, False, False, ['BT', 'const'], [pstok])

    def softmax_norm(self, acc, acctok, dst, dsttok, ri):
        acc, acctok = self.acc_finish(acc, acctok)
        accv = acc[:, 0:260].rearrange("p (i d) -> p i d", d=65)
        rc = self.sm[:, 16 + 4 * ri:20 + 4 * ri]
        rtok = 'rc%d' % ri
        self.S.op('dve', lambda: self.nc.vector.reciprocal(out=rc, in_=accv[:, :, 64]), [acctok], [rtok])
        self.tt('dve', dst[:], accv[:, :, 0:64], rc.unsqueeze(2).broadcast_to([128, 4, 64]), ALU.mult,
                [acctok, rtok], [dsttok])

    def attn_B(self, J, h, OG, ogtok):
        accs = []
        for c in range(2):
            acc, acctok = self.bank[3 + c], 'b%d' % (3 + c)
            accs.append((acc, acctok))
        for a in range(4 * J + 4):
            m = a - 4 * J
            c0 = 128 * max(0, m)
            qsl = slice(J * 512 + c0, (J + 1) * 512)
            ksl = slice(a * 128, (a + 1) * 128)
            pss = [self.sbank() for c in range(2)]
            nc = self.nc
            o0, o1 = pss[0][0][:, c0:512], pss[1][0][:, c0:512]
            k0, k1 = self.KT[0:64, h, ksl], self.KT[64:128, h, ksl]
            q0, q1 = self.QT[0:64, h, qsl], self.QT[64:128, h, qsl]

            def qk2(o0=o0, o1=o1, k0=k0, k1=k1, q0=q0, q1=q1):
                nc.tensor.matmul(o0, lhsT=k0, rhs=q0, start=True, stop=False, skip_group_check=True)
                return nc.tensor.matmul(o1, lhsT=k1, rhs=q1, start=True, stop=False, skip_group_check=True)
            self.S.op('pe', qk2, ['KT', 'QT'], [pss[0][1], pss[1][1]], cost=(512 - c0 + 110) / 2.2)
            for c in range(2):
                ps, pstok = pss[c]
                self.near_bias(ps, pstok, a, J, c0, h)
            for c in range(2):
                acc, acctok = accs[c]
                ps, pstok = pss[c]
                p = self.rr('PT', 4)
                PT, pttok = self.PT[p], 'PT%d' % p
                self.act(PT[:, c0:512], ps[:, c0:512], AF.Exp, [pstok, 'c31'], [pttok], bias=self.c31[:, h:h + 1])
                self.pv(acc, acctok, PT, pttok, a, h, c0, a == 0)
        T0, T1, T2 = self.EP[0], self.EP[1], self.EP[2]
        self.softmax_norm(accs[0][0], accs[0][1], T0, 'EP0', 0)
        self.softmax_norm(accs[1][0], accs[1][1], T1, 'EP1', 1)
        f = lambda t: t[:].rearrange("p i d -> p (i d)")
        self.stt(f(T2), f(T1), self.neglam, f(T0), ALU.mult, ALU.add, ['EP0', 'EP1', 'neglam'], ['EP2'])
        self.tt('pool', T0[:], T2[:], T2[:], ALU.mult, ['EP2'], ['EP0'])
        ss = self.sm[:, 32:36]
        self.S.op('dve', lambda: self.nc.vector.tensor_reduce(out=ss, in_=T0[:], axis=AX.X, op=ALU.add), ['EP0'], ['ss'])
        self.act(ss, ss, AF.Ln, ['ss'], ['ss'], scale=1.0 / 64, bias=1e-6)
        self.act(ss, ss, AF.Exp, ['ss'], ['ss'], scale=-0.5)
        self.tt('dve', T1[:], T2[:], ss.unsqueeze(2).broadcast_to([128, 4, 64]), ALU.mult, ['EP2', 'ss'], ['EP1'])
        self.tt('pool', T0[:], T1[:], self.G64[:].unsqueeze(1).broadcast_to([128, 4, 64]), ALU.mult, ['EP1', 'G64'], ['EP0'])
        self.tt('pool', OG[:, :, h * 64:(h + 1) * 64], T0[:], self.SG[:, 4 * J:4 * J + 4, h * 64:(h + 1) * 64], ALU.mult,
                ['EP0', 'SG'], [ogtok])

    def need_sel(self, i):
        return 128 * (i + 1) > self.NSEL

    def attn_CD(self, J, h, OG, ogtok, isC):
        ab = 3 + self.rr('acc', 2)
        acc, acctok = self.bank[ab], 'b%d' % ab
        if isC:
            hs = h // 2
        else:
            hs = h
        for a in range(4 * J + 4):
            m = a - 4 * J
            c0 = 128 * max(0, m)
            qsl = slice(J * 512 + c0, (J + 1) * 512)
            ksl = slice(a * 128, (a + 1) * 128)
            ps, pstok = self.sbank()
            self.mm(ps[:, c0:512], self.KT[:, hs, ksl], self.QT[:, h, qsl], True, False, ['KT', 'QT'], [pstok])
            if isC:
                self.near_bias(ps, pstok, a, J, c0, 4 + h)
                for il in range(c0 // 128, 4):
                    i = 4 * J + il
                    if self.need_sel(i):
                        self.mm(ps[:, il * 128:(il + 1) * 128], self.MB[:, il, ksl], self.c_ident, False, False,
                                [('MB', il), 'const'], [pstok])
            elif m >= 0:
                self.mm(ps[:, c0:c0 + 128], self.c_CM, self.c_ident, False, False, ['const'], [pstok])
            p = self.rr('PT', 4)
            PT, pttok = self.PT[p], 'PT%d' % p
            if isC:
                self.act(PT[:, c0:512], ps[:, c0:512], AF.Exp, [pstok, 'c31'], [pttok], bias=self.c31[:, 4 + h:5 + h])
            else:
                self.act(PT[:, c0:512], ps[:, c0:512], AF.Exp, [pstok], [pttok])
            self.pv(acc, acctok, PT, pttok, a, h, c0, a == 0)
        e = self.rr('EPc', 2)
        T0, ttok = self.EP[e], 'EP%d' % e
        self.softmax_norm(acc, acctok, T0, ttok, 2 + e)
        self.tt('pool', OG[:, :, h * 64:(h + 1) * 64], T0[:], self.SG[:, 4 * J:4 * J + 4, h * 64:(h + 1) * 64], ALU.mult,
                [ttok, 'SG'], [ogtok])

    def indexer_all(self):
        for J in range(self.NJ):
            self.indexer(J)

    def indexer(self, J):
        sm = self.sm
        nc = self.nc
        for il in range(4):
            i = 4 * J + il
            if not self.need_sel(i):
                continue
            Nk = 128 * (i + 1)
            isl = slice(i * 128, (i + 1) * 128)
            kq = self.rr('SCq', 2)
            SC, sctk = self.SCs[kq], 'SC%d' % kq
            for hh in range(8):
                self.ts('pool', self.DG[:, hh, :], self.c_ident, self.IW[:, i, hh:hh + 1], 0.0, ALU.mult, ALU.add,
                        ['const', 'IW'], ['DG'])
            for kc in range((Nk + 511) // 512):
                n = min(512, Nk - kc * 512)
                ksl = slice(kc * 512, kc * 512 + n)
                sc, sctok = self.bank[7], 'b7'
                for hh in range(8):
                    cg, rb = hh // 3, (hh % 3) * 32
                    d = 6
                    dps, dtok = self.bank[d], 'b%d' % d
                    self.mm(dps[:, 0:n], self.IQ[rb:rb + 32, cg, isl], self.IK3[rb:rb + 32, ksl], True, True, ['IQ', 'IK3'], [dtok])
                    r = self.rr('RL', 3)
                    RL, rtok = self.RL[r], 'RL%d' % r
                    if True:
                        self.act(RL[:, 0:n], dps[:, 0:n], AF.Relu, [dtok], [rtok])
                    else:
                        self.ts('dve', RL[:, 0:n], dps[:, 0:n], 0.0, None, ALU.max, None, [dtok], [rtok])
                    self.mm(sc[:, 0:n], self.DG[:, hh, :], RL[:, 0:n], hh == 0, hh == 7, ['DG', rtok], [sctok])
                self.cp('act', SC[:, ksl], sc[:, 0:n], [sctok], [sctk])
            SCv = SC[:, 0:Nk]
            AM, LO, W0, T, CNT, V = (sm[:, 36:37], sm[:, 37:38], sm[:, 38:39], sm[:, 39:40], sm[:, 40:41], sm[:, 41:42])
            WJ = sm[:, 44:44 + self.NIT]
            self.S.op('dve', lambda SCv=SCv, AM=AM: nc.vector.tensor_reduce(out=AM, in_=SCv, axis=AX.X, op=ALU.max,
                                                                         apply_absolute_value=True), [sctk], ['AM'])
            self.tt('pool', SC[:, Nk - 128:Nk], SC[:, Nk - 128:Nk], self.c_CMf, ALU.add, [sctk, 'CMf'], [sctk])
            self.ts('dve', W0, AM, 2.002, None, ALU.mult, None, ['AM'], ['W0'])
            self.ts('dve', WJ, self.P2[:, 0:self.NIT], W0, None, ALU.mult, None, ['P2', 'W0'], ['WJ'])
            self.memset('dve', T, 0.0, ['T'])
            for j in range(self.NIT):
                self.ts('dve', self.JK[:, 0:Nk], SCv, T, 0.0, ALU.is_ge, ALU.add, [sctk, 'T'], ['JK', 'CNT'], accum=CNT)
                self.ts('dve', V, CNT, float(self.NSEL), 0.5, ALU.is_ge, ALU.subtract, ['CNT'], ['V'])
                self.stt(T, V, WJ[:, j:j + 1], T, ALU.mult, ALU.add, ['V', 'WJ', 'T'], ['T'])
            self.stt(LO, WJ[:, self.NIT - 1:self.NIT], -0.5, T, ALU.mult, ALU.add, ['WJ', 'T'], ['LO'])
            self.ts('dve', self.JK[:, 0:Nk], SCv, LO, NEG, ALU.is_lt, ALU.mult, [sctk, 'LO'], ['JK'])
            self.dma('pool', self.d_mb[i, :, 0:Nk], self.JK[:, 0:Nk], ['JK'], [('mbd', i)])

    def output_phase(self, l, s):
        nc = self.nc
        sm = self.sm
        src = self.xsrc(l, s)
        dst = self.d_out[s] if l == self.L - 1 else self.d_xmid[s]
        b7 = self.bank[7][:].bitcast(BF16)
        for pz in range(4):
            self.dma('sp', self.WS[0][:], self.d_wout[l, :, :, pz * 256:(pz + 1) * 256], (), ['WS0'])
            self.cp('pool', self.WO[:, :, pz * 256:(pz + 1) * 256], self.WS[0][:], ['WS0'], ['QT'])
        self.dma('sp', self.LNG, self.d_lnp[l, 0], (), ['KT'])
        self.dma('sp', self.LNB, self.d_lnp[l, 1], ['KT'], ['KT'])
        sets = [dict(XR=self.XR[:], XRt='XB0', Z=self.Z[:], Zt='XB1', ZC=self.ZC, ZCt=['XBb0', 'XBb1'],
                     OGB=self.OGB[:], OGBt='OGB', OGT=self.OGT[:], OGTt='OGT')]
        if self.Sq >= 2048:
            sets.append(
                dict(XR=self.SCraw[:, 0:1024], XRt='SC0', Z=self.SCraw[:, 1024:2048], Zt='SC0',
                     ZC=self.SC2[:, 0:1024], ZCt=['SC1'],
                     OGB=self.SC2[:, 1024:1536].bitcast(BF16), OGBt='SC1',
                     OGT=self.SC2[:, 1536:2048].bitcast(BF16).rearrange("p (c t) -> p c t", c=8), OGTt='SC1'))
        for tb in range(self.NB):
            tsl = slice(tb * 128, (tb + 1) * 128)
            B_ = sets[tb % len(sets)]
            XR, Z, ZC, OGB, OGT = B_['XR'], B_['Z'], B_['ZC'], B_['OGB'], B_['OGT']
            XRt, Zt, ZCt, OGBt, OGTt = B_['XRt'], B_['Zt'], B_['ZCt'], B_['OGBt'], B_['OGTt']
            b7i = 7 if tb % 2 == 0 else 4
            b7 = self.bank[b7i][:].bitcast(BF16)
            b7t = 'b%d' % b7i
            self.dma('sp', OGB, self.d_ogd[tsl, :], [('ogd', tb)], [OGBt])
            self.dma('sp', XR, src[tsl, :], [('xm', s, tb)], [XRt])
            for c in range(8):
                self.tr(b7[:, c * 128:(c + 1) * 128], OGB[:, c * 128:(c + 1) * 128], [OGBt], [b7t])
            self.cp('act', OGT, b7.rearrange("p (c t) -> p c t", c=8), [b7t], [OGTt])
            sa = 48 + 8 * (tb % 2)
            for half in range(2):
                bi = (5 + half) if tb % 2 == 0 else (2 + half)
                ps, pstok = self.bank[bi], 'b%d' % bi
                for c in range(8):
                    self.mm(ps[:, :], OGT[:, c, :], self.WO[:, c, half * 512:(half + 1) * 512], c == 0, c == 7,
                            [OGTt, 'QT'], [pstok])
                hs = slice(half * 512, (half + 1) * 512)
                self.stt(Z[:, hs], XR[:, hs], ALPHA, ps[:, :], ALU.mult, ALU.add, [XRt, pstok], [Zt, ('su', tb % 2, half)],
                         accum=sm[:, sa + half:sa + half + 1])
            nm, ssq, lv, rstd = sm[:, sa + 2:sa + 3], sm[:, sa + 3:sa + 4], sm[:, sa + 4:sa + 5], sm[:, sa + 5:sa + 6]
            u = tb % 2
            self.tt('dve', nm, sm[:, sa:sa + 1], sm[:, sa + 1:sa + 2], ALU.add, [('su', u, 0), ('su', u, 1)], [('nm', u)])
            self.ts('dve', nm, nm, -1.0 / D_MODEL, None, ALU.mult, None, [('nm', u)], [('nm', u)])
            self.act(ZC, Z, AF.Identity, [Zt, ('nm', u)], ZCt, bias=nm)
            self.act(Z, ZC, AF.Square, ZCt, [Zt, ('ssq', u)], accum=ssq)
            self.act(lv, ssq, AF.Ln, [('ssq', u)], [('lv', u)], scale=1.0 / D_MODEL, bias=NORM_EPS)
            self.act(rstd, lv, AF.Exp, [('lv', u)], [('rstd', u)], scale=-0.5)
            self.stt(Z, ZC, rstd, self.LNG, ALU.mult, ALU.mult, ZCt + [('rstd', u), 'KT', Zt], [Zt])
            self.tt('pool', ZC, Z, self.LNB, ALU.add, [Zt, 'KT'], ZCt)
            W = [('xm', s, tb)] if l < self.L - 1 else [('out', s, tb)]
            self.dma('pool', dst[tsl, :], ZC, ZCt, W)


def t5_bucket_np(dist):
    max_exact = 16
    n = np.maximum(dist, 0)
    nf = np.maximum(n, max_exact).astype(np.float32)
    large = max_exact + (np.log(nf / np.float32(max_exact)) / np.float32(math.log(128 / max_exact))
                         * np.float32(32 - max_exact)).astype(np.int32)
    large = np.minimum(large, 31)
    return np.where(n < max_exact, n, large)


def host_consts(S):
    q = np.arange(128)[:, None]
    k = np.arange(128)[None, :]
    c = np.zeros((128, 7, 128), np.float32)
    c[:, 0] = np.eye(128, dtype=np.float32)
    c[:, 1] = np.where(q >= k, -1.0, 0.0)
    c[:, 2] = -1.0
    c[:, 3] = np.where(k <= q, 0.0, NEG)
    c[:, 4] = np.where(k < q, 0.0, NEG)
    c[:, 5] = np.where(q < k, 1.0, 0.0)
    c[:, 6] = 1.0
    pos = np.arange(S, dtype=np.float32)
    inv_freq = (np.float32(10000.0) ** (-np.arange(16, dtype=np.float32) / np.float32(16))).astype(np.float32)
    ang = pos[:, None] * inv_freq[None, :]
    rope = np.zeros((2, 128, S), np.float32)
    rope[0, 64:80] = np.cos(ang).T
    rope[0, 80:96] = np.cos(ang).T
    rope[1, 64:80] = np.sin(ang).T
    rope[1, 80:96] = np.sin(ang).T
    return c, rope


def host_layout(inp, S, L):
    f = lambda a: np.ascontiguousarray(np.asarray(a, dtype=np.float32))
    w_in = f(inp["w_in"])[:L]
    w_out = f(inp["w_out"])[:L]
    d = {}
    d["w_in_l"] = f(w_in.reshape(L, 8, 128, D_IN).transpose(0, 2, 1, 3))
    d["w_out_l"] = f(w_out.reshape(L, 8, 128, D_MODEL).transpose(0, 2, 1, 3))
    lnp = np.stack([f(inp["ln_g"])[:L], f(inp["ln_b"])[:L]], axis=1)
    d["lnp"] = f(np.broadcast_to(lnp[:, :, None, :], (L, 2, 128, D_MODEL)))
    rb = f(inp["rel_bias"])
    q = np.arange(128)[:, None]
    k = np.arange(128)[None, :]
    bd = t5_bucket_np(np.maximum(q - k, 0))
    bo = t5_bucket_np(128 + q - k)
    relb = np.zeros((128, 8, 2, 128), np.float32)
    relb[:, :, 0, :] = rb[bd].transpose(0, 2, 1)
    relb[:, :, 1, :] = rb[bo].transpose(0, 2, 1)
    d["relb"] = relb
    d["c31"] = f(np.broadcast_to(rb[31][None, :], (128, 8)))
    d["dlam"] = f(np.broadcast_to(f(inp["diff_lambda"])[:L].reshape(L, 1, 128), (L, 128, 128)))
    d["subln"] = f(np.broadcast_to(f(inp["diff_subln"])[:L][:, None, :], (L, 128, 64)))
    d["qn"] = f(f(inp["mla_q_norm"])[:L].reshape(L, 2, 128).transpose(0, 2, 1))
    d["kvn"] = f(f(inp["mla_kv_norm"])[:L].reshape(L, 128, 1))
    d["wuq"] = f(f(inp["mla_w_uq"])[:L].reshape(L, 2, 128, 384).transpose(0, 2, 1, 3))
    d["wukv"] = f(inp["mla_w_ukv"])[:L]
    c, rope = host_consts(S)
    d["consts"] = c
    d["rope"] = rope
    return d


_CACHE = {}


def run(inp, S, NSEQ, L, n_cores, NIT=13, groups="ABCD", core0=0):
    key = (S, NSEQ, L, NIT, groups)
    if key not in _CACHE:
        b = Builder(S=S, NSEQ=NSEQ, L=L, NIT=NIT, groups=groups)
        nc = b.build()
        print("built: ops/waits/sems/cnt/ndma", b.S.stats, flush=True)
        _CACHE[key] = nc
    nc = _CACHE[key]
    shared = host_layout(inp, S, L)
    x = np.ascontiguousarray(np.asarray(inp["x"], dtype=np.float32))
    in_maps = []
    for c in range(n_cores):
        m = dict(shared)
        m["x"] = np.ascontiguousarray(x[c * NSEQ:(c + 1) * NSEQ])
        in_maps.append(m)
    res = run_bass_kernel_spmd(nc, in_maps, core_ids=list(range(core0, core0 + n_cores)))
    return np.concatenate([np.asarray(r["out"]) for r in res.results], axis=0).astype(np.float32)


def kernel(x, w_in, w_out, ln_g, ln_b, rel_bias, diff_lambda, diff_subln,
           mla_q_norm, mla_kv_norm, mla_w_uq, mla_w_ukv):
    inp = dict(x=x, w_in=w_in, w_out=w_out, ln_g=ln_g, ln_b=ln_b, rel_bias=rel_bias, diff_lambda=diff_lambda,
               diff_subln=diff_subln, mla_q_norm=mla_q_norm, mla_kv_norm=mla_kv_norm, mla_w_uq=mla_w_uq,
               mla_w_ukv=mla_w_ukv)
    return run(inp, 2048, 2, DEPTH, 8)
```

```python
import math
from contextlib import ExitStack

import numpy as np
import concourse.bass as bass
import concourse.mybir as mybir
from concourse.bass_utils import run_bass_kernel_spmd

F32 = mybir.dt.float32
BF16 = mybir.dt.bfloat16
AF = mybir.ActivationFunctionType
ALU = mybir.AluOpType
AX = mybir.AxisListType

D_MODEL = 1024
DEPTH = 2
ALPHA = (2.0 * DEPTH) ** 0.25
NORM_EPS = 1e-5
NEG = -30000.0
D_IN = 4040

OFF = {}
_o = 0
for _n, _w in (("a_q", 256), ("a_k", 256), ("a_v", 256), ("b_q", 256), ("b_k", 256), ("b_v", 256),
               ("c_q", 256), ("c_k", 256), ("c_v", 256), ("c_iq", 256), ("c_ik", 32), ("c_iw", 8),
               ("d_cq", 256), ("d_ckv", 128), ("d_kr", 32), ("gate", 1024)):
    OFF[_n] = (_o, _w)
    _o += _w


class Sched:
    CH = 20000
    K_DMA = 8

    def __init__(self, nc, es):
        self.nc = nc
        self.es = es
        self.ops = []
        self.eng = {'pe': nc.tensor, 'act': nc.scalar, 'dve': nc.vector, 'pool': nc.gpsimd, 'sp': nc.sync}
        self.reorder = True

    def op(self, eng, fn, reads=(), writes=(), dma=False, cost=None):
        isb = lambda t: isinstance(t, str) and len(t) == 2 and t[0] == 'b' and t[1].isdigit()
        br = [t for t in reads if isb(t)]
        if br:
            reads = [t for t in reads if not isb(t)]
            writes = list(writes) + [t for t in br if t not in writes]
        if cost is None:
            cost = 2500.0 if dma else 300.0
        self.ops.append(dict(eng=eng, fn=fn, reads=tuple(reads), writes=tuple(writes), dma=dma,
                             signal=False, deps=(), cost=float(cost)))

    def schedule(self):
        import heapq
        ops = self.ops
        n = len(ops)
        succ = [[] for _ in range(n)]
        indeg = [0] * n
        for i, o in enumerate(ops):
            indeg[i] = len(o['alldeps'])
            for p in o['alldeps']:
                succ[p].append(i)
        fin = [0.0] * n
        ready_t = [0.0] * n
        engs = list(self.eng)
        heaps = {e: [] for e in engs}
        now = {e: [] for e in engs}
        free = {e: 0.0 for e in engs}
        for i, o in enumerate(ops):
            if indeg[i] == 0:
                heapq.heappush(heaps[o['eng']], (0.0, i))
        order = []
        while len(order) < n:
            best = None
            for e in engs:
                h, nw = heaps[e], now[e]
                while h and h[0][0] <= free[e]:
                    heapq.heappush(nw, heapq.heappop(h)[1])
                if nw:
                    cand = (free[e], nw[0], e, True)
                elif h:
                    cand = (h[0][0], h[0][1], e, False)
                else:
                    continue
                if best is None or cand[:2] < best[:2]:
                    best = cand
            st, i, e, from_now = best
            if from_now:
                heapq.heappop(now[e])
            else:
                heapq.heappop(heaps[e])
            o = ops[i]
            if o['dma']:
                free[e] = st + 150.0
                fin[i] = st + o['cost']
            else:
                free[e] = st + o['cost']
                fin[i] = st + o['cost'] + 60.0
            order.append(i)
            for j in succ[i]:
                indeg[j] -= 1
                if fin[i] > ready_t[j]:
                    ready_t[j] = fin[i]
                if indeg[j] == 0:
                    heapq.heappush(heaps[ops[j]['eng']], (ready_t[j], j))
        self.est_ns = max(fin) if fin else 0.0
        self.fin = fin
        self.eng_busy = {e: sum(o['cost'] if not o['dma'] else 150.0 for o in ops if o['eng'] == e) / 1e6 for e in engs}
        return order

    def finalize(self):
        nc = self.nc
        ops = self.ops
        last_w = {}
        readers = {}
        for i, o in enumerate(ops):
            deps = set()
            for r in o['reads']:
                if r in last_w:
                    deps.add(last_w[r])
            for w in o['writes']:
                if w in last_w:
                    deps.add(last_w[w])
                rd = readers.get(w)
                if rd:
                    deps.update(rd)
            deps.discard(i)
            o['alldeps'] = sorted(deps)
            for w in o['writes']:
                last_w[w] = i
                readers[w] = []
            for r in o['reads']:
                if r not in o['writes']:
                    readers.setdefault(r, []).append(i)
        order = self.schedule() if self.reorder else list(range(len(ops)))
        pos = [0] * len(ops)
        for k, i in enumerate(order):
            pos[i] = k
        for i in order:
            o = ops[i]
            latest = {}
            keep = []
            for p in o['alldeps']:
                po = ops[p]
                if po['dma']:
                    keep.append(p)
                    continue
                if po['eng'] == 'pe' and o['eng'] == 'pe' and not o['dma']:
                    continue
                e = po['eng']
                if e not in latest or pos[p] > pos[latest[e]]:
                    latest[e] = p
            keep.extend(latest.values())
            for p in keep:
                ops[p]['signal'] = True
            o['deps'] = keep
        sems = {}

        def getsem(key):
            if key not in sems:
                sems[key] = self.es.enter_context(nc.semaphore("s_%s_%s" % key))
            return sems[key]
        cnt = {e: 0 for e in self.eng}
        ndma = {e: 0 for e in self.eng}
        dma_sig = {e: [] for e in self.eng}
        waited = {e: {} for e in self.eng}
        self.nwaits = 0

        def do_wait(E, sig):
            sem, val = sig
            k = id(sem)
            if waited[E].get(k, 0) < val:
                self.eng[E].wait_ge(sem, val)
                waited[E][k] = val
                self.nwaits += 1
        for i in order:
            o = ops[i]
            E = o['eng']
            for p in o['deps']:
                do_wait(E, ops[p]['sig'])
            if o['dma']:
                n = ndma[E]
                if n >= self.K_DMA:
                    do_wait(E, dma_sig[E][n - self.K_DMA])
                sem = getsem((E + 'd', n % self.K_DMA))
                val = 16 * (n // self.K_DMA + 1)
                ins = o['fn']()
                ins.then_inc(sem, 16)
                o['sig'] = (sem, val)
                dma_sig[E].append(o['sig'])
                ndma[E] = n + 1
            else:
                ins = o['fn']()
                if o['signal']:
                    c = cnt[E]
                    sem = getsem((E, c // self.CH))
                    ins.then_inc(sem, 1)
                    o['sig'] = (sem, c % self.CH + 1)
                    cnt[E] = c + 1
            o['fn'] = None
        for E in self.eng:
            for s in dma_sig[E][-self.K_DMA:]:
                do_wait('sp', s)
        self.nc.sync.nop()
        self.stats = (len(ops), self.nwaits, len(sems), dict(cnt), dict(ndma), getattr(self, 'est_ns', 0.0) / 1e6, getattr(self, 'eng_busy', None))


class Builder:
    def __init__(self, S=2048, NSEQ=2, L=2, NIT=13, groups="ABCD"):
        self.Sq = S
        self.NSEQ = NSEQ
        self.L = L
        self.NIT = NIT
        self.groups = groups
        self.NB = S // 128
        self.NJ = S // 512
        self.NSEL = min(256, S // 4)
        self.rot = {}
        self.debug = False
        self.dbg_names = []
        self.marks = []

    def mm(self, out, lhsT, rhs, start, stop, R, W):
        nc = self.nc
        N = rhs.free_size()
        self.S.op('pe', lambda: nc.tensor.matmul(out, lhsT=lhsT, rhs=rhs, start=start, stop=stop,
                                                 skip_group_check=True), R, W, cost=(max(N, 64) + 110) / 2.2)

    def tr(self, out, in_, R, W):
        nc = self.nc
        idt = self.c_ident
        self.S.op('pe', lambda: nc.tensor.transpose(out, in_, idt), list(R) + ['const'], W, cost=110.0)

    def act(self, out, in_, func, R, W, scale=1.0, bias=0.0, accum=None):
        nc = self.nc
        self.S.op('act', lambda: nc.scalar.activation(out=out, in_=in_, func=func, bias=bias, scale=scale,
                                                      accum_out=accum), R, W, cost=in_.free_size() * 0.9 + 180)

    def ts(self, eng, out, in0, s1, s2, op0, op1, R, W, accum=None):
        eng = 'dve'
        e = self.nc.vector if eng == 'dve' else self.nc.gpsimd
        c = in0.free_size() * 1.05 + 120
        if op1 is None:
            self.S.op(eng, lambda: e.tensor_scalar(out=out, in0=in0, scalar1=s1, scalar2=None, op0=op0), R, W, cost=c)
        else:
            self.S.op(eng, lambda: e.tensor_scalar(out=out, in0=in0, scalar1=s1, scalar2=s2, op0=op0, op1=op1,
                                                   accum_out=accum), R, W, cost=c)

    def tt(self, eng, out, in0, in1, op, R, W):
        eng = 'dve'
        e = self.nc.vector if eng == 'dve' else self.nc.gpsimd
        self.S.op(eng, lambda: e.tensor_tensor(out=out, in0=in0, in1=in1, op=op), R, W, cost=in0.free_size() * 1.3 + 120)

    def stt(self, out, in0, scalar, in1, op0, op1, R, W, accum=None):
        nc = self.nc
        self.S.op('dve', lambda: nc.vector.scalar_tensor_tensor(out=out, in0=in0, scalar=scalar, in1=in1,
                                                                op0=op0, op1=op1, accum_out=accum), R, W,
                  cost=in0.free_size() * 1.3 + 120)

    def cp(self, eng, out, in_, R, W):
        nc = self.nc
        if eng == 'pool':
            eng = 'dve'
        N = in_.free_size()
        if eng == 'act':
            self.S.op('act', lambda: nc.scalar.copy(out=out, in_=in_), R, W, cost=N * 0.9 + 180)
        elif eng == 'dve':
            self.S.op('dve', lambda: nc.vector.tensor_copy(out=out, in_=in_), R, W, cost=N * 0.6 + 120)
        else:
            self.S.op('pool', lambda: nc.gpsimd.tensor_copy(out=out, in_=in_), R, W, cost=N * 0.6 + 250)

    def memset(self, eng, ap, val, W):
        eng = 'dve'
        e = self.nc.vector if eng == 'dve' else self.nc.gpsimd
        self.S.op(eng, lambda: e.memset(ap, val), (), W, cost=ap.free_size() * 0.6 + 120)

    def dma(self, q, out, in_, R, W):
        e = {'sp': self.nc.sync, 'pool': self.nc.gpsimd, 'act': self.nc.scalar}[q]
        self.S.op(q, lambda: e.dma_start(out=out, in_=in_), R, W, dma=True, cost=2200 + out.free_size() * 128 * 4 / 150.0)

    def dbg(self, name, ap, toks):
        if not getattr(self, 'debug', False):
            return
        d = self.nc.dram_tensor("dbg_" + name, list(ap.shape), ap.dtype, kind="ExternalOutput").ap()
        self.dma('sp', d, ap, toks, [('dbg', name)])
        self.dbg_names.append("dbg_" + name)

    def mark(self, label):
        self.marks.append((label, len(self.S.ops)))

    def rr(self, name, n):
        v = self.rot.get(name, 0)
        self.rot[name] = v + 1
        return v % n

    def evac_eng(self):
        return ('act', 'dve')[self.rr('evac', 2)]

    def evac(self, out, in_, R, W, scale=1.0, eng=None):
        eng = eng or self.evac_eng()
        if eng == 'act':
            if scale == 1.0:
                self.cp('act', out, in_, R, W)
            else:
                self.act(out, in_, AF.Copy, R, W, scale=scale)
        else:
            if scale == 1.0:
                self.cp('dve', out, in_, R, W)
            else:
                self.ts('dve', out, in_, scale, None, ALU.mult, None, R, W)

    def build(self):
        S_, NSEQ, L, NB = self.Sq, self.NSEQ, self.L, self.NB
        nc = bass.Bass("TRN2", target_bir_lowering=False)
        self.nc = nc
        dt_in = lambda name, shape: nc.dram_tensor(name, list(shape), F32, kind="ExternalInput").ap()
        self.d_x = dt_in("x", [NSEQ, S_, D_MODEL])
        self.d_win = dt_in("w_in_l", [L, 128, 8, D_IN])
        self.d_wout = dt_in("w_out_l", [L, 128, 8, D_MODEL])
        self.d_lnp = dt_in("lnp", [L, 2, 128, D_MODEL])
        self.d_relb = dt_in("relb", [128, 8, 2, 128])
        self.d_c31 = dt_in("c31", [128, 8])
        self.d_dlam = dt_in("dlam", [L, 128, 128])
        self.d_subln = dt_in("subln", [L, 128, 64])
        self.d_qn = dt_in("qn", [L, 128, 2])
        self.d_kvn = dt_in("kvn", [L, 128, 1])
        self.d_wuq = dt_in("wuq", [L, 128, 2, 384])
        self.d_wukv = dt_in("wukv", [L, 128, 512])
        self.d_consts = dt_in("consts", [128, 7, 128])
        self.d_rope = dt_in("rope", [2, 128, S_])
        self.d_out = nc.dram_tensor("out", [NSEQ, S_, D_MODEL], F32, kind="ExternalOutput").ap()
        self.d_xmid = nc.dram_tensor("xmid", [NSEQ, S_, D_MODEL], F32, kind="Internal").ap()
        self.d_ogd = nc.dram_tensor("ogd", [S_, D_MODEL], BF16, kind="Internal").ap()
        self.d_mb = nc.dram_tensor("mbd", [NB, 128, S_], BF16, kind="Internal").ap()
        es = ExitStack()
        with es:
            self.S = Sched(nc, es)
            self.alloc(es)
            self.setup_consts()
            for l in range(L):
                self.layer_setup(l)
                for s in range(NSEQ):
                    self.mark('xT %d %d' % (l, s))
                    self.build_xT(l, s)
                    self.dbg('xT', self.xT[:], ['xT'])
                    if "C" in self.groups:
                        self.mark('proj I')
                        self.project_indexer(l)
                    for g in "ABDC":
                        if g in self.groups:
                            self.mark('proj ' + g)
                            self.project_group(l, g)
                            self.dbg('QT' + g, self.QT, ['QT'])
                            self.dbg('KT' + g, self.KT, ['KT'])
                            self.dbg('VA' + g, self.VA[:], ['VA'])
                            self.dbg('SG' + g, self.SG[:], ['SG'])
                            self.mark('att ' + g)
                            self.attend_group(l, g)
                            self.dbg('OG' + g, self.OG[0][:], ['OG0'])
                        else:
                            self.zero_group(g)
                        if g == "A" and "C" in self.groups:
                            self.mark('indexer')
                            if "B" in self.groups:
                                self.indexer(0)
                            else:
                                self.indexer_all()
                    self.mark('out')
                    self.output_phase(l, s)
            self.S.finalize()
        return nc

    def alloc(self, es):
        nc = self.nc
        S_, NB = self.Sq, self.NB
        sb = lambda name, shape, dt: es.enter_context(nc.sbuf_tensor("sb_" + name, list(shape), dt))
        self.bank = [es.enter_context(nc.psum_tensor("bank%d" % k, [128, 512], F32)) for k in range(8)]
        self.cst_b = sb("cst_b", [128, 7, 128], BF16)
        self.c_ident = self.cst_b[:, 0, :]
        self.c_negtri = self.cst_b[:, 1, :]
        self.c_negones = self.cst_b[:, 2, :]
        self.c_CM = self.cst_b[:, 3, :]
        self.c_MA = self.cst_b[:, 4, :]
        self.c_strictT = self.cst_b[:, 5, :]
        self.c_ones = self.cst_b[:, 6, :]
        self.CMf = sb("CMf", [128, 128], F32)
        self.IDf = sb("IDf", [128, 128], F32)
        self.OTs = [sb("OTs%d" % k, [65, 512], F32) for k in range(1)]
        self.c_CMf = self.CMf[:]
        self.BT = sb("BT", [128, 8, 2, 128], BF16)
        self.c31 = sb("c31", [128, 8], F32)
        self.dummy = sb("dummy", [128, 2], F32)
        self.DL = sb("DL", [128, 128], F32)
        self.G64 = sb("G64", [128, 64], F32)
        self.sm = sb("sm", [128, 96], F32)
        self.QN = sb("QN", [128, 2], F32)
        self.KVN = sb("KVN", [128, 1], F32)
        self.WUQb = sb("WUQb", [128, 2, 384], BF16)
        self.WROT = sb("WROT", [128, 2, 4, 32], BF16)
        self.WUKVb = sb("WUKVb", [128, 512], BF16)
        self.WS = [sb("WS0", [128, 8, 256], F32)]
        self.WB = [sb("WB%d" % k, [128, 8, 256], BF16) for k in range(2)]
        self.WX = sb("WX", [128, 8, 128], BF16)
        self.xT = sb("xT", [128, 8, S_], BF16)
        self.XB = [sb("XB%d" % k, [128, 1024], F32) for k in range(2)]
        self.XBbraw = sb("XBbraw", [128, 2048], BF16)
        self.XBb = [self.XBbraw[:, k * 1024:(k + 1) * 1024] for k in range(2)]
        self.QTraw = sb("QTraw", [128, 8192], BF16)
        self.KTraw = sb("KTraw", [128, 8192], BF16)
        self.QT = self.QTraw[:, 0:4 * S_].rearrange("p (a s) -> p a s", a=4)
        self.KT = self.KTraw[:, 0:4 * S_].rearrange("p (a s) -> p a s", a=4)
        self.WO = self.QTraw[:, :].rearrange("p (a s) -> p a s", a=8)
        self.LNG = self.KTraw[:, 0:2048].bitcast(F32)
        self.LNB = self.KTraw[:, 2048:4096].bitcast(F32)
        self.IQ = sb("IQ", [128, 3, S_], BF16)
        if S_ >= 2048:
            iqf = self.IQ[:].rearrange("p a s -> p (a s)")
            self.XS = [iqf[:, k * 2048:(k + 1) * 2048].bitcast(F32) for k in range(2)]
            self.XSb = [iqf[:, 4096 + k * 1024:4096 + (k + 1) * 1024] for k in range(2)]
            self.xs_tok = (['IQs0', 'IQs1'], ['IQb0', 'IQb1'])
        else:
            self.XS = self.XSb = None
        self.IK3 = sb("IK3", [128, S_], BF16)
        self.VA = sb("VA", [128, NB, 4, 65], BF16)
        self.SG = sb("SG", [128, NB, 256], BF16)
        self.IW = sb("IW", [128, NB, 8], F32)
        mbsz = max(4 * S_, 7680)
        nscr = mbsz + S_ + 1536
        self.SCRb = sb("SCRb", [128, nscr], BF16)
        X = self.SCRb
        self.MB = X[:, 0:4 * S_].rearrange("p (a s) -> p a s", a=4)
        self.JK = X[:, mbsz:mbsz + S_]
        self.RL = [X[:, mbsz + S_ + k * 512:mbsz + S_ + (k + 1) * 512] for k in range(3)]
        self.CQT = X[:, 0:1024].rearrange("p (a s) -> p a s", a=2)
        self.CKVT = X[:, 1024:1536]
        self.SQT = X[:, 1536:3072].rearrange("p (a s) -> p a s", a=3)
        self.KRr = X[:, 3072:3584]
        self.RSQ = X[:, 3584:4608].bitcast(F32)
        self.RSK = X[:, 4608:5632].bitcast(F32)
        self.ROPE = X[:, 5632:7680].bitcast(F32).rearrange("p (a s) -> p a s", a=2)
        self.scr_tokens = [('MB', i) for i in range(4)] + ['CQT', 'CKVT', 'SQT', 'KRr', 'RSQ', 'RSK', 'ROPE']
        self.E = [sb("E%d" % k, [128, 512], F32) for k in range(3)]
        self.SPb = [sb("SPb%d" % k, [128, 512], BF16) for k in range(3)]
        self.MT = [self.E[0], self.E[1]]
        self.R32 = sb("R32", [128, 512], F32)
        self.Rbf = sb("Rbf", [128, 512], BF16)
        self.PT = [sb("PT%d" % k, [128, 512], BF16) for k in range(4)]
        self.SCraw = sb("SC", [128, max(S_, 2048)], F32)
        self.SC2 = sb("SC2", [128, S_], F32)
        self.SCs = [self.SCraw[:, 0:S_], self.SC2[:, :]]
        self.P2 = sb("P2", [128, 32], F32)
        self.DG = sb("DG", [128, 8, 128], BF16)
        self.OG = [sb("OG%d" % k, [128, 4, 256], BF16) for k in range(1)]
        self.EP = [sb("EP%d" % k, [128, 4, 64], F32) for k in range(4)]
        self.OGB = sb("OGB", [128, 1024], BF16)
        self.OGT = sb("OGT", [128, 8, 128], BF16)
        self.XR = self.XB[0]
        self.Z = self.XB[1]
        self.ZC = self.XBbraw[:, :].bitcast(F32)

    def fence(self, tokens):
        d = self.dummy
        self.S.op('pool', lambda: self.nc.gpsimd.memset(d[:, 0:1], 0.0), (), list(tokens))

    def setup_consts(self):
        cst_f = self.SCraw[:, 0:896].rearrange("p (a s) -> p a s", a=7)
        self.dma('sp', cst_f, self.d_consts, (), ['SC0'])
        self.cp('dve', self.cst_b[:], cst_f, ['SC0'], ['const'])
        self.cp('dve', self.CMf[:], cst_f[:, 3, :], ['SC0'], ['CMf'])
        self.cp('dve', self.IDf[:], cst_f[:, 0, :], ['SC0'], ['const'])
        relb_f = self.SCraw[:, 0:2048].rearrange("p (h a s) -> p h a s", h=8, a=2)
        self.dma('sp', relb_f, self.d_relb, ['SC0'], ['SC0'])
        self.dma('sp', self.c31[:], self.d_c31, (), ['c31'])
        for h in range(8):
            self.ts('dve', relb_f[:, h, :, :], relb_f[:, h, :, :], self.c31[:, h:h + 1], None,
                    ALU.subtract, None, ['SC0', 'c31'], ['SC0'])
            self.tt('dve', relb_f[:, h, 0, :], relb_f[:, h, 0, :], self.c_CMf, ALU.add,
                    ['SC0', 'CMf'], ['SC0'])
        self.cp('dve', self.BT[:], relb_f, ['SC0'], ['BT'])
        self.memset('pool', self.VA[:, :, :, 64:65], 1.0, ['VA'])
        for j in range(self.NIT):
            self.memset('pool', self.P2[:, j:j + 1], 2.0 ** -(j + 1), ['P2'])

    def load_piece(self, l, col0, ncols, tag):
        k = self.rr('wb', 2)
        self.dma('sp', self.WS[0][:, :, 0:ncols], self.d_win[l, :, :, col0:col0 + ncols], (), ['WS0'])
        self.cp('pool', self.WB[k][:, :, 0:ncols], self.WS[0][:, :, 0:ncols], ['WS0'], ['WB%d' % k])
        return self.WB[k], 'WB%d' % k

    def layer_setup(self, l):
        sm = self.sm
        lam_init = 0.8 - 0.6 * math.exp(-0.3 * l)
        self.lam_init = lam_init
        self.dma('sp', self.DL[:], self.d_dlam[l], (), ['DL'])
        self.dma('sp', self.G64[:], self.d_subln[l], (), ['G64'])
        self.dma('sp', self.QN[:], self.d_qn[l], (), ['QN'])
        self.dma('sp', self.KVN[:], self.d_kvn[l], (), ['KVN'])
        self.tt('dve', self.DL[:, 0:32], self.DL[:, 0:32], self.DL[:, 32:64], ALU.mult, ['DL'], ['DL'])
        self.tt('dve', self.DL[:, 64:96], self.DL[:, 64:96], self.DL[:, 96:128], ALU.mult, ['DL'], ['DL'])
        self.S.op('dve', lambda: self.nc.vector.tensor_reduce(out=sm[:, 0:1], in_=self.DL[:, 0:32], axis=AX.X, op=ALU.add),
                  ['DL'], ['sm0'])
        self.S.op('dve', lambda: self.nc.vector.tensor_reduce(out=sm[:, 1:2], in_=self.DL[:, 64:96], axis=AX.X, op=ALU.add),
                  ['DL'], ['sm1'])
        self.act(sm[:, 2:3], sm[:, 0:1], AF.Exp, ['sm0'], ['sm2'])
        self.act(sm[:, 3:4], sm[:, 1:2], AF.Exp, ['sm1'], ['sm3'])
        self.tt('dve', sm[:, 4:5], sm[:, 3:4], sm[:, 2:3], ALU.subtract, ['sm2', 'sm3'], ['sm4'])
        self.ts('dve', sm[:, 5:6], sm[:, 4:5], -lam_init, None, ALU.add, None, ['sm4'], ['neglam'])
        self.neglam = sm[:, 5:6]
        self.ts('dve', self.G64[:], self.G64[:], 1.0 - lam_init, None, ALU.mult, None, ['G64'], ['G64'])
        k = 0
        wsv = self.WS[k][:].rearrange("p a b -> p (a b)")
        self.dma('sp', wsv[:, 0:768], self.d_wuq[l].rearrange("p a b -> p (a b)"), (), ['WS%d' % k])
        for c in range(2):
            self.ts('dve', self.WUQb[:, c, :], wsv[:, c * 384:(c + 1) * 384], self.QN[:, c:c + 1], None,
                    ALU.mult, None, ['WS%d' % k, 'QN'], ['WUQb'])
        for c in range(2):
            for h in range(4):
                self.ts('dve', self.WROT[:, c, h, 0:16], self.WUQb[:, c, h * 96 + 80:h * 96 + 96], -1.0, None,
                        ALU.mult, None, ['WUQb'], ['WROT'])
                self.cp('dve', self.WROT[:, c, h, 16:32], self.WUQb[:, c, h * 96 + 64:h * 96 + 80], ['WUQb'], ['WROT'])
        k = 0
        wsv = self.WS[k][:].rearrange("p a b -> p (a b)")
        self.dma('sp', wsv[:, 0:512], self.d_wukv[l], (), ['WS%d' % k])
        self.ts('dve', self.WUKVb[:], wsv[:, 0:512], self.KVN[:, 0:1], None, ALU.mult, None,
                ['WS%d' % k, 'KVN'], ['WUKVb'])

    def xsrc(self, l, s):
        return self.d_x[s] if l == 0 else self.d_xmid[s]

    def build_xT(self, l, s):
        self.cur = (l, s)
        src = self.xsrc(l, s)
        b7 = self.bank[7][:].bitcast(BF16)
        if self.XS is not None:
            XB, XBb, (xt_, xbt_) = self.XS, self.XSb, self.xs_tok
            self.fence(['IQ'] + xt_ + xbt_)
        else:
            XB, XBb = [t[:] for t in self.XB], self.XBb
            xt_, xbt_ = ['XB0', 'XB1'], ['XBb0', 'XBb1']
        for tb in range(self.NB):
            k = self.rr('xb', 2)
            self.dma('sp', XB[k], src[tb * 128:(tb + 1) * 128, :], [('xm', s, tb)], [xt_[k]])
            self.cp('dve', XBb[k], XB[k], [xt_[k]], [xbt_[k]])
            for c in range(8):
                self.tr(b7[:, c * 128:(c + 1) * 128], XBb[k][:, c * 128:(c + 1) * 128], [xbt_[k]], ['b7'])
            self.evac(self.xT[:, :, tb * 128:(tb + 1) * 128], b7.rearrange("p (c t) -> p c t", c=8), ['b7'], ['xT'])
        if self.XS is not None:
            self.fence(['IQ'] + xt_ + xbt_)

    def proj_fm(self, wt, wtok, c0, M, dest_fn, extra_R=()):
        for tc in range(self.NJ):
            b = self.rr('pb', 4)
            ps = self.bank[b]
            for c in range(8):
                self.mm(ps[0:M, :], wt[:, c, c0:c0 + M], self.xT[:, c, tc * 512:(tc + 1) * 512], c == 0, c == 7,
                        [wtok, 'xT'] + list(extra_R), ['b%d' % b])
            dest_fn(tc, ps, 'b%d' % b)

    def proj_tm(self, wt, wtok, c0, N, dest_fn):
        for tb in range(self.NB):
            b = self.rr('pb', 4)
            ps = self.bank[b]
            for c in range(8):
                self.mm(ps[:, 0:N], self.xT[:, c, tb * 128:(tb + 1) * 128], wt[:, c, c0:c0 + N], c == 0, c == 7,
                        [wtok, 'xT'], ['b%d' % b])
            dest_fn(tb, ps, 'b%d' % b)

    def project_group(self, l, g):
        S_ = self.Sq
        gi = "ABCD".index(g)
        qname, kname, vname = {"A": ("a_q", "a_k", "a_v"), "B": ("b_q", "b_k", "b_v"),
                               "C": ("c_q", "c_k", "c_v"), "D": (None, None, None)}[g]
        wt, wtok = self.load_piece(l, OFF["gate"][0] + gi * 256, 256, 'gate')

        def gate_dest(tb, ps, btok):
            self.act(self.SG[:, tb, :], ps[:, 0:256], AF.Silu, [btok], ['SG'])
        self.proj_tm(wt, wtok, 0, 256, gate_dest)
        if g in "ABC":
            wt, wtok = self.load_piece(l, OFF[vname][0], 256, 'v')

            def v_dest(tb, ps, btok):
                self.evac(self.VA[:, tb, :, 0:64], ps[:, 0:256].rearrange("p (h d) -> p h d", h=4), [btok], ['VA'])
            self.proj_tm(wt, wtok, 0, 256, v_dest)
            if g == "B":
                qs = 32 ** -0.5
                wt, wtok = self.load_piece(l, OFF[qname][0], 256, qname)
                for h in range(4):
                    self.memset('dve', self.WX[:, :, 32:96], 0.0, ['WX'])
                    self.cp('dve', self.WX[:, :, 0:32], wt[:, :, h * 64:h * 64 + 32], [wtok], ['WX'])
                    self.cp('dve', self.WX[:, :, 96:128], wt[:, :, h * 64 + 32:h * 64 + 64], [wtok], ['WX'])

                    def qdest(tc, ps, btok, h=h):
                        self.evac(self.QT[:, h, tc * 512:(tc + 1) * 512], ps[:, :], [btok], ['QT'], scale=qs)
                    self.proj_fm(self.WX, 'WX', 0, 128, qdest)
                wt, wtok = self.load_piece(l, OFF[kname][0], 256, kname)
                for h in range(4):
                    self.cp('dve', self.WX[:, :, 0:64], wt[:, :, h * 64:h * 64 + 64], [wtok], ['WX'])
                    self.cp('dve', self.WX[:, :, 64:128], wt[:, :, h * 64:h * 64 + 64], [wtok], ['WX'])

                    def kdest(tc, ps, btok, h=h):
                        self.evac(self.KT[:, h, tc * 512:(tc + 1) * 512], ps[:, :], [btok], ['KT'])
                    self.proj_fm(self.WX, 'WX', 0, 128, kdest)
            else:
                qs = 64 ** -0.5
                for cg in range(2):
                    self.memset('dve', self.QT[64:128, 2 * cg, :], 0.0, ['QT'])
                    self.memset('dve', self.QT[0:64, 2 * cg + 1, :], 0.0, ['QT'])
                wt, wtok = self.load_piece(l, OFF[qname][0], 256, qname)
                for cg in range(2):
                    def qdest(tc, ps, btok, cg=cg):
                        tsl = slice(tc * 512, (tc + 1) * 512)
                        self.evac(self.QT[0:64, 2 * cg, tsl], ps[0:64, :], [btok], ['QT'], scale=qs)
                        self.evac(self.QT[64:128, 2 * cg + 1, tsl], ps[64:128, :], [btok], ['QT'], scale=qs)
                    self.proj_fm(wt, wtok, cg * 128, 128, qdest)
                wt, wtok = self.load_piece(l, OFF[kname][0], 256, kname)
                for cg in range(2):
                    def kdest(tc, ps, btok, cg=cg):
                        self.evac(self.KT[:, cg, tc * 512:(tc + 1) * 512], ps[:, :], [btok], ['KT'])
                    self.proj_fm(wt, wtok, cg * 128, 128, kdest)
        else:
            self.fence(self.scr_tokens)
            self.memset('dve', self.QT[96:128, :, :], 0.0, ['QT'])
            self.memset('dve', self.KT[96:128, :, :], 0.0, ['KT'])
            self.project_mla(l)
            self.fence(self.scr_tokens)

    def project_indexer(self, l):
            wt, wtok = self.load_piece(l, OFF["c_iq"][0], 256, 'iq')
            for cg in range(3):
                M = 96 if cg < 2 else 64

                def dest(tc, ps, btok, cg=cg, M=M):
                    self.evac(self.IQ[0:M, cg, tc * 512:(tc + 1) * 512], ps[0:M, :], [btok], ['IQ'], scale=32 ** -0.5)
                self.proj_fm(wt, wtok, cg * 96, M, dest)
            wt, wtok = self.load_piece(l, OFF["c_ik"][0], 40, 'ik')
            for r in range(3):
                self.cp('pool', self.WX[:, :, r * 32:(r + 1) * 32], wt[:, :, 0:32], [wtok], ['WX'])

            def ik_dest(tc, ps, btok):
                self.evac(self.IK3[0:96, tc * 512:(tc + 1) * 512], ps[0:96, :], [btok], ['IK3'])
            self.proj_fm(self.WX, 'WX', 0, 96, ik_dest)

            def iw_dest(tb, ps, btok):
                self.evac(self.IW[:, tb, :], ps[:, 0:8], [btok], ['IW'], scale=8 ** -0.5, eng='dve')
            self.proj_tm(wt, wtok, 32, 8, iw_dest)

    def project_mla(self, l):
        S_ = self.Sq
        sc_q = 96 ** -0.5
        wcq, tcq = self.load_piece(l, OFF["d_cq"][0], 256, 'cq')
        wkv, tkv = self.load_piece(l, OFF["d_ckv"][0], 160, 'ckv')
        self.ts('pool', self.WX[:, :, 0:16], wkv[:, :, 144:160], -1.0, None, ALU.mult, None, [tkv], ['WX'])
        self.cp('pool', self.WX[:, :, 16:32], wkv[:, :, 128:144], [tkv], ['WX'])
        bk = self.bank
        for tc in range(self.NJ):
            tsl = slice(tc * 512, (tc + 1) * 512)
            self.dma('sp', self.ROPE[:], self.d_rope[:, :, tsl].rearrange("a p t -> p a t"), (), ['ROPE'])
            Ct = self.ROPE[:, 0, :]
            St = self.ROPE[:, 1, :]
            for cg in range(3):
                b = self.rr('pb', 4)
                ps = bk[b]
                wt, wtok, c0 = (wcq, tcq, cg * 128) if cg < 2 else (wkv, tkv, 0)
                for c in range(8):
                    self.mm(ps[:, :], wt[:, c, c0:c0 + 128], self.xT[:, c, tsl], c == 0, c == 7, [wtok, 'xT'], ['b%d' % b])
                dst = self.CQT[:, cg, :] if cg < 2 else self.CKVT[:]
                self.cp('dve', dst, ps[:, :], ['b%d' % b], ['CQT' if cg < 2 else 'CKVT'])
                self.act(self.SQT[:, cg, :], ps[:, :], AF.Square, ['b%d' % b], ['SQT'])
            for which, ncg, dst, dtok, rank, scl in (('q', (0, 1), self.RSQ, 'RSQ', 256, sc_q), ('k', (2,), self.RSK, 'RSK', 128, 1.0)):
                b = self.rr('pb', 4)
                ps = bk[b]
                for j, cg in enumerate(ncg):
                    self.mm(ps[:, :], self.c_ones, self.SQT[:, cg, :], j == 0, j == len(ncg) - 1, ['const', 'SQT'], ['b%d' % b])
                self.act(dst[:], ps[:, :], AF.Ln, ['b%d' % b], [dtok], scale=1.0 / rank, bias=1e-6)
                self.act(dst[:], dst[:], AF.Exp, [dtok], [dtok], scale=-0.5)
                if scl != 1.0:
                    self.ts('dve', dst[:], dst[:], scl, None, ALU.mult, None, [dtok], [dtok])
            b = self.rr('pb', 4)
            b2 = self.rr('pb', 4)
            for c in range(8):
                self.mm(bk[b][64:96, :], wkv[:, c, 128:160], self.xT[:, c, tsl], c == 0, c == 7, [tkv, 'xT'], ['b%d' % b])
            for c in range(8):
                self.mm(bk[b2][64:96, :], self.WX[:, c, 0:32], self.xT[:, c, tsl], c == 0, c == 7, ['WX', 'xT'], ['b%d' % b2])
            m0, m1 = self.MT[0], self.MT[1]
            self.tt('dve', m0[64:96, :], bk[b][64:96, :], Ct[64:96, :], ALU.mult, ['b%d' % b, 'ROPE'], ['E0'])
            self.tt('dve', m1[64:96, :], bk[b2][64:96, :], St[64:96, :], ALU.mult, ['b%d' % b2, 'ROPE'], ['E1'])
            self.tt('pool', self.KRr[64:96, :], m0[64:96, :], m1[64:96, :], ALU.add, ['E0', 'E1'], ['KRr'])
            for h in range(4):
                b = self.rr('pb', 4)
                self.mm(bk[b][0:64, :], self.WUKVb[:, h * 128:h * 128 + 64], self.CKVT[:], True, True,
                        ['WUKVb', 'CKVT'], ['b%d' % b])
                self.tt('dve', self.KT[0:64, h, tsl], bk[b][0:64, :], self.RSK[0:64, :], ALU.mult,
                        ['b%d' % b, 'RSK'], ['KT'])
                self.cp('pool', self.KT[64:96, h, tsl], self.KRr[64:96, :], ['KRr'], ['KT'])
                b = self.rr('pb', 4)
                b2 = self.rr('pb', 4)
                for c in range(2):
                    self.mm(bk[b][0:96, :], self.WUQb[:, c, h * 96:(h + 1) * 96], self.CQT[:, c, :], c == 0, c == 1,
                            ['WUQb', 'CQT'], ['b%d' % b])
                for c in range(2):
                    self.mm(bk[b2][64:96, :], self.WROT[:, c, h, :], self.CQT[:, c, :], c == 0, c == 1,
                            ['WROT', 'CQT'], ['b%d' % b2])
                self.tt('dve', self.QT[0:64, h, tsl], bk[b][0:64, :], self.RSQ[0:64, :], ALU.mult,
                        ['b%d' % b, 'RSQ'], ['QT'])
                self.tt('dve', m0[64:96, :], bk[b][64:96, :], Ct[64:96, :], ALU.mult, ['b%d' % b, 'ROPE'], ['E0'])
                self.tt('dve', m1[64:96, :], bk[b2][64:96, :], St[64:96, :], ALU.mult, ['b%d' % b2, 'ROPE'], ['E1'])
                self.tt('pool', m0[64:96, :], m0[64:96, :], m1[64:96, :], ALU.add, ['E0', 'E1'], ['E0'])
                self.tt('pool', self.QT[64:96, h, tsl], m0[64:96, :], self.RSQ[64:96, :], ALU.mult, ['E0', 'RSQ'], ['QT'])
            for t4 in range(4):
                tb = tc * 4 + t4
                b = self.rr('pb', 4)
                ps = bk[b]
                for h in range(4):
                    self.mm(ps[:, h * 64:(h + 1) * 64], self.CKVT[:, t4 * 128:(t4 + 1) * 128],
                            self.WUKVb[:, h * 128 + 64:h * 128 + 128], True, True, ['CKVT', 'WUKVb'], ['b%d' % b])
                self.mm(ps[:, 256:257], self.SQT[:, 2, t4 * 128:(t4 + 1) * 128], self.c_ones[:, 0:1], True, True,
                        ['SQT', 'const'], ['b%d' % b])
                sm = self.sm
                self.act(sm[:, 8:9], ps[:, 256:257], AF.Ln, ['b%d' % b], ['sm8'], scale=1.0 / 128, bias=1e-6)
                self.act(sm[:, 9:10], sm[:, 8:9], AF.Exp, ['sm8'], ['sm9'], scale=-0.5)
                self.ts('dve', self.VA[:, tb, :, 0:64], ps[:, 0:256].rearrange("p (h d) -> p h d", h=4), sm[:, 9:10], None,
                        ALU.mult, None, ['b%d' % b, 'sm9'], ['VA'])

    def attend_group(self, l, g):
        self.sbanks = [0, 1, 2, 5, 6, 7] if g == "D" else [0, 1, 2, 5]
        for J in range(self.NJ):
            k = 0
            OG, ogtok = self.OG[k], 'OG%d' % k
            if g == "C":
                for il in range(4):
                    i = 4 * J + il
                    if self.need_sel(i):
                        Nk = 128 * (i + 1)
                        self.dma('sp', self.MB[:, il, 0:Nk], self.d_mb[i, :, 0:Nk], [('mbd', i)], [('MB', il)])
            for h in range(4):
                if g == "A":
                    self.attn_A(J, h, OG, ogtok)
                elif g == "B":
                    self.attn_B(J, h, OG, ogtok)
                elif g == "C":
                    self.attn_CD(J, h, OG, ogtok, True)
                else:
                    self.attn_CD(J, h, OG, ogtok, False)
            gi = "ABCD".index(g)
            dst = self.d_ogd[J * 512:(J + 1) * 512, gi * 256:(gi + 1) * 256].rearrange("(i p) c -> p i c", p=128)
            self.dma('pool', dst, OG[:], [ogtok], [('ogd', J * 4 + i) for i in range(4)])
            if g == "B" and "C" in self.groups and J + 1 < self.NJ:
                self.indexer(J + 1)

    def zero_group(self, g):
        gi = "ABCD".index(g)
        for J in range(self.NJ):
            k = 0
            OG, ogtok = self.OG[k], 'OG%d' % k
            self.memset('pool', OG[:], 0.0, [ogtok])
            dst = self.d_ogd[J * 512:(J + 1) * 512, gi * 256:(gi + 1) * 256].rearrange("(i p) c -> p i c", p=128)
            self.dma('pool', dst, OG[:], [ogtok], [('ogd', J * 4 + i) for i in range(4)])

    def sbank(self):
        b = self.sbanks[self.rr('sb', len(self.sbanks))]
        return self.bank[b], 'b%d' % b

    def pv(self, acc, acctok, PT, pttok, a, h, c0, first):
        self.mm(acc[0:65, c0:512], self.VA[:, a, h, :], PT[:, c0:512], bool(first), True, [pttok, 'VA'], [acctok])

    def acc_finish(self, acc, acctok):
        k = 0
        ot, ottok = self.OTs[k], 'OTs%d' % k
        self.evac(ot[:, :], acc[0:65, :], [acctok], [ottok])
        accT, acctT = self.sbank()
        nc = self.nc
        for il in range(4):
            o_ = accT[:, il * 65:(il + 1) * 65]
            i_ = ot[:, il * 128:(il + 1) * 128]
            idf = self.IDf[0:65, 0:65]
            self.S.op('pe', lambda o_=o_, i_=i_, idf=idf: nc.tensor.transpose(o_, i_, idf), [ottok, 'const'], [acctT], cost=110.0)
        return accT, acctT

    def attn_A(self, J, h, OG, ogtok):
        hb = (h % 2) * 64
        hs = h // 2
        ab = 3 + self.rr('acc', 2)
        acc, acctok = self.bank[ab], 'b%d' % ab
        self.memset('dve', acc[0:65, :], 0.0, [acctok])
        self.memset('pool', self.R32[:], 0.0, ['R32'])
        amax = 4 * J + 3
        for a in range(amax, -1, -1):
            m = a - 4 * J
            c0 = 128 * max(0, m)
            qsl = slice(J * 512 + c0, (J + 1) * 512)
            ksl = slice(a * 128, (a + 1) * 128)
            K = self.KT[:, hs, ksl]
            Q = self.QT[:, h, qsl]
            ps, pstok = self.sbank()
            self.mm(ps[:, c0:512], K, Q, True, True, ['KT', 'QT'], [pstok])
            e = self.rr('E', 3)
            E, etok = self.E[e], 'E%d' % e
            SP, sptok = self.SPb[e], 'SP%d' % e
            self.act(E[:, c0:512], ps[:, c0:512], AF.Exp, [pstok], [etok])
            self.act(SP[:, c0:512], E[:, c0:512], AF.Ln, [etok], [sptok], bias=1.0)
            if m >= 0:
                self.tt('pool', SP[:, c0:c0 + 128], SP[:, c0:c0 + 128], self.c_strictT, ALU.mult, [sptok, 'const'], [sptok])
            ps2, ps2tok = ps, pstok
            first = (a == amax)
            if m >= 0:
                self.mm(ps2[:, c0:c0 + 128], self.c_MA, self.c_ident, False, False, ['const', etok], [ps2tok])
            if not first:
                self.mm(ps2[:, c0:512], self.c_negones, self.Rbf[:, c0:512], False, False, ['const', 'Rbf', etok], [ps2tok])
            self.mm(ps2[:, c0:512], self.c_negtri, SP[:, c0:512], False, True, ['const', sptok], [ps2tok])
            p = self.rr('PT', 4)
            PT, pttok = self.PT[p], 'PT%d' % p
            self.act(PT[:, c0:512], ps2[:, c0:512], AF.Exp, [ps2tok], [pttok])
            self.pv(acc, acctok, PT, pttok, a, h, c0, False)
            if a > 0:
                self.tt('pool', self.R32[:, c0:512], self.R32[:, c0:512], SP[:, c0:512], ALU.add, ['R32', sptok], ['R32'])
                cn = 128 * max(0, m - 1)
                self.cp('pool', self.Rbf[:, cn:512], self.R32[:, cn:512], ['R32'], ['Rbf'])
        accT, acctT = self.acc_finish(acc, acctok)
        accv = accT[:, 0:260].rearrange("p (i d) -> p i d", d=65)
        self.tt('dve', OG[:, :, h * 64:(h + 1) * 64], accv[:, :, 0:64], self.SG[:, 4 * J:4 * J + 4, h * 64:(h + 1) * 64],
                ALU.mult, [acctT, 'SG'], [ogtok])

    def near_bias(self, ps, pstok, a, J, c0, bh):
        for il in range(c0 // 128, 4):
            i = 4 * J + il
            if i == a:
                self.mm(ps[:, il * 128:(il + 1) * 128], self.BT[:, bh, 0, :], self.c_ident, False, False, ['BT', 'const'], [pstok])
            elif i == a + 1:
                self.mm(ps[:, il * 128:(il + 1) * 128], self.BT[:, bh, 1, :], self.c_ident, False, False, ['BT', 'const'], [pstok])

    def softmax_norm(self, acc, acctok, dst, dsttok, ri):
        acc, acctok = self.acc_finish(acc, acctok)
        accv = acc[:, 0:260].rearrange("p (i d) -> p i d", d=65)
        rc = self.sm[:, 16 + 4 * ri:20 + 4 * ri]
        rtok = 'rc%d' % ri
        self.S.op('dve', lambda: self.nc.vector.reciprocal(out=rc, in_=accv[:, :, 64]), [acctok], [rtok])
        self.tt('dve', dst[:], accv[:, :, 0:64], rc.unsqueeze(2).broadcast_to([128, 4, 64]), ALU.mult,
                [acctok, rtok], [dsttok])

    def attn_B(self, J, h, OG, ogtok):
        accs = []
        for c in range(2):
            acc, acctok = self.bank[3 + c], 'b%d' % (3 + c)
            accs.append((acc, acctok))
        for a in range(4 * J + 4):
            m = a - 4 * J
            c0 = 128 * max(0, m)
            qsl = slice(J * 512 + c0, (J + 1) * 512)
            ksl = slice(a * 128, (a + 1) * 128)
            pss = [self.sbank() for c in range(2)]
            nc = self.nc
            o0, o1 = pss[0][0][:, c0:512], pss[1][0][:, c0:512]
            k0, k1 = self.KT[0:64, h, ksl], self.KT[64:128, h, ksl]
            q0, q1 = self.QT[0:64, h, qsl], self.QT[64:128, h, qsl]

            def qk2(o0=o0, o1=o1, k0=k0, k1=k1, q0=q0, q1=q1):
                nc.tensor.matmul(o0, lhsT=k0, rhs=q0, start=True, stop=False, skip_group_check=True)
                return nc.tensor.matmul(o1, lhsT=k1, rhs=q1, start=True, stop=False, skip_group_check=True)
            self.S.op('pe', qk2, ['KT', 'QT'], [pss[0][1], pss[1][1]], cost=(512 - c0 + 110) / 2.2)
            for c in range(2):
                ps, pstok = pss[c]
                self.near_bias(ps, pstok, a, J, c0, h)
            for c in range(2):
                acc, acctok = accs[c]
                ps, pstok = pss[c]
                p = self.rr('PT', 4)
                PT, pttok = self.PT[p], 'PT%d' % p
                self.act(PT[:, c0:512], ps[:, c0:512], AF.Exp, [pstok, 'c31'], [pttok], bias=self.c31[:, h:h + 1])
                self.pv(acc, acctok, PT, pttok, a, h, c0, a == 0)
        T0, T1, T2 = self.EP[0], self.EP[1], self.EP[2]
        self.softmax_norm(accs[0][0], accs[0][1], T0, 'EP0', 0)
        self.softmax_norm(accs[1][0], accs[1][1], T1, 'EP1', 1)
        f = lambda t: t[:].rearrange("p i d -> p (i d)")
        self.stt(f(T2), f(T1), self.neglam, f(T0), ALU.mult, ALU.add, ['EP0', 'EP1', 'neglam'], ['EP2'])
        self.tt('pool', T0[:], T2[:], T2[:], ALU.mult, ['EP2'], ['EP0'])
        ss = self.sm[:, 32:36]
        self.S.op('dve', lambda: self.nc.vector.tensor_reduce(out=ss, in_=T0[:], axis=AX.X, op=ALU.add), ['EP0'], ['ss'])
        self.act(ss, ss, AF.Ln, ['ss'], ['ss'], scale=1.0 / 64, bias=1e-6)
        self.act(ss, ss, AF.Exp, ['ss'], ['ss'], scale=-0.5)
        self.tt('dve', T1[:], T2[:], ss.unsqueeze(2).broadcast_to([128, 4, 64]), ALU.mult, ['EP2', 'ss'], ['EP1'])
        self.tt('pool', T0[:], T1[:], self.G64[:].unsqueeze(1).broadcast_to([128, 4, 64]), ALU.mult, ['EP1', 'G64'], ['EP0'])
        self.tt('pool', OG[:, :, h * 64:(h + 1) * 64], T0[:], self.SG[:, 4 * J:4 * J + 4, h * 64:(h + 1) * 64], ALU.mult,
                ['EP0', 'SG'], [ogtok])

    def need_sel(self, i):
        return 128 * (i + 1) > self.NSEL

    def attn_CD(self, J, h, OG, ogtok, isC):
        ab = 3 + self.rr('acc', 2)
        acc, acctok = self.bank[ab], 'b%d' % ab
        if isC:
            hs = h // 2
        else:
            hs = h
        for a in range(4 * J + 4):
            m = a - 4 * J
            c0 = 128 * max(0, m)
            qsl = slice(J * 512 + c0, (J + 1) * 512)
            ksl = slice(a * 128, (a + 1) * 128)
            ps, pstok = self.sbank()
            self.mm(ps[:, c0:512], self.KT[:, hs, ksl], self.QT[:, h, qsl], True, False, ['KT', 'QT'], [pstok])
            if isC:
                self.near_bias(ps, pstok, a, J, c0, 4 + h)
                for il in range(c0 // 128, 4):
                    i = 4 * J + il
                    if self.need_sel(i):
                        self.mm(ps[:, il * 128:(il + 1) * 128], self.MB[:, il, ksl], self.c_ident, False, False,
                                [('MB', il), 'const'], [pstok])
            elif m >= 0:
                self.mm(ps[:, c0:c0 + 128], self.c_CM, self.c_ident, False, False, ['const'], [pstok])
            p = self.rr('PT', 4)
            PT, pttok = self.PT[p], 'PT%d' % p
            if isC:
                self.act(PT[:, c0:512], ps[:, c0:512], AF.Exp, [pstok, 'c31'], [pttok], bias=self.c31[:, 4 + h:5 + h])
            else:
                self.act(PT[:, c0:512], ps[:, c0:512], AF.Exp, [pstok], [pttok])
            self.pv(acc, acctok, PT, pttok, a, h, c0, a == 0)
        e = self.rr('EPc', 2)
        T0, ttok = self.EP[e], 'EP%d' % e
        self.softmax_norm(acc, acctok, T0, ttok, 2 + e)
        self.tt('pool', OG[:, :, h * 64:(h + 1) * 64], T0[:], self.SG[:, 4 * J:4 * J + 4, h * 64:(h + 1) * 64], ALU.mult,
                [ttok, 'SG'], [ogtok])

    def indexer_all(self):
        for J in range(self.NJ):
            self.indexer(J)

    def indexer(self, J):
        sm = self.sm
        nc = self.nc
        for il in range(4):
            i = 4 * J + il
            if not self.need_sel(i):
                continue
            Nk = 128 * (i + 1)
            isl = slice(i * 128, (i + 1) * 128)
            kq = self.rr('SCq', 2)
            SC, sctk = self.SCs[kq], 'SC%d' % kq
            for hh in range(8):
                self.ts('pool', self.DG[:, hh, :], self.c_ident, self.IW[:, i, hh:hh + 1], 0.0, ALU.mult, ALU.add,
                        ['const', 'IW'], ['DG'])
            for kc in range((Nk + 511) // 512):
                n = min(512, Nk - kc * 512)
                ksl = slice(kc * 512, kc * 512 + n)
                sc, sctok = self.bank[7], 'b7'
                for hh in range(8):
                    cg, rb = hh // 3, (hh % 3) * 32
                    d = 6
                    dps, dtok = self.bank[d], 'b%d' % d
                    self.mm(dps[:, 0:n], self.IQ[rb:rb + 32, cg, isl], self.IK3[rb:rb + 32, ksl], True, True, ['IQ', 'IK3'], [dtok])
                    r = self.rr('RL', 3)
                    RL, rtok = self.RL[r], 'RL%d' % r
                    if True:
                        self.act(RL[:, 0:n], dps[:, 0:n], AF.Relu, [dtok], [rtok])
                    else:
                        self.ts('dve', RL[:, 0:n], dps[:, 0:n], 0.0, None, ALU.max, None, [dtok], [rtok])
                    self.mm(sc[:, 0:n], self.DG[:, hh, :], RL[:, 0:n], hh == 0, hh == 7, ['DG', rtok], [sctok])
                self.cp('act', SC[:, ksl], sc[:, 0:n], [sctok], [sctk])
            SCv = SC[:, 0:Nk]
            AM, LO, W0, T, CNT, V = (sm[:, 36:37], sm[:, 37:38], sm[:, 38:39], sm[:, 39:40], sm[:, 40:41], sm[:, 41:42])
            WJ = sm[:, 44:44 + self.NIT]
            self.S.op('dve', lambda SCv=SCv, AM=AM: nc.vector.tensor_reduce(out=AM, in_=SCv, axis=AX.X, op=ALU.max,
                                                                         apply_absolute_value=True), [sctk], ['AM'])
            self.tt('pool', SC[:, Nk - 128:Nk], SC[:, Nk - 128:Nk], self.c_CMf, ALU.add, [sctk, 'CMf'], [sctk])
            self.ts('dve', W0, AM, 2.002, None, ALU.mult, None, ['AM'], ['W0'])
            self.ts('dve', WJ, self.P2[:, 0:self.NIT], W0, None, ALU.mult, None, ['P2', 'W0'], ['WJ'])
            self.memset('dve', T, 0.0, ['T'])
            for j in range(self.NIT):
                self.ts('dve', self.JK[:, 0:Nk], SCv, T, 0.0, ALU.is_ge, ALU.add, [sctk, 'T'], ['JK', 'CNT'], accum=CNT)
                self.ts('dve', V, CNT, float(self.NSEL), 0.5, ALU.is_ge, ALU.subtract, ['CNT'], ['V'])
                self.stt(T, V, WJ[:, j:j + 1], T, ALU.mult, ALU.add, ['V', 'WJ', 'T'], ['T'])
            self.stt(LO, WJ[:, self.NIT - 1:self.NIT], -0.5, T, ALU.mult, ALU.add, ['WJ', 'T'], ['LO'])
            self.ts('dve', self.JK[:, 0:Nk], SCv, LO, NEG, ALU.is_lt, ALU.mult, [sctk, 'LO'], ['JK'])
            self.dma('pool', self.d_mb[i, :, 0:Nk], self.JK[:, 0:Nk], ['JK'], [('mbd', i)])

    def output_phase(self, l, s):
        nc = self.nc
        sm = self.sm
        src = self.xsrc(l, s)
        dst = self.d_out[s] if l == self.L - 1 else self.d_xmid[s]
        b7 = self.bank[7][:].bitcast(BF16)
        for pz in range(4):
            self.dma('sp', self.WS[0][:], self.d_wout[l, :, :, pz * 256:(pz + 1) * 256], (), ['WS0'])
            self.cp('pool', self.WO[:, :, pz * 256:(pz + 1) * 256], self.WS[0][:], ['WS0'], ['QT'])
        self.dma('sp', self.LNG, self.d_lnp[l, 0], (), ['KT'])
        self.dma('sp', self.LNB, self.d_lnp[l, 1], ['KT'], ['KT'])
        sets = [dict(XR=self.XR[:], XRt='XB0', Z=self.Z[:], Zt='XB1', ZC=self.ZC, ZCt=['XBb0', 'XBb1'],
                     OGB=self.OGB[:], OGBt='OGB', OGT=self.OGT[:], OGTt='OGT')]
        if self.Sq >= 2048:
            sets.append(
                dict(XR=self.SCraw[:, 0:1024], XRt='SC0', Z=self.SCraw[:, 1024:2048], Zt='SC0',
                     ZC=self.SC2[:, 0:1024], ZCt=['SC1'],
                     OGB=self.SC2[:, 1024:1536].bitcast(BF16), OGBt='SC1',
                     OGT=self.SC2[:, 1536:2048].bitcast(BF16).rearrange("p (c t) -> p c t", c=8), OGTt='SC1'))
        for tb in range(self.NB):
            tsl = slice(tb * 128, (tb + 1) * 128)
            B_ = sets[tb % len(sets)]
            XR, Z, ZC, OGB, OGT = B_['XR'], B_['Z'], B_['ZC'], B_['OGB'], B_['OGT']
            XRt, Zt, ZCt, OGBt, OGTt = B_['XRt'], B_['Zt'], B_['ZCt'], B_['OGBt'], B_['OGTt']
            b7i = 7 if tb % 2 == 0 else 4
            b7 = self.bank[b7i][:].bitcast(BF16)
            b7t = 'b%d' % b7i
            self.dma('sp', OGB, self.d_ogd[tsl, :], [('ogd', tb)], [OGBt])
            self.dma('sp', XR, src[tsl, :], [('xm', s, tb)], [XRt])
            for c in range(8):
                self.tr(b7[:, c * 128:(c + 1) * 128], OGB[:, c * 128:(c + 1) * 128], [OGBt], [b7t])
            self.cp('act', OGT, b7.rearrange("p (c t) -> p c t", c=8), [b7t], [OGTt])
            sa = 64 + 8 * (tb % 2)
            for half in range(2):
                bi = (5 + half) if tb % 2 == 0 else (2 + half)
                ps, pstok = self.bank[bi], 'b%d' % bi
                for c in range(8):
                    self.mm(ps[:, :], OGT[:, c, :], self.WO[:, c, half * 512:(half + 1) * 512], c == 0, c == 7,
                            [OGTt, 'QT'], [pstok])
                hs = slice(half * 512, (half + 1) * 512)
                self.stt(Z[:, hs], XR[:, hs], ALPHA, ps[:, :], ALU.mult, ALU.add, [XRt, pstok], [Zt, ('su', tb % 2, half)],
                         accum=sm[:, sa + half:sa + half + 1])
            nm, ssq, lv, rstd = sm[:, sa + 2:sa + 3], sm[:, sa + 3:sa + 4], sm[:, sa + 4:sa + 5], sm[:, sa + 5:sa + 6]
            msq, bt = sm[:, sa + 6:sa + 7], sm[:, sa + 7:sa + 8]
            u = tb % 2
            self.tt('dve', nm, sm[:, sa:sa + 1], sm[:, sa + 1:sa + 2], ALU.add, [('su', u, 0), ('su', u, 1)], [('nm', u)])
            self.ts('dve', nm, nm, -1.0 / D_MODEL, None, ALU.mult, None, [('nm', u)], [('nm', u)])
            self.act(ZC, Z, AF.Square, [Zt], ZCt + [('ssq', u)], accum=ssq)
            self.tt('dve', msq, nm, nm, ALU.mult, [('nm', u)], [('msq', u)])
            self.ts('dve', bt, msq, -1.0, NORM_EPS, ALU.mult, ALU.add, [('msq', u)], [('bt', u)])
            self.act(lv, ssq, AF.Ln, [('ssq', u), ('bt', u)], [('lv', u)], scale=1.0 / D_MODEL, bias=bt)
            self.act(rstd, lv, AF.Exp, [('lv', u)], [('rstd', u)], scale=-0.5)
            self.stt(ZC, Z, nm, self.LNG, ALU.add, ALU.mult, [Zt, ('nm', u), 'KT'] + ZCt, ZCt)
            self.stt(Z, ZC, rstd, self.LNB, ALU.mult, ALU.add, ZCt + [('rstd', u), 'KT', Zt], [Zt])
            W = [('xm', s, tb)] if l < self.L - 1 else [('out', s, tb)]
            self.dma('pool', dst[tsl, :], Z, [Zt], W)


def t5_bucket_np(dist):
    max_exact = 16
    n = np.maximum(dist, 0)
    nf = np.maximum(n, max_exact).astype(np.float32)
    large = max_exact + (np.log(nf / np.float32(max_exact)) / np.float32(math.log(128 / max_exact))
                         * np.float32(32 - max_exact)).astype(np.int32)
    large = np.minimum(large, 31)
    return np.where(n < max_exact, n, large)


def host_consts(S):
    q = np.arange(128)[:, None]
    k = np.arange(128)[None, :]
    c = np.zeros((128, 7, 128), np.float32)
    c[:, 0] = np.eye(128, dtype=np.float32)
    c[:, 1] = np.where(q >= k, -1.0, 0.0)
    c[:, 2] = -1.0
    c[:, 3] = np.where(k <= q, 0.0, NEG)
    c[:, 4] = np.where(k < q, 0.0, NEG)
    c[:, 5] = np.where(q < k, 1.0, 0.0)
    c[:, 6] = 1.0
    pos = np.arange(S, dtype=np.float32)
    inv_freq = (np.float32(10000.0) ** (-np.arange(16, dtype=np.float32) / np.float32(16))).astype(np.float32)
    ang = pos[:, None] * inv_freq[None, :]
    rope = np.zeros((2, 128, S), np.float32)
    rope[0, 64:80] = np.cos(ang).T
    rope[0, 80:96] = np.cos(ang).T
    rope[1, 64:80] = np.sin(ang).T
    rope[1, 80:96] = np.sin(ang).T
    return c, rope


def host_layout(inp, S, L):
    f = lambda a: np.ascontiguousarray(np.asarray(a, dtype=np.float32))
    w_in = f(inp["w_in"])[:L]
    w_out = f(inp["w_out"])[:L]
    d = {}
    d["w_in_l"] = f(w_in.reshape(L, 8, 128, D_IN).transpose(0, 2, 1, 3))
    d["w_out_l"] = f(w_out.reshape(L, 8, 128, D_MODEL).transpose(0, 2, 1, 3))
    lnp = np.stack([f(inp["ln_g"])[:L], f(inp["ln_b"])[:L]], axis=1)
    d["lnp"] = f(np.broadcast_to(lnp[:, :, None, :], (L, 2, 128, D_MODEL)))
    rb = f(inp["rel_bias"])
    q = np.arange(128)[:, None]
    k = np.arange(128)[None, :]
    bd = t5_bucket_np(np.maximum(q - k, 0))
    bo = t5_bucket_np(128 + q - k)
    relb = np.zeros((128, 8, 2, 128), np.float32)
    relb[:, :, 0, :] = rb[bd].transpose(0, 2, 1)
    relb[:, :, 1, :] = rb[bo].transpose(0, 2, 1)
    d["relb"] = relb
    d["c31"] = f(np.broadcast_to(rb[31][None, :], (128, 8)))
    d["dlam"] = f(np.broadcast_to(f(inp["diff_lambda"])[:L].reshape(L, 1, 128), (L, 128, 128)))
    d["subln"] = f(np.broadcast_to(f(inp["diff_subln"])[:L][:, None, :], (L, 128, 64)))
    d["qn"] = f(f(inp["mla_q_norm"])[:L].reshape(L, 2, 128).transpose(0, 2, 1))
    d["kvn"] = f(f(inp["mla_kv_norm"])[:L].reshape(L, 128, 1))
    d["wuq"] = f(f(inp["mla_w_uq"])[:L].reshape(L, 2, 128, 384).transpose(0, 2, 1, 3))
    d["wukv"] = f(inp["mla_w_ukv"])[:L]
    c, rope = host_consts(S)
    d["consts"] = c
    d["rope"] = rope
    return d


_CACHE = {}


def run(inp, S, NSEQ, L, n_cores, NIT=13, groups="ABCD", core0=0):
    key = (S, NSEQ, L, NIT, groups)
    if key not in _CACHE:
        b = Builder(S=S, NSEQ=NSEQ, L=L, NIT=NIT, groups=groups)
        nc = b.build()
        print("built: ops/waits/sems/cnt/ndma", b.S.stats, flush=True)
        _CACHE[key] = nc
    nc = _CACHE[key]
    shared = host_layout(inp, S, L)
    x = np.ascontiguousarray(np.asarray(inp["x"], dtype=np.float32))
    in_maps = []
    for c in range(n_cores):
        m = dict(shared)
        m["x"] = np.ascontiguousarray(x[c * NSEQ:(c + 1) * NSEQ])
        in_maps.append(m)
    res = run_bass_kernel_spmd(nc, in_maps, core_ids=list(range(core0, core0 + n_cores)))
    return np.concatenate([np.asarray(r["out"]) for r in res.results], axis=0).astype(np.float32)


def kernel(x, w_in, w_out, ln_g, ln_b, rel_bias, diff_lambda, diff_subln,
           mla_q_norm, mla_kv_norm, mla_w_uq, mla_w_ukv):
    inp = dict(x=x, w_in=w_in, w_out=w_out, ln_g=ln_g, ln_b=ln_b, rel_bias=rel_bias, diff_lambda=diff_lambda,
               diff_subln=diff_subln, mla_q_norm=mla_q_norm, mla_kv_norm=mla_kv_norm, mla_w_uq=mla_w_uq,
               mla_w_ukv=mla_w_ukv)
    return run(inp, 2048, 2, DEPTH, 8)
```

```python
import math
from contextlib import ExitStack

import numpy as np
import concourse.bass as bass
import concourse.mybir as mybir
from concourse.bass_utils import run_bass_kernel_spmd

F32 = mybir.dt.float32
BF16 = mybir.dt.bfloat16
AF = mybir.ActivationFunctionType
ALU = mybir.AluOpType
AX = mybir.AxisListType

D_MODEL = 1024
DEPTH = 2
ALPHA = (2.0 * DEPTH) ** 0.25
NORM_EPS = 1e-5
NEG = -30000.0
D_IN = 4040

OFF = {}
_o = 0
for _n, _w in (("a_q", 256), ("a_k", 256), ("a_v", 256), ("b_q", 256), ("b_k", 256), ("b_v", 256),
               ("c_q", 256), ("c_k", 256), ("c_v", 256), ("c_iq", 256), ("c_ik", 32), ("c_iw", 8),
               ("d_cq", 256), ("d_ckv", 128), ("d_kr", 32), ("gate", 1024)):
    OFF[_n] = (_o, _w)
    _o += _w


class Sched:
    CH = 20000
    K_DMA = 8

    def __init__(self, nc, es):
        self.nc = nc
        self.es = es
        self.ops = []
        self.eng = {'pe': nc.tensor, 'act': nc.scalar, 'dve': nc.vector, 'pool': nc.gpsimd, 'sp': nc.sync}
        self.reorder = True

    def op(self, eng, fn, reads=(), writes=(), dma=False, cost=None):
        isb = lambda t: isinstance(t, str) and len(t) == 2 and t[0] == 'b' and t[1].isdigit()
        br = [t for t in reads if isb(t)]
        if br:
            reads = [t for t in reads if not isb(t)]
            writes = list(writes) + [t for t in br if t not in writes]
        if cost is None:
            cost = 2500.0 if dma else 300.0
        self.ops.append(dict(eng=eng, fn=fn, reads=tuple(reads), writes=tuple(writes), dma=dma,
                             signal=False, deps=(), cost=float(cost)))

    def schedule(self):
        import heapq
        ops = self.ops
        n = len(ops)
        succ = [[] for _ in range(n)]
        indeg = [0] * n
        for i, o in enumerate(ops):
            indeg[i] = len(o['alldeps'])
            for p in o['alldeps']:
                succ[p].append(i)
        fin = [0.0] * n
        ready_t = [0.0] * n
        engs = list(self.eng)
        heaps = {e: [] for e in engs}
        now = {e: [] for e in engs}
        free = {e: 0.0 for e in engs}
        for i, o in enumerate(ops):
            if indeg[i] == 0:
                heapq.heappush(heaps[o['eng']], (0.0, i))
        order = []
        while len(order) < n:
            best = None
            for e in engs:
                h, nw = heaps[e], now[e]
                while h and h[0][0] <= free[e]:
                    heapq.heappush(nw, heapq.heappop(h)[1])
                if nw:
                    cand = (free[e], nw[0], e, True)
                elif h:
                    cand = (h[0][0], h[0][1], e, False)
                else:
                    continue
                if best is None or cand[:2] < best[:2]:
                    best = cand
            st, i, e, from_now = best
            if from_now:
                heapq.heappop(now[e])
            else:
                heapq.heappop(heaps[e])
            o = ops[i]
            if o['dma']:
                free[e] = st + 150.0
                fin[i] = st + o['cost']
            else:
                free[e] = st + o['cost']
                fin[i] = st + o['cost'] + 60.0
            order.append(i)
            for j in succ[i]:
                indeg[j] -= 1
                if fin[i] > ready_t[j]:
                    ready_t[j] = fin[i]
                if indeg[j] == 0:
                    heapq.heappush(heaps[ops[j]['eng']], (ready_t[j], j))
        self.est_ns = max(fin) if fin else 0.0
        self.fin = fin
        self.eng_busy = {e: sum(o['cost'] if not o['dma'] else 150.0 for o in ops if o['eng'] == e) / 1e6 for e in engs}
        return order

    def finalize(self):
        nc = self.nc
        ops = self.ops
        last_w = {}
        readers = {}
        for i, o in enumerate(ops):
            deps = set()
            for r in o['reads']:
                if r in last_w:
                    deps.add(last_w[r])
            for w in o['writes']:
                if w in last_w:
                    deps.add(last_w[w])
                rd = readers.get(w)
                if rd:
                    deps.update(rd)
            deps.discard(i)
            o['alldeps'] = sorted(deps)
            for w in o['writes']:
                last_w[w] = i
                readers[w] = []
            for r in o['reads']:
                if r not in o['writes']:
                    readers.setdefault(r, []).append(i)
        order = self.schedule() if self.reorder else list(range(len(ops)))
        pos = [0] * len(ops)
        for k, i in enumerate(order):
            pos[i] = k
        for i in order:
            o = ops[i]
            latest = {}
            keep = []
            for p in o['alldeps']:
                po = ops[p]
                if po['dma']:
                    keep.append(p)
                    continue
                if po['eng'] == 'pe' and o['eng'] == 'pe' and not o['dma']:
                    continue
                e = po['eng']
                if e not in latest or pos[p] > pos[latest[e]]:
                    latest[e] = p
            keep.extend(latest.values())
            for p in keep:
                ops[p]['signal'] = True
            o['deps'] = keep
        sems = {}

        def getsem(key):
            if key not in sems:
                sems[key] = self.es.enter_context(nc.semaphore("s_%s_%s" % key))
            return sems[key]
        cnt = {e: 0 for e in self.eng}
        ndma = {e: 0 for e in self.eng}
        dma_sig = {e: [] for e in self.eng}
        waited = {e: {} for e in self.eng}
        self.nwaits = 0

        def do_wait(E, sig):
            sem, val = sig
            k = id(sem)
            if waited[E].get(k, 0) < val:
                self.eng[E].wait_ge(sem, val)
                waited[E][k] = val
                self.nwaits += 1
        for i in order:
            o = ops[i]
            E = o['eng']
            for p in o['deps']:
                do_wait(E, ops[p]['sig'])
            if o['dma']:
                n = ndma[E]
                if n >= self.K_DMA:
                    do_wait(E, dma_sig[E][n - self.K_DMA])
                sem = getsem((E + 'd', n % self.K_DMA))
                val = 16 * (n // self.K_DMA + 1)
                ins = o['fn']()
                ins.then_inc(sem, 16)
                o['sig'] = (sem, val)
                dma_sig[E].append(o['sig'])
                ndma[E] = n + 1
            else:
                ins = o['fn']()
                if o['signal']:
                    c = cnt[E]
                    sem = getsem((E, c // self.CH))
                    ins.then_inc(sem, 1)
                    o['sig'] = (sem, c % self.CH + 1)
                    cnt[E] = c + 1
            o['fn'] = None
        for E in self.eng:
            for s in dma_sig[E][-self.K_DMA:]:
                do_wait('sp', s)
        self.nc.sync.nop()
        self.stats = (len(ops), self.nwaits, len(sems), dict(cnt), dict(ndma), getattr(self, 'est_ns', 0.0) / 1e6, getattr(self, 'eng_busy', None))


class Builder:
    def __init__(self, S=2048, NSEQ=2, L=2, NIT=13, groups="ABCD"):
        self.Sq = S
        self.NSEQ = NSEQ
        self.L = L
        self.NIT = NIT
        self.groups = groups
        self.NB = S // 128
        self.NJ = S // 512
        self.NSEL = min(256, S // 4)
        self.rot = {}
        self.debug = False
        self.dbg_names = []
        self.marks = []

    def mm(self, out, lhsT, rhs, start, stop, R, W):
        nc = self.nc
        N = rhs.free_size()
        self.S.op('pe', lambda: nc.tensor.matmul(out, lhsT=lhsT, rhs=rhs, start=start, stop=stop,
                                                 skip_group_check=True), R, W, cost=(max(N, 64) + 110) / 2.2)

    def tr(self, out, in_, R, W):
        nc = self.nc
        idt = self.c_ident
        self.S.op('pe', lambda: nc.tensor.transpose(out, in_, idt), list(R) + ['const'], W, cost=110.0)

    def act(self, out, in_, func, R, W, scale=1.0, bias=0.0, accum=None):
        nc = self.nc
        self.S.op('act', lambda: nc.scalar.activation(out=out, in_=in_, func=func, bias=bias, scale=scale,
                                                      accum_out=accum), R, W, cost=in_.free_size() * 0.9 + 180)

    def ts(self, eng, out, in0, s1, s2, op0, op1, R, W, accum=None):
        eng = 'dve'
        e = self.nc.vector if eng == 'dve' else self.nc.gpsimd
        c = in0.free_size() * 1.05 + 120
        if op1 is None:
            self.S.op(eng, lambda: e.tensor_scalar(out=out, in0=in0, scalar1=s1, scalar2=None, op0=op0), R, W, cost=c)
        else:
            self.S.op(eng, lambda: e.tensor_scalar(out=out, in0=in0, scalar1=s1, scalar2=s2, op0=op0, op1=op1,
                                                   accum_out=accum), R, W, cost=c)

    def tt(self, eng, out, in0, in1, op, R, W):
        eng = 'dve'
        e = self.nc.vector if eng == 'dve' else self.nc.gpsimd
        self.S.op(eng, lambda: e.tensor_tensor(out=out, in0=in0, in1=in1, op=op), R, W, cost=in0.free_size() * 1.3 + 120)

    def stt(self, out, in0, scalar, in1, op0, op1, R, W, accum=None):
        nc = self.nc
        self.S.op('dve', lambda: nc.vector.scalar_tensor_tensor(out=out, in0=in0, scalar=scalar, in1=in1,
                                                                op0=op0, op1=op1, accum_out=accum), R, W,
                  cost=in0.free_size() * 1.3 + 120)

    def cp(self, eng, out, in_, R, W):
        nc = self.nc
        if eng == 'pool':
            eng = 'dve'
        N = in_.free_size()
        if eng == 'act':
            self.S.op('act', lambda: nc.scalar.copy(out=out, in_=in_), R, W, cost=N * 0.9 + 180)
        elif eng == 'dve':
            self.S.op('dve', lambda: nc.vector.tensor_copy(out=out, in_=in_), R, W, cost=N * 0.6 + 120)
        else:
            self.S.op('pool', lambda: nc.gpsimd.tensor_copy(out=out, in_=in_), R, W, cost=N * 0.6 + 250)

    def memset(self, eng, ap, val, W):
        eng = 'dve'
        e = self.nc.vector if eng == 'dve' else self.nc.gpsimd
        self.S.op(eng, lambda: e.memset(ap, val), (), W, cost=ap.free_size() * 0.6 + 120)

    def dma(self, q, out, in_, R, W):
        e = {'sp': self.nc.sync, 'pool': self.nc.gpsimd, 'act': self.nc.scalar}[q]
        self.S.op(q, lambda: e.dma_start(out=out, in_=in_), R, W, dma=True, cost=2200 + out.free_size() * 128 * 4 / 150.0)

    def dbg(self, name, ap, toks):
        if not getattr(self, 'debug', False):
            return
        d = self.nc.dram_tensor("dbg_" + name, list(ap.shape), ap.dtype, kind="ExternalOutput").ap()
        self.dma('sp', d, ap, toks, [('dbg', name)])
        self.dbg_names.append("dbg_" + name)

    def mark(self, label):
        self.marks.append((label, len(self.S.ops)))

    def rr(self, name, n):
        v = self.rot.get(name, 0)
        self.rot[name] = v + 1
        return v % n

    def evac_eng(self):
        return ('act', 'dve')[self.rr('evac', 2)]

    def evac(self, out, in_, R, W, scale=1.0, eng=None):
        eng = eng or self.evac_eng()
        if eng == 'act':
            if scale == 1.0:
                self.cp('act', out, in_, R, W)
            else:
                self.act(out, in_, AF.Copy, R, W, scale=scale)
        else:
            if scale == 1.0:
                self.cp('dve', out, in_, R, W)
            else:
                self.ts('dve', out, in_, scale, None, ALU.mult, None, R, W)

    def build(self):
        S_, NSEQ, L, NB = self.Sq, self.NSEQ, self.L, self.NB
        nc = bass.Bass("TRN2", target_bir_lowering=False)
        self.nc = nc
        dt_in = lambda name, shape: nc.dram_tensor(name, list(shape), F32, kind="ExternalInput").ap()
        self.d_x = dt_in("x", [NSEQ, S_, D_MODEL])
        self.d_win = dt_in("w_in_l", [L, 128, 8, D_IN])
        self.d_wout = dt_in("w_out_l", [L, 128, 8, D_MODEL])
        self.d_lnp = dt_in("lnp", [L, 2, 128, D_MODEL])
        self.d_relb = dt_in("relb", [128, 8, 2, 128])
        self.d_c31 = dt_in("c31", [128, 8])
        self.d_dlam = dt_in("dlam", [L, 128, 128])
        self.d_subln = dt_in("subln", [L, 128, 64])
        self.d_qn = dt_in("qn", [L, 128, 2])
        self.d_kvn = dt_in("kvn", [L, 128, 1])
        self.d_wuq = dt_in("wuq", [L, 128, 2, 384])
        self.d_wukv = dt_in("wukv", [L, 128, 512])
        self.d_consts = dt_in("consts", [128, 7, 128])
        self.d_rope = dt_in("rope", [2, 128, S_])
        self.d_out = nc.dram_tensor("out", [NSEQ, S_, D_MODEL], F32, kind="ExternalOutput").ap()
        self.d_xmid = nc.dram_tensor("xmid", [NSEQ, S_, D_MODEL], F32, kind="Internal").ap()
        self.d_ogd = nc.dram_tensor("ogd", [S_, D_MODEL], BF16, kind="Internal").ap()
        self.d_mb = nc.dram_tensor("mbd", [NB, 128, S_], BF16, kind="Internal").ap()
        es = ExitStack()
        with es:
            self.S = Sched(nc, es)
            self.alloc(es)
            self.setup_consts()
            for l in range(L):
                self.layer_setup(l)
                for s in range(NSEQ):
                    self.mark('xT %d %d' % (l, s))
                    self.build_xT(l, s)
                    self.dbg('xT', self.xT[:], ['xT'])
                    if "C" in self.groups:
                        self.mark('proj I')
                        self.project_indexer(l)
                    for g in "ABDC":
                        if g in self.groups:
                            self.mark('proj ' + g)
                            self.project_group(l, g)
                            self.dbg('QT' + g, self.QT, ['QT'])
                            self.dbg('KT' + g, self.KT, ['KT'])
                            self.dbg('VA' + g, self.VA[:], ['VA'])
                            self.dbg('SG' + g, self.SG[:], ['SG'])
                            self.mark('att ' + g)
                            self.attend_group(l, g)
                            self.dbg('OG' + g, self.OG[0][:], ['OG0'])
                        else:
                            self.zero_group(g)
                        if g == "A" and "C" in self.groups:
                            self.mark('indexer')
                            if "B" in self.groups:
                                self.indexer(0)
                            else:
                                self.indexer_all()
                    self.mark('out')
                    self.output_phase(l, s)
            self.S.finalize()
        return nc

    def alloc(self, es):
        nc = self.nc
        S_, NB = self.Sq, self.NB
        sb = lambda name, shape, dt: es.enter_context(nc.sbuf_tensor("sb_" + name, list(shape), dt))
        self.bank = [es.enter_context(nc.psum_tensor("bank%d" % k, [128, 512], F32)) for k in range(8)]
        self.cst_b = sb("cst_b", [128, 7, 128], BF16)
        self.c_ident = self.cst_b[:, 0, :]
        self.c_negtri = self.cst_b[:, 1, :]
        self.c_negones = self.cst_b[:, 2, :]
        self.c_CM = self.cst_b[:, 3, :]
        self.c_MA = self.cst_b[:, 4, :]
        self.c_strictT = self.cst_b[:, 5, :]
        self.c_ones = self.cst_b[:, 6, :]
        self.CMf = sb("CMf", [128, 128], F32)
        self.IDf = sb("IDf", [128, 128], F32)
        self.OTs = [sb("OTs%d" % k, [65, 512], F32) for k in range(1)]
        self.c_CMf = self.CMf[:]
        self.BT = sb("BT", [128, 8, 2, 128], BF16)
        self.c31 = sb("c31", [128, 8], F32)
        self.dummy = sb("dummy", [128, 2], F32)
        self.DL = sb("DL", [128, 128], F32)
        self.G64 = sb("G64", [128, 64], F32)
        self.sm = sb("sm", [128, 96], F32)
        self.QN = sb("QN", [128, 2], F32)
        self.KVN = sb("KVN", [128, 1], F32)
        self.WUQb = sb("WUQb", [128, 2, 384], BF16)
        self.WROT = sb("WROT", [128, 2, 4, 32], BF16)
        self.WUKVb = sb("WUKVb", [128, 512], BF16)
        self.WS = [sb("WS0", [128, 8, 256], F32)]
        self.WB = [sb("WB%d" % k, [128, 8, 256], BF16) for k in range(2)]
        self.WX = sb("WX", [128, 8, 128], BF16)
        self.xT = sb("xT", [128, 8, S_], BF16)
        self.XB = [sb("XB%d" % k, [128, 1024], F32) for k in range(2)]
        self.XBbraw = sb("XBbraw", [128, 2048], BF16)
        self.XBb = [self.XBbraw[:, k * 1024:(k + 1) * 1024] for k in range(2)]
        self.QTraw = sb("QTraw", [128, 8192], BF16)
        self.KTraw = sb("KTraw", [128, 8192], BF16)
        self.QT = self.QTraw[:, 0:4 * S_].rearrange("p (a s) -> p a s", a=4)
        self.KT = self.KTraw[:, 0:4 * S_].rearrange("p (a s) -> p a s", a=4)
        self.WO = self.QTraw[:, :].rearrange("p (a s) -> p a s", a=8)
        self.LNG = self.KTraw[:, 0:2048].bitcast(F32)
        self.LNB = self.KTraw[:, 2048:4096].bitcast(F32)
        self.IQ = sb("IQ", [128, 3, S_], BF16)
        if S_ >= 2048:
            iqf = self.IQ[:].rearrange("p a s -> p (a s)")
            self.XS = [iqf[:, k * 2048:(k + 1) * 2048].bitcast(F32) for k in range(2)]
            self.XSb = [iqf[:, 4096 + k * 1024:4096 + (k + 1) * 1024] for k in range(2)]
            self.xs_tok = (['IQs0', 'IQs1'], ['IQb0', 'IQb1'])
        else:
            self.XS = self.XSb = None
        self.IK3 = sb("IK3", [128, S_], BF16)
        self.VA = sb("VA", [128, NB, 4, 65], BF16)
        self.SG = sb("SG", [128, NB, 256], BF16)
        self.IW = sb("IW", [128, NB, 8], F32)
        mbsz = max(4 * S_, 7680)
        nscr = mbsz + S_ + 1536
        self.SCRb = sb("SCRb", [128, nscr], BF16)
        X = self.SCRb
        self.MB = X[:, 0:4 * S_].rearrange("p (a s) -> p a s", a=4)
        self.JK = X[:, mbsz:mbsz + S_]
        self.RL = [X[:, mbsz + S_ + k * 512:mbsz + S_ + (k + 1) * 512] for k in range(3)]
        self.CQT = X[:, 0:1024].rearrange("p (a s) -> p a s", a=2)
        self.CKVT = X[:, 1024:1536]
        self.SQT = X[:, 1536:3072].rearrange("p (a s) -> p a s", a=3)
        self.KRr = X[:, 3072:3584]
        self.RSQ = X[:, 3584:4608].bitcast(F32)
        self.RSK = X[:, 4608:5632].bitcast(F32)
        self.ROPE = X[:, 5632:7680].bitcast(F32).rearrange("p (a s) -> p a s", a=2)
        self.scr_tokens = [('MB', i) for i in range(4)] + ['CQT', 'CKVT', 'SQT', 'KRr', 'RSQ', 'RSK', 'ROPE']
        self.E = [sb("E%d" % k, [128, 512], F32) for k in range(3)]
        self.SPb = [sb("SPb%d" % k, [128, 512], BF16) for k in range(3)]
        self.MT = [self.E[0], self.E[1]]
        self.R32 = sb("R32", [128, 512], F32)
        self.Rbf = sb("Rbf", [128, 512], BF16)
        self.PT = [sb("PT%d" % k, [128, 512], BF16) for k in range(4)]
        self.SCraw = sb("SC", [128, max(S_, 2048)], F32)
        self.SC2 = sb("SC2", [128, S_], F32)
        self.SCs = [self.SCraw[:, 0:S_], self.SC2[:, :]]
        self.P2 = sb("P2", [128, 32], F32)
        self.DG = sb("DG", [128, 8, 128], BF16)
        self.OG = [sb("OG%d" % k, [128, 4, 256], BF16) for k in range(1)]
        self.EP = [sb("EP%d" % k, [128, 4, 64], F32) for k in range(4)]
        self.OGB = sb("OGB", [128, 1024], BF16)
        self.OGT = sb("OGT", [128, 8, 128], BF16)
        self.XR = self.XB[0]
        self.Z = self.XB[1]
        self.ZC = self.XBbraw[:, :].bitcast(F32)

    def fence(self, tokens):
        d = self.dummy
        self.S.op('pool', lambda: self.nc.gpsimd.memset(d[:, 0:1], 0.0), (), list(tokens))

    def setup_consts(self):
        cst_f = self.SCraw[:, 0:896].rearrange("p (a s) -> p a s", a=7)
        self.dma('sp', cst_f, self.d_consts, (), ['SC0'])
        self.cp('dve', self.cst_b[:], cst_f, ['SC0'], ['const'])
        self.cp('dve', self.CMf[:], cst_f[:, 3, :], ['SC0'], ['CMf'])
        self.cp('dve', self.IDf[:], cst_f[:, 0, :], ['SC0'], ['const'])
        relb_f = self.SCraw[:, 0:2048].rearrange("p (h a s) -> p h a s", h=8, a=2)
        self.dma('sp', relb_f, self.d_relb, ['SC0'], ['SC0'])
        self.dma('sp', self.c31[:], self.d_c31, (), ['c31'])
        for h in range(8):
            self.ts('dve', relb_f[:, h, :, :], relb_f[:, h, :, :], self.c31[:, h:h + 1], None,
                    ALU.subtract, None, ['SC0', 'c31'], ['SC0'])
            self.tt('dve', relb_f[:, h, 0, :], relb_f[:, h, 0, :], self.c_CMf, ALU.add,
                    ['SC0', 'CMf'], ['SC0'])
        self.cp('dve', self.BT[:], relb_f, ['SC0'], ['BT'])
        self.memset('pool', self.VA[:, :, :, 64:65], 1.0, ['VA'])
        for j in range(self.NIT):
            self.memset('pool', self.P2[:, j:j + 1], 2.0 ** -(j + 1), ['P2'])

    def load_piece(self, l, col0, ncols, tag):
        k = self.rr('wb', 2)
        self.dma('sp', self.WS[0][:, :, 0:ncols], self.d_win[l, :, :, col0:col0 + ncols], (), ['WS0'])
        self.cp('pool', self.WB[k][:, :, 0:ncols], self.WS[0][:, :, 0:ncols], ['WS0'], ['WB%d' % k])
        return self.WB[k], 'WB%d' % k

    def layer_setup(self, l):
        sm = self.sm
        lam_init = 0.8 - 0.6 * math.exp(-0.3 * l)
        self.lam_init = lam_init
        self.dma('sp', self.DL[:], self.d_dlam[l], (), ['DL'])
        self.dma('sp', self.G64[:], self.d_subln[l], (), ['G64'])
        self.dma('sp', self.QN[:], self.d_qn[l], (), ['QN'])
        self.dma('sp', self.KVN[:], self.d_kvn[l], (), ['KVN'])
        self.tt('dve', self.DL[:, 0:32], self.DL[:, 0:32], self.DL[:, 32:64], ALU.mult, ['DL'], ['DL'])
        self.tt('dve', self.DL[:, 64:96], self.DL[:, 64:96], self.DL[:, 96:128], ALU.mult, ['DL'], ['DL'])
        self.S.op('dve', lambda: self.nc.vector.tensor_reduce(out=sm[:, 0:1], in_=self.DL[:, 0:32], axis=AX.X, op=ALU.add),
                  ['DL'], ['sm0'])
        self.S.op('dve', lambda: self.nc.vector.tensor_reduce(out=sm[:, 1:2], in_=self.DL[:, 64:96], axis=AX.X, op=ALU.add),
                  ['DL'], ['sm1'])
        self.act(sm[:, 2:3], sm[:, 0:1], AF.Exp, ['sm0'], ['sm2'])
        self.act(sm[:, 3:4], sm[:, 1:2], AF.Exp, ['sm1'], ['sm3'])
        self.tt('dve', sm[:, 4:5], sm[:, 3:4], sm[:, 2:3], ALU.subtract, ['sm2', 'sm3'], ['sm4'])
        self.ts('dve', sm[:, 5:6], sm[:, 4:5], -lam_init, None, ALU.add, None, ['sm4'], ['neglam'])
        self.neglam = sm[:, 5:6]
        self.ts('dve', self.G64[:], self.G64[:], 1.0 - lam_init, None, ALU.mult, None, ['G64'], ['G64'])
        k = 0
        wsv = self.WS[k][:].rearrange("p a b -> p (a b)")
        self.dma('sp', wsv[:, 0:768], self.d_wuq[l].rearrange("p a b -> p (a b)"), (), ['WS%d' % k])
        for c in range(2):
            self.ts('dve', self.WUQb[:, c, :], wsv[:, c * 384:(c + 1) * 384], self.QN[:, c:c + 1], None,
                    ALU.mult, None, ['WS%d' % k, 'QN'], ['WUQb'])
        for c in range(2):
            for h in range(4):
                self.ts('dve', self.WROT[:, c, h, 0:16], self.WUQb[:, c, h * 96 + 80:h * 96 + 96], -1.0, None,
                        ALU.mult, None, ['WUQb'], ['WROT'])
                self.cp('dve', self.WROT[:, c, h, 16:32], self.WUQb[:, c, h * 96 + 64:h * 96 + 80], ['WUQb'], ['WROT'])
        k = 0
        wsv = self.WS[k][:].rearrange("p a b -> p (a b)")
        self.dma('sp', wsv[:, 0:512], self.d_wukv[l], (), ['WS%d' % k])
        self.ts('dve', self.WUKVb[:], wsv[:, 0:512], self.KVN[:, 0:1], None, ALU.mult, None,
                ['WS%d' % k, 'KVN'], ['WUKVb'])

    def xsrc(self, l, s):
        return self.d_x[s] if l == 0 else self.d_xmid[s]

    def build_xT(self, l, s):
        self.cur = (l, s)
        src = self.xsrc(l, s)
        b7 = self.bank[7][:].bitcast(BF16)
        if self.XS is not None:
            XB, XBb, (xt_, xbt_) = self.XS, self.XSb, self.xs_tok
            self.fence(['IQ'] + xt_ + xbt_)
        else:
            XB, XBb = [t[:] for t in self.XB], self.XBb
            xt_, xbt_ = ['XB0', 'XB1'], ['XBb0', 'XBb1']
        for tb in range(self.NB):
            k = self.rr('xb', 2)
            self.dma('sp', XB[k], src[tb * 128:(tb + 1) * 128, :], [('xm', s, tb)], [xt_[k]])
            self.cp('dve', XBb[k], XB[k], [xt_[k]], [xbt_[k]])
            for c in range(8):
                self.tr(b7[:, c * 128:(c + 1) * 128], XBb[k][:, c * 128:(c + 1) * 128], [xbt_[k]], ['b7'])
            self.evac(self.xT[:, :, tb * 128:(tb + 1) * 128], b7.rearrange("p (c t) -> p c t", c=8), ['b7'], ['xT'])
        if self.XS is not None:
            self.fence(['IQ'] + xt_ + xbt_)

    def proj_fm(self, wt, wtok, c0, M, dest_fn, extra_R=()):
        for tc in range(self.NJ):
            b = self.rr('pb', 4)
            ps = self.bank[b]
            for c in range(8):
                self.mm(ps[0:M, :], wt[:, c, c0:c0 + M], self.xT[:, c, tc * 512:(tc + 1) * 512], c == 0, c == 7,
                        [wtok, 'xT'] + list(extra_R), ['b%d' % b])
            dest_fn(tc, ps, 'b%d' % b)

    def proj_tm(self, wt, wtok, c0, N, dest_fn):
        for tb in range(self.NB):
            b = self.rr('pb', 4)
            ps = self.bank[b]
            for c in range(8):
                self.mm(ps[:, 0:N], self.xT[:, c, tb * 128:(tb + 1) * 128], wt[:, c, c0:c0 + N], c == 0, c == 7,
                        [wtok, 'xT'], ['b%d' % b])
            dest_fn(tb, ps, 'b%d' % b)

    def project_group(self, l, g):
        S_ = self.Sq
        gi = "ABCD".index(g)
        qname, kname, vname = {"A": ("a_q", "a_k", "a_v"), "B": ("b_q", "b_k", "b_v"),
                               "C": ("c_q", "c_k", "c_v"), "D": (None, None, None)}[g]
        wt, wtok = self.load_piece(l, OFF["gate"][0] + gi * 256, 256, 'gate')

        def gate_dest(tb, ps, btok):
            self.act(self.SG[:, tb, :], ps[:, 0:256], AF.Silu, [btok], ['SG'])
        self.proj_tm(wt, wtok, 0, 256, gate_dest)
        if g in "ABC":
            wt, wtok = self.load_piece(l, OFF[vname][0], 256, 'v')

            def v_dest(tb, ps, btok):
                self.evac(self.VA[:, tb, :, 0:64], ps[:, 0:256].rearrange("p (h d) -> p h d", h=4), [btok], ['VA'])
            self.proj_tm(wt, wtok, 0, 256, v_dest)
            if g == "B":
                qs = 32 ** -0.5
                wt, wtok = self.load_piece(l, OFF[qname][0], 256, qname)
                for h in range(4):
                    self.memset('dve', self.WX[:, :, 32:96], 0.0, ['WX'])
                    self.cp('dve', self.WX[:, :, 0:32], wt[:, :, h * 64:h * 64 + 32], [wtok], ['WX'])
                    self.cp('dve', self.WX[:, :, 96:128], wt[:, :, h * 64 + 32:h * 64 + 64], [wtok], ['WX'])

                    def qdest(tc, ps, btok, h=h):
                        self.evac(self.QT[:, h, tc * 512:(tc + 1) * 512], ps[:, :], [btok], ['QT'], scale=qs)
                    self.proj_fm(self.WX, 'WX', 0, 128, qdest)
                wt, wtok = self.load_piece(l, OFF[kname][0], 256, kname)
                for h in range(4):
                    self.cp('dve', self.WX[:, :, 0:64], wt[:, :, h * 64:h * 64 + 64], [wtok], ['WX'])
                    self.cp('dve', self.WX[:, :, 64:128], wt[:, :, h * 64:h * 64 + 64], [wtok], ['WX'])

                    def kdest(tc, ps, btok, h=h):
                        self.evac(self.KT[:, h, tc * 512:(tc + 1) * 512], ps[:, :], [btok], ['KT'])
                    self.proj_fm(self.WX, 'WX', 0, 128, kdest)
            else:
                qs = 64 ** -0.5
                for cg in range(2):
                    self.memset('dve', self.QT[64:128, 2 * cg, :], 0.0, ['QT'])
                    self.memset('dve', self.QT[0:64, 2 * cg + 1, :], 0.0, ['QT'])
                wt, wtok = self.load_piece(l, OFF[qname][0], 256, qname)
                for cg in range(2):
                    def qdest(tc, ps, btok, cg=cg):
                        tsl = slice(tc * 512, (tc + 1) * 512)
                        self.evac(self.QT[0:64, 2 * cg, tsl], ps[0:64, :], [btok], ['QT'], scale=qs)
                        self.evac(self.QT[64:128, 2 * cg + 1, tsl], ps[64:128, :], [btok], ['QT'], scale=qs)
                    self.proj_fm(wt, wtok, cg * 128, 128, qdest)
                wt, wtok = self.load_piece(l, OFF[kname][0], 256, kname)
                for cg in range(2):
                    def kdest(tc, ps, btok, cg=cg):
                        self.evac(self.KT[:, cg, tc * 512:(tc + 1) * 512], ps[:, :], [btok], ['KT'])
                    self.proj_fm(wt, wtok, cg * 128, 128, kdest)
        else:
            self.fence(self.scr_tokens)
            self.memset('dve', self.QT[96:128, :, :], 0.0, ['QT'])
            self.memset('dve', self.KT[96:128, :, :], 0.0, ['KT'])
            self.project_mla(l)
            self.fence(self.scr_tokens)

    def project_indexer(self, l):
            wt, wtok = self.load_piece(l, OFF["c_iq"][0], 256, 'iq')
            for cg in range(3):
                M = 96 if cg < 2 else 64

                def dest(tc, ps, btok, cg=cg, M=M):
                    self.evac(self.IQ[0:M, cg, tc * 512:(tc + 1) * 512], ps[0:M, :], [btok], ['IQ'], scale=32 ** -0.5)
                self.proj_fm(wt, wtok, cg * 96, M, dest)
            wt, wtok = self.load_piece(l, OFF["c_ik"][0], 40, 'ik')
            for r in range(3):
                self.cp('pool', self.WX[:, :, r * 32:(r + 1) * 32], wt[:, :, 0:32], [wtok], ['WX'])

            def ik_dest(tc, ps, btok):
                self.evac(self.IK3[0:96, tc * 512:(tc + 1) * 512], ps[0:96, :], [btok], ['IK3'])
            self.proj_fm(self.WX, 'WX', 0, 96, ik_dest)

            def iw_dest(tb, ps, btok):
                self.evac(self.IW[:, tb, :], ps[:, 0:8], [btok], ['IW'], scale=8 ** -0.5, eng='dve')
            self.proj_tm(wt, wtok, 32, 8, iw_dest)

    def project_mla(self, l):
        S_ = self.Sq
        sc_q = 96 ** -0.5
        wcq, tcq = self.load_piece(l, OFF["d_cq"][0], 256, 'cq')
        wkv, tkv = self.load_piece(l, OFF["d_ckv"][0], 160, 'ckv')
        self.ts('pool', self.WX[:, :, 0:16], wkv[:, :, 144:160], -1.0, None, ALU.mult, None, [tkv], ['WX'])
        self.cp('pool', self.WX[:, :, 16:32], wkv[:, :, 128:144], [tkv], ['WX'])
        bk = self.bank
        for tc in range(self.NJ):
            tsl = slice(tc * 512, (tc + 1) * 512)
            self.dma('sp', self.ROPE[:], self.d_rope[:, :, tsl].rearrange("a p t -> p a t"), (), ['ROPE'])
            Ct = self.ROPE[:, 0, :]
            St = self.ROPE[:, 1, :]
            for cg in range(3):
                b = self.rr('pb', 4)
                ps = bk[b]
                wt, wtok, c0 = (wcq, tcq, cg * 128) if cg < 2 else (wkv, tkv, 0)
                for c in range(8):
                    self.mm(ps[:, :], wt[:, c, c0:c0 + 128], self.xT[:, c, tsl], c == 0, c == 7, [wtok, 'xT'], ['b%d' % b])
                dst = self.CQT[:, cg, :] if cg < 2 else self.CKVT[:]
                self.cp('act', dst, ps[:, :], ['b%d' % b], ['CQT' if cg < 2 else 'CKVT'])
                self.act(self.SQT[:, cg, :], ps[:, :], AF.Square, ['b%d' % b], ['SQT'])
            for which, ncg, dst, dtok, rank, scl in (('q', (0, 1), self.RSQ, 'RSQ', 256, sc_q), ('k', (2,), self.RSK, 'RSK', 128, 1.0)):
                b = self.rr('pb', 4)
                ps = bk[b]
                for j, cg in enumerate(ncg):
                    self.mm(ps[:, :], self.c_ones, self.SQT[:, cg, :], j == 0, j == len(ncg) - 1, ['const', 'SQT'], ['b%d' % b])
                self.act(dst[:], ps[:, :], AF.Ln, ['b%d' % b], [dtok], scale=1.0 / rank, bias=1e-6)
                self.act(dst[:], dst[:], AF.Exp, [dtok], [dtok], scale=-0.5)
                if scl != 1.0:
                    self.ts('dve', dst[:], dst[:], scl, None, ALU.mult, None, [dtok], [dtok])
            b = self.rr('pb', 4)
            b2 = self.rr('pb', 4)
            for c in range(8):
                self.mm(bk[b][64:96, :], wkv[:, c, 128:160], self.xT[:, c, tsl], c == 0, c == 7, [tkv, 'xT'], ['b%d' % b])
            for c in range(8):
                self.mm(bk[b2][64:96, :], self.WX[:, c, 0:32], self.xT[:, c, tsl], c == 0, c == 7, ['WX', 'xT'], ['b%d' % b2])
            m0, m1 = self.MT[0], self.MT[1]
            self.tt('dve', m0[64:96, :], bk[b][64:96, :], Ct[64:96, :], ALU.mult, ['b%d' % b, 'ROPE'], ['E0'])
            self.tt('dve', m1[64:96, :], bk[b2][64:96, :], St[64:96, :], ALU.mult, ['b%d' % b2, 'ROPE'], ['E1'])
            self.tt('pool', self.KRr[64:96, :], m0[64:96, :], m1[64:96, :], ALU.add, ['E0', 'E1'], ['KRr'])
            for h in range(4):
                b = self.rr('pb', 4)
                self.mm(bk[b][0:64, :], self.WUKVb[:, h * 128:h * 128 + 64], self.CKVT[:], True, True,
                        ['WUKVb', 'CKVT'], ['b%d' % b])
                self.tt('dve', self.KT[0:64, h, tsl], bk[b][0:64, :], self.RSK[0:64, :], ALU.mult,
                        ['b%d' % b, 'RSK'], ['KT'])
                self.cp('act', self.KT[64:96, h, tsl], self.KRr[64:96, :], ['KRr'], ['KT'])
                b = self.rr('pb', 4)
                b2 = self.rr('pb', 4)
                for c in range(2):
                    self.mm(bk[b][0:96, :], self.WUQb[:, c, h * 96:(h + 1) * 96], self.CQT[:, c, :], c == 0, c == 1,
                            ['WUQb', 'CQT'], ['b%d' % b])
                for c in range(2):
                    self.mm(bk[b2][64:96, :], self.WROT[:, c, h, :], self.CQT[:, c, :], c == 0, c == 1,
                            ['WROT', 'CQT'], ['b%d' % b2])
                self.tt('dve', self.QT[0:64, h, tsl], bk[b][0:64, :], self.RSQ[0:64, :], ALU.mult,
                        ['b%d' % b, 'RSQ'], ['QT'])
                self.tt('dve', m0[64:96, :], bk[b][64:96, :], Ct[64:96, :], ALU.mult, ['b%d' % b, 'ROPE'], ['E0'])
                self.tt('dve', m1[64:96, :], bk[b2][64:96, :], St[64:96, :], ALU.mult, ['b%d' % b2, 'ROPE'], ['E1'])
                self.tt('pool', m0[64:96, :], m0[64:96, :], m1[64:96, :], ALU.add, ['E0', 'E1'], ['E0'])
                self.tt('pool', self.QT[64:96, h, tsl], m0[64:96, :], self.RSQ[64:96, :], ALU.mult, ['E0', 'RSQ'], ['QT'])
            for t4 in range(4):
                tb = tc * 4 + t4
                b = self.rr('pb', 4)
                ps = bk[b]
                for h in range(4):
                    self.mm(ps[:, h * 64:(h + 1) * 64], self.CKVT[:, t4 * 128:(t4 + 1) * 128],
                            self.WUKVb[:, h * 128 + 64:h * 128 + 128], True, True, ['CKVT', 'WUKVb'], ['b%d' % b])
                self.mm(ps[:, 256:257], self.SQT[:, 2, t4 * 128:(t4 + 1) * 128], self.c_ones[:, 0:1], True, True,
                        ['SQT', 'const'], ['b%d' % b])
                sm = self.sm
                self.act(sm[:, 8:9], ps[:, 256:257], AF.Ln, ['b%d' % b], ['sm8'], scale=1.0 / 128, bias=1e-6)
                self.act(sm[:, 9:10], sm[:, 8:9], AF.Exp, ['sm8'], ['sm9'], scale=-0.5)
                self.ts('dve', self.VA[:, tb, :, 0:64], ps[:, 0:256].rearrange("p (h d) -> p h d", h=4), sm[:, 9:10], None,
                        ALU.mult, None, ['b%d' % b, 'sm9'], ['VA'])

    def attend_group(self, l, g):
        self.sbanks = [0, 1, 2, 5, 6, 7] if g == "D" else [0, 1, 2, 5]
        for J in range(self.NJ):
            k = 0
            OG, ogtok = self.OG[k], 'OG%d' % k
            if g == "C":
                for il in range(4):
                    i = 4 * J + il
                    if self.need_sel(i):
                        Nk = 128 * (i + 1)
                        self.dma('sp', self.MB[:, il, 0:Nk], self.d_mb[i, :, 0:Nk], [('mbd', i)], [('MB', il)])
            for h in range(4):
                if g == "A":
                    self.attn_A(J, h, OG, ogtok)
                elif g == "B":
                    self.attn_B(J, h, OG, ogtok)
                elif g == "C":
                    self.attn_CD(J, h, OG, ogtok, True)
                else:
                    self.attn_CD(J, h, OG, ogtok, False)
            gi = "ABCD".index(g)
            dst = self.d_ogd[J * 512:(J + 1) * 512, gi * 256:(gi + 1) * 256].rearrange("(i p) c -> p i c", p=128)
            self.dma('pool', dst, OG[:], [ogtok], [('ogd', J * 4 + i) for i in range(4)])
            if g == "B" and "C" in self.groups and J + 1 < self.NJ:
                self.indexer(J + 1)

    def zero_group(self, g):
        gi = "ABCD".index(g)
        for J in range(self.NJ):
            k = 0
            OG, ogtok = self.OG[k], 'OG%d' % k
            self.memset('pool', OG[:], 0.0, [ogtok])
            dst = self.d_ogd[J * 512:(J + 1) * 512, gi * 256:(gi + 1) * 256].rearrange("(i p) c -> p i c", p=128)
            self.dma('pool', dst, OG[:], [ogtok], [('ogd', J * 4 + i) for i in range(4)])

    def sbank(self):
        b = self.sbanks[self.rr('sb', len(self.sbanks))]
        return self.bank[b], 'b%d' % b

    def pv(self, acc, acctok, PT, pttok, a, h, c0, first):
        self.mm(acc[0:65, c0:512], self.VA[:, a, h, :], PT[:, c0:512], bool(first), True, [pttok, 'VA'], [acctok])

    def acc_finish(self, acc, acctok):
        k = 0
        ot, ottok = self.OTs[k], 'OTs%d' % k
        self.evac(ot[:, :], acc[0:65, :], [acctok], [ottok])
        accT, acctT = self.sbank()
        nc = self.nc
        for il in range(4):
            o_ = accT[:, il * 65:(il + 1) * 65]
            i_ = ot[:, il * 128:(il + 1) * 128]
            idf = self.IDf[0:65, 0:65]
            self.S.op('pe', lambda o_=o_, i_=i_, idf=idf: nc.tensor.transpose(o_, i_, idf), [ottok, 'const'], [acctT], cost=110.0)
        return accT, acctT

    def attn_A(self, J, h, OG, ogtok):
        hb = (h % 2) * 64
        hs = h // 2
        ab = 3 + self.rr('acc', 2)
        acc, acctok = self.bank[ab], 'b%d' % ab
        self.memset('dve', acc[0:65, :], 0.0, [acctok])
        self.memset('pool', self.R32[:], 0.0, ['R32'])
        amax = 4 * J + 3
        for a in range(amax, -1, -1):
            m = a - 4 * J
            c0 = 128 * max(0, m)
            qsl = slice(J * 512 + c0, (J + 1) * 512)
            ksl = slice(a * 128, (a + 1) * 128)
            K = self.KT[:, hs, ksl]
            Q = self.QT[:, h, qsl]
            ps, pstok = self.sbank()
            self.mm(ps[:, c0:512], K, Q, True, True, ['KT', 'QT'], [pstok])
            e = self.rr('E', 3)
            E, etok = self.E[e], 'E%d' % e
            SP, sptok = self.SPb[e], 'SP%d' % e
            self.act(E[:, c0:512], ps[:, c0:512], AF.Exp, [pstok], [etok])
            self.act(SP[:, c0:512], E[:, c0:512], AF.Ln, [etok], [sptok], bias=1.0)
            if m >= 0:
                self.tt('pool', SP[:, c0:c0 + 128], SP[:, c0:c0 + 128], self.c_strictT, ALU.mult, [sptok, 'const'], [sptok])
            ps2, ps2tok = ps, pstok
            first = (a == amax)
            if m >= 0:
                self.mm(ps2[:, c0:c0 + 128], self.c_MA, self.c_ident, False, False, ['const', etok], [ps2tok])
            if not first:
                self.mm(ps2[:, c0:512], self.c_negones, self.Rbf[:, c0:512], False, False, ['const', 'Rbf', etok], [ps2tok])
            self.mm(ps2[:, c0:512], self.c_negtri, SP[:, c0:512], False, True, ['const', sptok], [ps2tok])
            p = self.rr('PT', 4)
            PT, pttok = self.PT[p], 'PT%d' % p
            self.act(PT[:, c0:512], ps2[:, c0:512], AF.Exp, [ps2tok], [pttok])
            self.pv(acc, acctok, PT, pttok, a, h, c0, False)
            if a > 0:
                self.tt('pool', self.R32[:, c0:512], self.R32[:, c0:512], SP[:, c0:512], ALU.add, ['R32', sptok], ['R32'])
                cn = 128 * max(0, m - 1)
                self.cp('pool', self.Rbf[:, cn:512], self.R32[:, cn:512], ['R32'], ['Rbf'])
        accT, acctT = self.acc_finish(acc, acctok)
        accv = accT[:, 0:260].rearrange("p (i d) -> p i d", d=65)
        self.tt('dve', OG[:, :, h * 64:(h + 1) * 64], accv[:, :, 0:64], self.SG[:, 4 * J:4 * J + 4, h * 64:(h + 1) * 64],
                ALU.mult, [acctT, 'SG'], [ogtok])

    def near_bias(self, ps, pstok, a, J, c0, bh):
        for il in range(c0 // 128, 4):
            i = 4 * J + il
            if i == a:
                self.mm(ps[:, il * 128:(il + 1) * 128], self.BT[:, bh, 0, :], self.c_ident, False, False, ['BT', 'const'], [pstok])
            elif i == a + 1:
                self.mm(ps[:, il * 128:(il + 1) * 128], self.BT[:, bh, 1, :], self.c_ident, False, False, ['BT', 'const'], [pstok])

    def softmax_norm(self, acc, acctok, dst, dsttok, ri):
        acc, acctok = self.acc_finish(acc, acctok)
        accv = acc[:, 0:260].rearrange("p (i d) -> p i d", d=65)
        rc = self.sm[:, 16 + 4 * ri:20 + 4 * ri]
        rtok = 'rc%d' % ri
        self.S.op('dve', lambda: self.nc.vector.reciprocal(out=rc, in_=accv[:, :, 64]), [acctok], [rtok])
        self.tt('dve', dst[:], accv[:, :, 0:64], rc.unsqueeze(2).broadcast_to([128, 4, 64]), ALU.mult,
                [acctok, rtok], [dsttok])

    def attn_B(self, J, h, OG, ogtok):
        accs = []
        for c in range(2):
            acc, acctok = self.bank[3 + c], 'b%d' % (3 + c)
            accs.append((acc, acctok))
        for a in range(4 * J + 4):
            m = a - 4 * J
            c0 = 128 * max(0, m)
            qsl = slice(J * 512 + c0, (J + 1) * 512)
            ksl = slice(a * 128, (a + 1) * 128)
            pss = [self.sbank() for c in range(2)]
            nc = self.nc
            o0, o1 = pss[0][0][:, c0:512], pss[1][0][:, c0:512]
            k0, k1 = self.KT[0:64, h, ksl], self.KT[64:128, h, ksl]
            q0, q1 = self.QT[0:64, h, qsl], self.QT[64:128, h, qsl]

            def qk2(o0=o0, o1=o1, k0=k0, k1=k1, q0=q0, q1=q1):
                nc.tensor.matmul(o0, lhsT=k0, rhs=q0, start=True, stop=False, skip_group_check=True)
                return nc.tensor.matmul(o1, lhsT=k1, rhs=q1, start=True, stop=False, skip_group_check=True)
            self.S.op('pe', qk2, ['KT', 'QT'], [pss[0][1], pss[1][1]], cost=(512 - c0 + 110) / 2.2)
            for c in range(2):
                ps, pstok = pss[c]
                self.near_bias(ps, pstok, a, J, c0, h)
            for c in range(2):
                acc, acctok = accs[c]
                ps, pstok = pss[c]
                p = self.rr('PT', 4)
                PT, pttok = self.PT[p], 'PT%d' % p
                self.act(PT[:, c0:512], ps[:, c0:512], AF.Exp, [pstok, 'c31'], [pttok], bias=self.c31[:, h:h + 1])
                self.pv(acc, acctok, PT, pttok, a, h, c0, a == 0)
        T0, T1, T2 = self.EP[0], self.EP[1], self.EP[2]
        self.softmax_norm(accs[0][0], accs[0][1], T0, 'EP0', 0)
        self.softmax_norm(accs[1][0], accs[1][1], T1, 'EP1', 1)
        f = lambda t: t[:].rearrange("p i d -> p (i d)")
        self.stt(f(T2), f(T1), self.neglam, f(T0), ALU.mult, ALU.add, ['EP0', 'EP1', 'neglam'], ['EP2'])
        self.tt('pool', T0[:], T2[:], T2[:], ALU.mult, ['EP2'], ['EP0'])
        ss = self.sm[:, 32:36]
        self.S.op('dve', lambda: self.nc.vector.tensor_reduce(out=ss, in_=T0[:], axis=AX.X, op=ALU.add), ['EP0'], ['ss'])
        self.act(ss, ss, AF.Ln, ['ss'], ['ss'], scale=1.0 / 64, bias=1e-6)
        self.act(ss, ss, AF.Exp, ['ss'], ['ss'], scale=-0.5)
        self.tt('dve', T1[:], T2[:], ss.unsqueeze(2).broadcast_to([128, 4, 64]), ALU.mult, ['EP2', 'ss'], ['EP1'])
        self.tt('pool', T0[:], T1[:], self.G64[:].unsqueeze(1).broadcast_to([128, 4, 64]), ALU.mult, ['EP1', 'G64'], ['EP0'])
        self.tt('pool', OG[:, :, h * 64:(h + 1) * 64], T0[:], self.SG[:, 4 * J:4 * J + 4, h * 64:(h + 1) * 64], ALU.mult,
                ['EP0', 'SG'], [ogtok])

    def need_sel(self, i):
        return 128 * (i + 1) > self.NSEL

    def attn_CD(self, J, h, OG, ogtok, isC):
        ab = 3 + self.rr('acc', 2)
        acc, acctok = self.bank[ab], 'b%d' % ab
        if isC:
            hs = h // 2
        else:
            hs = h
        for a in range(4 * J + 4):
            m = a - 4 * J
            c0 = 128 * max(0, m)
            qsl = slice(J * 512 + c0, (J + 1) * 512)
            ksl = slice(a * 128, (a + 1) * 128)
            ps, pstok = self.sbank()
            self.mm(ps[:, c0:512], self.KT[:, hs, ksl], self.QT[:, h, qsl], True, False, ['KT', 'QT'], [pstok])
            if isC:
                self.near_bias(ps, pstok, a, J, c0, 4 + h)
                for il in range(c0 // 128, 4):
                    i = 4 * J + il
                    if self.need_sel(i):
                        self.mm(ps[:, il * 128:(il + 1) * 128], self.MB[:, il, ksl], self.c_ident, False, False,
                                [('MB', il), 'const'], [pstok])
            elif m >= 0:
                self.mm(ps[:, c0:c0 + 128], self.c_CM, self.c_ident, False, False, ['const'], [pstok])
            p = self.rr('PT', 4)
            PT, pttok = self.PT[p], 'PT%d' % p
            if isC:
                self.act(PT[:, c0:512], ps[:, c0:512], AF.Exp, [pstok, 'c31'], [pttok], bias=self.c31[:, 4 + h:5 + h])
            else:
                self.act(PT[:, c0:512], ps[:, c0:512], AF.Exp, [pstok], [pttok])
            self.pv(acc, acctok, PT, pttok, a, h, c0, a == 0)
        e = self.rr('EPc', 2)
        T0, ttok = self.EP[e], 'EP%d' % e
        self.softmax_norm(acc, acctok, T0, ttok, 2 + e)
        self.tt('pool', OG[:, :, h * 64:(h + 1) * 64], T0[:], self.SG[:, 4 * J:4 * J + 4, h * 64:(h + 1) * 64], ALU.mult,
                [ttok, 'SG'], [ogtok])

    def indexer_all(self):
        for J in range(self.NJ):
            self.indexer(J)

    def indexer(self, J):
        sm = self.sm
        nc = self.nc
        for il in range(4):
            i = 4 * J + il
            if not self.need_sel(i):
                continue
            Nk = 128 * (i + 1)
            isl = slice(i * 128, (i + 1) * 128)
            kq = self.rr('SCq', 2)
            SC, sctk = self.SCs[kq], 'SC%d' % kq
            for hh in range(8):
                self.ts('pool', self.DG[:, hh, :], self.c_ident, self.IW[:, i, hh:hh + 1], 0.0, ALU.mult, ALU.add,
                        ['const', 'IW'], ['DG'])
            for kc in range((Nk + 511) // 512):
                n = min(512, Nk - kc * 512)
                ksl = slice(kc * 512, kc * 512 + n)
                sc, sctok = self.bank[7], 'b7'
                for hh in range(8):
                    cg, rb = hh // 3, (hh % 3) * 32
                    d = 6
                    dps, dtok = self.bank[d], 'b%d' % d
                    self.mm(dps[:, 0:n], self.IQ[rb:rb + 32, cg, isl], self.IK3[rb:rb + 32, ksl], True, True, ['IQ', 'IK3'], [dtok])
                    r = self.rr('RL', 3)
                    RL, rtok = self.RL[r], 'RL%d' % r
                    if True:
                        self.act(RL[:, 0:n], dps[:, 0:n], AF.Relu, [dtok], [rtok])
                    else:
                        self.ts('dve', RL[:, 0:n], dps[:, 0:n], 0.0, None, ALU.max, None, [dtok], [rtok])
                    self.mm(sc[:, 0:n], self.DG[:, hh, :], RL[:, 0:n], hh == 0, hh == 7, ['DG', rtok], [sctok])
                self.cp('act', SC[:, ksl], sc[:, 0:n], [sctok], [sctk])
            SCv = SC[:, 0:Nk]
            AM, LO, W0, T, CNT, V = (sm[:, 36:37], sm[:, 37:38], sm[:, 38:39], sm[:, 39:40], sm[:, 40:41], sm[:, 41:42])
            WJ = sm[:, 44:44 + self.NIT]
            self.S.op('dve', lambda SCv=SCv, AM=AM: nc.vector.tensor_reduce(out=AM, in_=SCv, axis=AX.X, op=ALU.max,
                                                                         apply_absolute_value=True), [sctk], ['AM'])
            self.tt('pool', SC[:, Nk - 128:Nk], SC[:, Nk - 128:Nk], self.c_CMf, ALU.add, [sctk, 'CMf'], [sctk])
            self.ts('dve', W0, AM, 2.002, None, ALU.mult, None, ['AM'], ['W0'])
            self.ts('dve', WJ, self.P2[:, 0:self.NIT], W0, None, ALU.mult, None, ['P2', 'W0'], ['WJ'])
            self.memset('dve', T, 0.0, ['T'])
            for j in range(self.NIT):
                self.ts('dve', self.JK[:, 0:Nk], SCv, T, 0.0, ALU.is_ge, ALU.add, [sctk, 'T'], ['JK', 'CNT'], accum=CNT)
                self.ts('dve', V, CNT, float(self.NSEL), 0.5, ALU.is_ge, ALU.subtract, ['CNT'], ['V'])
                self.stt(T, V, WJ[:, j:j + 1], T, ALU.mult, ALU.add, ['V', 'WJ', 'T'], ['T'])
            self.stt(LO, WJ[:, self.NIT - 1:self.NIT], -0.5, T, ALU.mult, ALU.add, ['WJ', 'T'], ['LO'])
            self.ts('dve', self.JK[:, 0:Nk], SCv, LO, NEG, ALU.is_lt, ALU.mult, [sctk, 'LO'], ['JK'])
            self.dma('pool', self.d_mb[i, :, 0:Nk], self.JK[:, 0:Nk], ['JK'], [('mbd', i)])

    def output_phase(self, l, s):
        nc = self.nc
        sm = self.sm
        src = self.xsrc(l, s)
        dst = self.d_out[s] if l == self.L - 1 else self.d_xmid[s]
        b7 = self.bank[7][:].bitcast(BF16)
        for pz in range(4):
            self.dma('sp', self.WS[0][:], self.d_wout[l, :, :, pz * 256:(pz + 1) * 256], (), ['WS0'])
            self.cp('pool', self.WO[:, :, pz * 256:(pz + 1) * 256], self.WS[0][:], ['WS0'], ['QT'])
        self.dma('sp', self.LNG, self.d_lnp[l, 0], (), ['KT'])
        self.dma('sp', self.LNB, self.d_lnp[l, 1], ['KT'], ['KT'])
        sets = [dict(XR=self.XR[:], XRt='XB0', Z=self.Z[:], Zt='XB1', ZC=self.ZC, ZCt=['XBb0', 'XBb1'],
                     OGB=self.OGB[:], OGBt='OGB', OGT=self.OGT[:], OGTt='OGT')]
        if self.Sq >= 2048:
            sets.append(
                dict(XR=self.SCraw[:, 0:1024], XRt='SC0', Z=self.SCraw[:, 1024:2048], Zt='SC0',
                     ZC=self.SC2[:, 0:1024], ZCt=['SC1'],
                     OGB=self.SC2[:, 1024:1536].bitcast(BF16), OGBt='SC1',
                     OGT=self.SC2[:, 1536:2048].bitcast(BF16).rearrange("p (c t) -> p c t", c=8), OGTt='SC1'))
        for tb in range(self.NB):
            tsl = slice(tb * 128, (tb + 1) * 128)
            B_ = sets[tb % len(sets)]
            XR, Z, ZC, OGB, OGT = B_['XR'], B_['Z'], B_['ZC'], B_['OGB'], B_['OGT']
            XRt, Zt, ZCt, OGBt, OGTt = B_['XRt'], B_['Zt'], B_['ZCt'], B_['OGBt'], B_['OGTt']
            b7i = 7 if tb % 2 == 0 else 4
            b7 = self.bank[b7i][:].bitcast(BF16)
            b7t = 'b%d' % b7i
            self.dma('sp', OGB, self.d_ogd[tsl, :], [('ogd', tb)], [OGBt])
            self.dma('sp', XR, src[tsl, :], [('xm', s, tb)], [XRt])
            for c in range(8):
                self.tr(b7[:, c * 128:(c + 1) * 128], OGB[:, c * 128:(c + 1) * 128], [OGBt], [b7t])
            self.cp('act', OGT, b7.rearrange("p (c t) -> p c t", c=8), [b7t], [OGTt])
            sa = 64 + 8 * (tb % 2)
            for half in range(2):
                bi = (5 + half) if tb % 2 == 0 else (2 + half)
                ps, pstok = self.bank[bi], 'b%d' % bi
                for c in range(8):
                    self.mm(ps[:, :], OGT[:, c, :], self.WO[:, c, half * 512:(half + 1) * 512], c == 0, c == 7,
                            [OGTt, 'QT'], [pstok])
                hs = slice(half * 512, (half + 1) * 512)
                self.stt(Z[:, hs], XR[:, hs], ALPHA, ps[:, :], ALU.mult, ALU.add, [XRt, pstok], [Zt, ('su', tb % 2, half)],
                         accum=sm[:, sa + half:sa + half + 1])
            nm, ssq, lv, rstd = sm[:, sa + 2:sa + 3], sm[:, sa + 3:sa + 4], sm[:, sa + 4:sa + 5], sm[:, sa + 5:sa + 6]
            msq, bt = sm[:, sa + 6:sa + 7], sm[:, sa + 7:sa + 8]
            u = tb % 2
            self.tt('dve', nm, sm[:, sa:sa + 1], sm[:, sa + 1:sa + 2], ALU.add, [('su', u, 0), ('su', u, 1)], [('nm', u)])
            self.ts('dve', nm, nm, -1.0 / D_MODEL, None, ALU.mult, None, [('nm', u)], [('nm', u)])
            self.act(ZC, Z, AF.Square, [Zt], ZCt + [('ssq', u)], accum=ssq)
            self.tt('dve', msq, nm, nm, ALU.mult, [('nm', u)], [('msq', u)])
            self.ts('dve', bt, msq, -1.0, NORM_EPS, ALU.mult, ALU.add, [('msq', u)], [('bt', u)])
            self.act(lv, ssq, AF.Ln, [('ssq', u), ('bt', u)], [('lv', u)], scale=1.0 / D_MODEL, bias=bt)
            self.act(rstd, lv, AF.Exp, [('lv', u)], [('rstd', u)], scale=-0.5)
            self.stt(ZC, Z, nm, self.LNG, ALU.add, ALU.mult, [Zt, ('nm', u), 'KT'] + ZCt, ZCt)
            self.stt(Z, ZC, rstd, self.LNB, ALU.mult, ALU.add, ZCt + [('rstd', u), 'KT', Zt], [Zt])
            W = [('xm', s, tb)] if l < self.L - 1 else [('out', s, tb)]
            self.dma('pool', dst[tsl, :], Z, [Zt], W)


def t5_bucket_np(dist):
    max_exact = 16
    n = np.maximum(dist, 0)
    nf = np.maximum(n, max_exact).astype(np.float32)
    large = max_exact + (np.log(nf / np.float32(max_exact)) / np.float32(math.log(128 / max_exact))
                         * np.float32(32 - max_exact)).astype(np.int32)
    large = np.minimum(large, 31)
    return np.where(n < max_exact, n, large)


def host_consts(S):
    q = np.arange(128)[:, None]
    k = np.arange(128)[None, :]
    c = np.zeros((128, 7, 128), np.float32)
    c[:, 0] = np.eye(128, dtype=np.float32)
    c[:, 1] = np.where(q >= k, -1.0, 0.0)
    c[:, 2] = -1.0
    c[:, 3] = np.where(k <= q, 0.0, NEG)
    c[:, 4] = np.where(k < q, 0.0, NEG)
    c[:, 5] = np.where(q < k, 1.0, 0.0)
    c[:, 6] = 1.0
    pos = np.arange(S, dtype=np.float32)
    inv_freq = (np.float32(10000.0) ** (-np.arange(16, dtype=np.float32) / np.float32(16))).astype(np.float32)
    ang = pos[:, None] * inv_freq[None, :]
    rope = np.zeros((2, 128, S), np.float32)
    rope[0, 64:80] = np.cos(ang).T
    rope[0, 80:96] = np.cos(ang).T
    rope[1, 64:80] = np.sin(ang).T
    rope[1, 80:96] = np.sin(ang).T
    return c, rope


def host_layout(inp, S, L):
    f = lambda a: np.ascontiguousarray(np.asarray(a, dtype=np.float32))
    w_in = f(inp["w_in"])[:L]
    w_out = f(inp["w_out"])[:L]
    d = {}
    d["w_in_l"] = f(w_in.reshape(L, 8, 128, D_IN).transpose(0, 2, 1, 3))
    d["w_out_l"] = f(w_out.reshape(L, 8, 128, D_MODEL).transpose(0, 2, 1, 3))
    lnp = np.stack([f(inp["ln_g"])[:L], f(inp["ln_b"])[:L]], axis=1)
    d["lnp"] = f(np.broadcast_to(lnp[:, :, None, :], (L, 2, 128, D_MODEL)))
    rb = f(inp["rel_bias"])
    q = np.arange(128)[:, None]
    k = np.arange(128)[None, :]
    bd = t5_bucket_np(np.maximum(q - k, 0))
    bo = t5_bucket_np(128 + q - k)
    relb = np.zeros((128, 8, 2, 128), np.float32)
    relb[:, :, 0, :] = rb[bd].transpose(0, 2, 1)
    relb[:, :, 1, :] = rb[bo].transpose(0, 2, 1)
    d["relb"] = relb
    d["c31"] = f(np.broadcast_to(rb[31][None, :], (128, 8)))
    d["dlam"] = f(np.broadcast_to(f(inp["diff_lambda"])[:L].reshape(L, 1, 128), (L, 128, 128)))
    d["subln"] = f(np.broadcast_to(f(inp["diff_subln"])[:L][:, None, :], (L, 128, 64)))
    d["qn"] = f(f(inp["mla_q_norm"])[:L].reshape(L, 2, 128).transpose(0, 2, 1))
    d["kvn"] = f(f(inp["mla_kv_norm"])[:L].reshape(L, 128, 1))
    d["wuq"] = f(f(inp["mla_w_uq"])[:L].reshape(L, 2, 128, 384).transpose(0, 2, 1, 3))
    d["wukv"] = f(inp["mla_w_ukv"])[:L]
    c, rope = host_consts(S)
    d["consts"] = c
    d["rope"] = rope
    return d


_CACHE = {}


def run(inp, S, NSEQ, L, n_cores, NIT=13, groups="ABCD", core0=0):
    key = (S, NSEQ, L, NIT, groups)
    if key not in _CACHE:
        b = Builder(S=S, NSEQ=NSEQ, L=L, NIT=NIT, groups=groups)
        nc = b.build()
        print("built: ops/waits/sems/cnt/ndma", b.S.stats, flush=True)
        _CACHE[key] = nc
    nc = _CACHE[key]
    shared = host_layout(inp, S, L)
    x = np.ascontiguousarray(np.asarray(inp["x"], dtype=np.float32))
    in_maps = []
    for c in range(n_cores):
        m = dict(shared)
        m["x"] = np.ascontiguousarray(x[c * NSEQ:(c + 1) * NSEQ])
        in_maps.append(m)
    res = run_bass_kernel_spmd(nc, in_maps, core_ids=list(range(core0, core0 + n_cores)))
    return np.concatenate([np.asarray(r["out"]) for r in res.results], axis=0).astype(np.float32)


def kernel(x, w_in, w_out, ln_g, ln_b, rel_bias, diff_lambda, diff_subln,
           mla_q_norm, mla_kv_norm, mla_w_uq, mla_w_ukv):
    inp = dict(x=x, w_in=w_in, w_out=w_out, ln_g=ln_g, ln_b=ln_b, rel_bias=rel_bias, diff_lambda=diff_lambda,
               diff_subln=diff_subln, mla_q_norm=mla_q_norm, mla_kv_norm=mla_kv_norm, mla_w_uq=mla_w_uq,
               mla_w_ukv=mla_w_ukv)
    return run(inp, 2048, 2, DEPTH, 8)
```
